# Optimizing a Trainium2 kernel written in Bass

```python
import math
import jax, jax.numpy as jnp
from jax import lax
import numpy as np

D_MODEL = 2048
BATCH = 1
SEQ = 16384
DEPTH = 1
DEC_BATCH = 32
DEC_SEQ = 16
PAST_LEN = 4096

CHUNK = 64
Q_BLOCK = 128
ROPE_THETA = 500000.0
EPS = 1e-6
D_PLE = 256

H_A = 8
DH_A = 64
DV_A = 2 * DH_A
W_A = H_A * DV_A
H_B = 8
DH_B = 128
W_B = H_B * DH_B
IDX_HEADS = 16
IDX_DIM = 64
TOPK_MAX = 256

SECTION_SIZES = (H_A * 2 * DH_A, H_A * 2 * DH_A, W_A, W_A,
                 W_B, W_B, W_B, W_B,
                 IDX_HEADS * IDX_DIM, IDX_DIM, IDX_HEADS,
                 2 * D_MODEL)
N_IN = sum(SECTION_SIZES)

kernel_name = 'diff_dsa_gated_hybrid_stream_step'


def rms_norm(x, g):
    xf = x.astype(jnp.float32)
    y = xf * lax.rsqrt(jnp.mean(xf * xf, axis=-1, keepdims=True) + EPS)
    return (y * g.astype(jnp.float32)).astype(x.dtype)


def partial_rope(x, pos):
    d = x.shape[-1]
    r = d // 4
    inv = ROPE_THETA ** (-jnp.arange(0, r, 2, dtype=jnp.float32) / r)
    ang = pos.astype(jnp.float32)[:, None] * inv[None, :]
    cos = jnp.cos(ang)[:, None, :]
    sin = jnp.sin(ang)[:, None, :]
    xr = x[..., :r].astype(jnp.float32)
    x1, x2 = xr[..., : r // 2], xr[..., r // 2:]
    rot = jnp.concatenate([x1 * cos - x2 * sin, x2 * cos + x1 * sin], axis=-1)
    return jnp.concatenate([rot.astype(x.dtype), x[..., r:]], axis=-1)


def chunk_end(pos):
    return (pos // CHUNK + 1) * CHUNK - 1


def layer_inputs(x, pos, ln_g, w_in, qn_a, kn_a, qn_b, kn_b, kn_i):
    B, T, _ = x.shape
    z = rms_norm(x, ln_g) @ w_in
    splits = np.cumsum(SECTION_SIZES)[:-1].tolist()
    qa, ka, va, ga, qb, kb, vb, gb, qi, ki, wi, mg = jnp.split(z, splits, axis=-1)
    qa = partial_rope(rms_norm(qa.reshape(B, T, 2 * H_A, DH_A), qn_a), pos).reshape(B, T, H_A, 2, DH_A)
    ka = partial_rope(rms_norm(ka.reshape(B, T, 2 * H_A, DH_A), kn_a), pos).reshape(B, T, H_A, 2, DH_A)
    va = va.reshape(B, T, H_A, DV_A)
    qb = partial_rope(rms_norm(qb.reshape(B, T, H_B, DH_B), qn_b), pos)
    kb = partial_rope(rms_norm(kb.reshape(B, T, H_B, DH_B), kn_b), pos)
    vb = vb.reshape(B, T, H_B, DH_B)
    qi = partial_rope(qi.reshape(B, T, IDX_HEADS, IDX_DIM), pos)
    ki = partial_rope(rms_norm(ki, kn_i)[:, :, None, :], pos)[:, :, 0, :]
    wi = wi * (IDX_HEADS ** -0.5)
    return qa, ka, va, ga, qb, kb, vb, gb, qi, ki, wi, mg


def diff_attend(q, k, v, q_pos, k_pos, lam, lam_init, sub_g):
    s = jnp.einsum('bqhcd,bkhcd->bhcqk', q, k, preferred_element_type=jnp.float32) * (DH_A ** -0.5)
    vis = k_pos[None, :] <= chunk_end(q_pos)[:, None]
    p = jax.nn.softmax(jnp.where(vis, s, -jnp.inf), axis=-1)
    a = p[:, :, 0] - lam * p[:, :, 1]
    o = jnp.einsum('bhqk,bkhd->bqhd', a.astype(v.dtype), v)
    return rms_norm(o, sub_g) * (1.0 - lam_init)


def dsa_attend(q, qi, wi, k, v, ki, q_pos, k_pos, topk):
    logits = jnp.einsum('bqhd,bkd->bqhk', qi, ki, preferred_element_type=jnp.float32) * (IDX_DIM ** -0.5)
    score = jnp.einsum('bqh,bqhk->bqk', wi.astype(jnp.float32), jax.nn.relu(logits))
    q_end = chunk_end(q_pos)
    vis = k_pos[None, :] <= q_end[:, None]
    _, idx = lax.top_k(jnp.where(vis, score, -jnp.inf), topk)
    valid = k_pos[idx] <= q_end[None, :, None]
    gather = jax.vmap(lambda a, i: a[i])
    k_sel = gather(k, idx)
    v_sel = gather(v, idx)
    s = jnp.einsum('bqhd,bqjhd->bhqj', q, k_sel, preferred_element_type=jnp.float32) * (DH_B ** -0.5)
    p = jax.nn.softmax(jnp.where(valid[:, None], s, -jnp.inf), axis=-1)
    return jnp.einsum('bhqj,bqjhd->bqhd', p.astype(v.dtype), v_sel)


def layer_output(x, p, oa, ga, ob, gb, mg, w_ba, w_bb, w_out, ple_g, w_pg, w_ple):
    B, T, _ = x.shape
    ya = (oa.reshape(B, T, W_A) * jax.nn.silu(ga)) @ w_ba
    yb = (ob.reshape(B, T, W_B) * jax.nn.silu(gb)) @ w_bb
    ma, mb = jnp.split(jax.nn.sigmoid(mg), 2, axis=-1)
    h = x + (ma * ya + mb * yb) @ w_out
    gate = jax.nn.sigmoid(rms_norm(h, ple_g) @ w_pg)
    return h + gate * (p @ w_ple)


def unblock(a):
    a = jnp.moveaxis(a, 0, 1)
    return a.reshape((a.shape[0], a.shape[1] * a.shape[2]) + a.shape[3:])


def setup_inputs(seed: int = 0) -> dict:
    key = jax.random.key(seed)
    ks = iter(jax.random.split(key, 32))

    def nrm(shape, scale=1.0):
        return scale * jax.random.normal(next(ks), shape, jnp.float32)

    def gain(shape):
        return 1.0 + 0.01 * nrm(shape)

    return {
        'x_prompt': nrm((BATCH, SEQ, D_MODEL)),
        'x_sample': nrm((DEC_BATCH, DEC_SEQ, D_MODEL)),
        'p_prompt': nrm((DEPTH, BATCH, SEQ, D_PLE)),
        'p_sample': nrm((DEPTH, DEC_BATCH, DEC_SEQ, D_PLE)),
        'cache_diff_k': nrm((DEPTH, DEC_BATCH, PAST_LEN, H_A, 2, DH_A)),
        'cache_diff_v': nrm((DEPTH, DEC_BATCH, PAST_LEN, H_A, DV_A)),
        'cache_dsa_k': nrm((DEPTH, DEC_BATCH, PAST_LEN, H_B, DH_B)),
        'cache_dsa_v': nrm((DEPTH, DEC_BATCH, PAST_LEN, H_B, DH_B)),
        'cache_idx_k': nrm((DEPTH, DEC_BATCH, PAST_LEN, IDX_DIM)),
        'ln_g': gain((DEPTH, D_MODEL)),
        'w_in': nrm((DEPTH, D_MODEL, N_IN), D_MODEL ** -0.5),
        'q_norm_a': gain((DEPTH, DH_A)),
        'k_norm_a': gain((DEPTH, DH_A)),
        'lam_q1': nrm((DEPTH, DH_A), 0.1),
        'lam_k1': nrm((DEPTH, DH_A), 0.1),
        'lam_q2': nrm((DEPTH, DH_A), 0.1),
        'lam_k2': nrm((DEPTH, DH_A), 0.1),
        'subln_a': gain((DEPTH, DV_A)),
        'q_norm_b': gain((DEPTH, DH_B)),
        'k_norm_b': gain((DEPTH, DH_B)),
        'k_norm_idx': gain((DEPTH, IDX_DIM)),
        'w_branch_a': nrm((DEPTH, W_A, D_MODEL), W_A ** -0.5),
        'w_branch_b': nrm((DEPTH, W_B, D_MODEL), W_B ** -0.5),
        'w_out': nrm((DEPTH, D_MODEL, D_MODEL), D_MODEL ** -0.5),
        'ple_norm': gain((DEPTH, D_MODEL)),
        'w_ple_gate': nrm((DEPTH, D_MODEL, D_MODEL), D_MODEL ** -0.5),
        'w_ple': nrm((DEPTH, D_PLE, D_MODEL), D_PLE ** -0.5),
    }


def reference(x_prompt, x_sample, p_prompt, p_sample, cache_diff_k, cache_diff_v, cache_dsa_k, cache_dsa_v,
              cache_idx_k, ln_g, w_in, q_norm_a, k_norm_a, lam_q1, lam_k1, lam_q2, lam_k2, subln_a,
              q_norm_b, k_norm_b, k_norm_idx, w_branch_a, w_branch_b, w_out, ple_norm, w_ple_gate, w_ple):
    t_p = x_prompt.shape[1]
    t_s = x_sample.shape[1]
    past = cache_diff_k.shape[2]
    n_blocks = t_p // Q_BLOCK
    topk_p = min(TOPK_MAX, t_p // 4)
    topk_s = min(TOPK_MAX, (past + t_s) // 4)
    pos_p = jnp.arange(t_p)
    pos_s = past + jnp.arange(t_s)
    kpos_s = jnp.arange(past + t_s)

    hp, hs = x_prompt, x_sample
    dkp, dvp, skp, svp, ikp = [], [], [], [], []
    dks, dvs, sks, svs, iks = [], [], [], [], []
    for l in range(DEPTH):
        lam_init = 0.8 - 0.6 * math.exp(-0.3 * l)
        lam = (jnp.exp(jnp.sum(lam_q1[l].astype(jnp.float32) * lam_k1[l].astype(jnp.float32)))
               - jnp.exp(jnp.sum(lam_q2[l].astype(jnp.float32) * lam_k2[l].astype(jnp.float32))) + lam_init)
        norms = (ln_g[l], w_in[l], q_norm_a[l], k_norm_a[l], q_norm_b[l], k_norm_b[l], k_norm_idx[l])
        outs = (w_branch_a[l], w_branch_b[l], w_out[l], ple_norm[l], w_ple_gate[l], w_ple[l])

        qa, ka, va, ga, qb, kb, vb, gb, qi, ki, wi, mg = layer_inputs(hp, pos_p, *norms)

        def block(i, qa=qa, ka=ka, va=va, qb=qb, kb=kb, vb=vb, qi=qi, ki=ki, wi=wi, lam=lam,
                  lam_init=lam_init, sg=subln_a[l]):
            s0 = i * Q_BLOCK
            qpos = s0 + jnp.arange(Q_BLOCK)
            sl = lambda a: lax.dynamic_slice_in_dim(a, s0, Q_BLOCK, axis=1)
            oa_b = diff_attend(sl(qa), ka, va, qpos, pos_p, lam, lam_init, sg)
            ob_b = dsa_attend(sl(qb), sl(qi), sl(wi), kb, vb, ki, qpos, pos_p, topk_p)
            return oa_b, ob_b

        oa, ob = lax.map(block, jnp.arange(n_blocks))
        hp = layer_output(hp, p_prompt[l], unblock(oa), ga, unblock(ob), gb, mg, *outs)
        dkp.append(ka); dvp.append(va); skp.append(kb); svp.append(vb); ikp.append(ki)

        qa, ka, va, ga, qb, kb, vb, gb, qi, ki, wi, mg = layer_inputs(hs, pos_s, *norms)
        ka_all = jnp.concatenate([cache_diff_k[l].astype(ka.dtype), ka], axis=1)
        va_all = jnp.concatenate([cache_diff_v[l].astype(va.dtype), va], axis=1)
        kb_all = jnp.concatenate([cache_dsa_k[l].astype(kb.dtype), kb], axis=1)
        vb_all = jnp.concatenate([cache_dsa_v[l].astype(vb.dtype), vb], axis=1)
        ki_all = jnp.concatenate([cache_idx_k[l].astype(ki.dtype), ki], axis=1)
        oa_s = diff_attend(qa, ka_all, va_all, pos_s, kpos_s, lam, lam_init, subln_a[l])
        ob_s = dsa_attend(qb, qi, wi, kb_all, vb_all, ki_all, pos_s, kpos_s, topk_s)
        hs = layer_output(hs, p_sample[l], oa_s, ga, ob_s, gb, mg, *outs)
        dks.append(ka); dvs.append(va); sks.append(kb); svs.append(vb); iks.append(ki)

    return (hp, hs,
            jnp.stack(dkp), jnp.stack(dvp), jnp.stack(skp), jnp.stack(svp), jnp.stack(ikp),
            jnp.stack(dks), jnp.stack(dvs), jnp.stack(sks), jnp.stack(svs), jnp.stack(iks))
```

```python
import contextlib
import numpy as np
import concourse.bass as bass
import concourse.mybir as mybir
from concourse.bass_utils import run_bass_kernel_spmd

F32 = mybir.dt.float32
BF16 = mybir.dt.bfloat16
ALU = mybir.AluOpType
AF = mybir.ActivationFunctionType
AX = mybir.AxisListType

NCORES = 8
D = 2048
NCH = 16
SEQ = 16384
NBLK = 16
NOWN = NBLK * 128 + 64
PAST = 4096
NKS = PAST + 16
N_IN = 13392
EPS = 1e-6
import os
DBG = bool(int(os.environ.get('KDBG', '0')))
KSTOP = int(os.environ.get('KSTOP', '9'))
SMALL = DBG and KSTOP < 2
KCORES = int(os.environ.get('KCORES', '8'))
KMAXOPS = int(os.environ.get('KMAXOPS', '100000000'))
KSEC = os.environ.get('KSEC', '')
KTRACE = [int(v) for v in os.environ.get('KTRACE', '').split(',')] if os.environ.get('KTRACE') else None
XROWS = 128 if SMALL else 16384
CROWS = 128 if SMALL else 4096
NEG = -30000.0
NBIS = 26
BIS_LO = -256.0
BIS_W = 512.0
TOPK = 256
LAM_INIT = 0.2


class Buf:
    __slots__ = ("name", "w", "r")

    def __init__(self, name):
        self.name = name
        self.w = None
        self.r = []


class Sched:
    def __init__(self, sems):
        self.free = list(sems)
        self.q = {e: [] for e in ("pe", "act", "dve", "pool", "sp")}
        self.cnt = {}
        self.sem = {}
        for e in ("pe", "act", "dve"):
            self.sem[e] = self.free.pop()
            self.cnt[e] = 0
        self.dsem = {"sp": [self.free.pop() for _ in range(20)],
                     "pool": [self.free.pop() for _ in range(20)]}
        self.dval = {}
        self.drot = {"sp": 0, "pool": 0}
        self.known = {e: {} for e in self.q}
        self.nops = 0

    def _waits(self, eng, r, w):
        need = {}
        for b in r:
            if b.w is not None:
                s, v = b.w
                need[s] = max(need.get(s, 0), v)
        for b in w:
            if b.w is not None:
                s, v = b.w
                need[s] = max(need.get(s, 0), v)
            for (s, v) in b.r:
                need[s] = max(need.get(s, 0), v)
        out = []
        kn = self.known[eng]
        for s, v in need.items():
            if eng == "pe" and s is self.sem.get("pe"):
                continue
            if kn.get(id(s), 0) >= v:
                continue
            kn[id(s)] = v
            out.append((s, v))
        return out

    def _commit(self, ev, r, w):
        for b in r:
            b.r.append(ev)
            if len(b.r) > 64:
                m = {}
                for (s, v) in b.r:
                    k = id(s)
                    if k not in m or m[k][1] < v:
                        m[k] = (s, v)
                b.r = list(m.values())
        for b in w:
            b.w = ev
            b.r = []

    def op(self, eng, fn, r=(), w=()):
        if self.nops >= KMAXOPS:
            return
        waits = self._waits(eng, r, w)
        if self.cnt[eng] >= 30000:
            self.sem[eng] = self.free.pop()
            self.cnt[eng] = 0
        self.cnt[eng] += 1
        s = self.sem[eng]
        ev = (s, self.cnt[eng])
        self.q[eng].append((waits, fn, s, 1))
        self._commit(ev, r, w)
        self.nops += 1
        if KTRACE and KTRACE[0] <= self.nops <= KTRACE[1]:
            print("OP", self.nops, eng, fn.__code__.co_firstlineno, flush=True)

    def dma(self, qe, fn, r=(), w=()):
        if self.nops >= KMAXOPS:
            return
        waits = self._waits(qe, r, w)
        i = self.drot[qe]
        self.drot[qe] = (i + 1) % len(self.dsem[qe])
        s = self.dsem[qe][i]
        prev = self.dval.get(id(s), 0)
        kn = self.known[qe]
        if prev > 0 and kn.get(id(s), 0) < prev:
            kn[id(s)] = prev
            waits.append((s, prev))
        self.dval[id(s)] = prev + 16
        ev = (s, prev + 16)
        self.q[qe].append((waits, fn, s, 16))
        self._commit(ev, r, w)
        self.nops += 1
        if KTRACE and KTRACE[0] <= self.nops <= KTRACE[1]:
            print("DMA", self.nops, qe, fn.__code__.co_firstlineno, flush=True)

    def barrier(self):
        evs = []
        for en in ("pe", "act", "dve"):
            if self.cnt[en] > 0:
                evs.append((self.sem[en], self.cnt[en]))
        for qe in ("sp", "pool"):
            for s in self.dsem[qe]:
                v = self.dval.get(id(s), 0)
                if v > 0:
                    evs.append((s, v))
        for eng in self.q:
            kn = self.known[eng]
            waits = []
            for (s, v) in evs:
                if kn.get(id(s), 0) >= v:
                    continue
                kn[id(s)] = v
                waits.append((s, v))
            self.q[eng].append((waits, None, None, 0))

    def emit(self, qe, e):
        for (waits, fn, s, inc) in self.q[qe]:
            for (ws, wv) in waits:
                e.wait_ge(ws, wv)
            if fn is not None:
                fn(e).then_inc(s, inc)

    def final_waits(self, e):
        for qe in ("sp", "pool"):
            for s in self.dsem[qe]:
                v = self.dval.get(id(s), 0)
                if v > 0:
                    e.wait_ge(s, v)
        for en in ("pe", "act", "dve"):
            if self.cnt[en] > 0:
                e.wait_ge(self.sem[en], self.cnt[en])


def build_program():
    nc = bass.Bass("TRN2", target_bir_lowering=False)
    es = contextlib.ExitStack()

    def din(name, shape, dt=F32):
        return nc.dram_tensor(name, list(shape), dt, kind="ExternalInput")

    def dout(name, shape, dt=F32):
        return nc.dram_tensor(name, list(shape), dt, kind="ExternalOutput")

    def dscr(name, shape, dt=BF16):
        if DBG and name in ("uT", "hbuf", "mT_scr", "qaT", "qbT", "qiT", "sga", "sgn_scr", "kiT", "kaT", "va"):
            return nc.dram_tensor(name, list(shape), dt, kind="ExternalOutput")
        return nc.dram_tensor(name, list(shape), dt, kind="Internal")

    x_all = din("x_all", [XROWS, D]).ap()
    x_own = din("x_own", [NOWN, D]).ap()
    p_own = din("p_own", [NOWN, 256]).ap()
    rope_all = din("rope_all", [XROWS, 48]).ap()
    rope_own = din("rope_own", [NOWN, 48]).ap()
    c_dk = din("c_dk", [4, CROWS, 1024]).ap()
    c_dv = din("c_dv", [4, CROWS, 1024]).ap()
    c_sk = din("c_sk", [4, CROWS, 1024]).ap()
    c_sv = din("c_sv", [4, CROWS, 1024]).ap()
    c_ik = din("c_ik", [4, CROWS, 64]).ap()
    w_in = din("w_in", [D, N_IN]).ap()
    ln_g_h = din("ln_g", [D])
    vec_h = {}
    for nm, n in (("q_norm_a", 64), ("k_norm_a", 64), ("lam_q1", 64), ("lam_k1", 64), ("lam_q2", 64),
                  ("lam_k2", 64), ("subln_a", 128), ("q_norm_b", 128), ("k_norm_b", 128), ("k_norm_idx", 64)):
        vec_h[nm] = din(nm, [n])
    w_ba = din("w_branch_a", [1024, D]).ap()
    w_bb = din("w_branch_b", [1024, D]).ap()
    w_out = din("w_out", [D, D]).ap()
    ple_g_h = din("ple_norm", [D])
    w_pg = din("w_ple_gate", [D, D]).ap()
    w_ple = din("w_ple", [256, D]).ap()
    ident_d = din("ident", [128, 128]).ap()
    maskT_d = din("maskT", [128, 8 * 128]).ap()
    maskqk_d = din("maskqk", [128, 8 * 128]).ap()

    if DBG:
        dbg_eo = dout("dbg_eo", [128, 256]).ap()
        dbg_ep = dout("dbg_ep", [128, 16]).ap()
        dbg_ps = dout("dbg_ps", [128, 1024]).ap()
        dbg_lam = dout("dbg_lam", [128, 8]).ap()
    y_own = dout("y_own", [NOWN, D]).ap()
    o_dk = dout("o_dk", [NOWN, 1024]).ap()
    o_dv = dout("o_dv", [NOWN, 1024]).ap()
    o_sk = dout("o_sk", [NOWN, 1024]).ap()
    o_sv = dout("o_sv", [NOWN, 1024]).ap()
    o_ik = dout("o_ik", [NOWN, 64]).ap()

    kaT = dscr("kaT", [1024, SEQ]).ap()
    va = dscr("va", [SEQ, 1024]).ap()
    kbT = dscr("kbT", [1024, SEQ]).ap()
    vb = dscr("vb", [SEQ, 1024]).ap()
    kiT = dscr("kiT", [64, SEQ]).ap()
    s_kaT = dscr("s_kaT", [4, 1024, NKS]).ap()
    s_va = dscr("s_va", [4, NKS, 1024]).ap()
    s_kbT = dscr("s_kbT", [4, 1024, NKS]).ap()
    s_vb = dscr("s_vb", [4, NKS, 1024]).ap()
    s_kiT = dscr("s_kiT", [4, 64, NKS]).ap()
    qaT = dscr("qaT", [1024, NOWN]).ap()
    qbT = dscr("qbT", [1024, NOWN]).ap()
    qiT = dscr("qiT", [1024, NOWN]).ap()
    sga = dscr("sga", [NOWN, 1024]).ap()
    sgb = dscr("sgb", [NOWN, 1024]).ap()
    smg = dscr("smg", [NOWN, 4096]).ap()
    uT = dscr("uT", [D, NOWN]).ap()
    hbuf = dscr("hbuf", [NOWN, D], F32).ap()
    sgn_scr = dscr("sgn_scr", [NOWN, 16], F32).ap()
    mT_scr = dscr("mT_scr", [D, NOWN]).ap()
    B_kv = Buf("kv_scr")
    B_skv = Buf("skv_scr")
    B_q = Buf("q_scr")
    B_g = Buf("g_scr")
    B_u = Buf("u_scr")
    B_h = Buf("h_scr")
    B_out = Buf("outs")

    def sb(name, shape, dt=F32):
        return es.enter_context(nc.sbuf_tensor(name, list(shape), dt))

    def ps(name, shape, dt=F32):
        return es.enter_context(nc.psum_tensor(name, list(shape), dt))

    sems = [es.enter_context(nc.semaphore(f"s{i}")) for i in range(90)]
    S = Sched(sems)

    ident_f = sb("ident_fsb", [128, 128]); B_identf = Buf("identf")
    ident_b = sb("ident_b", [128, 128], BF16); B_identb = Buf("identb")
    maskT = sb("maskT_sb", [128, 1024], BF16); B_maskT = Buf("maskT")
    maskqk = sb("maskqk_sb", [128, 1024], BF16); B_maskqk = Buf("maskqk")
    gcol = sb("gcol", [128, 16]); B_gcol = Buf("gcol")
    pgcol = sb("pgcol", [128, 16]); B_pgcol = Buf("pgcol")
    gains = sb("gains", [128, 6 * 128]); B_gains = Buf("gains")
    lamv = sb("lamv", [128, 4 * 64]); B_lamv = Buf("lamv")
    lamt = sb("lamt", [128, 8]); B_lamt = Buf("lamt")
    lamj = sb("lamj", [128, 64]); B_lamj = Buf("lamj")
    epsc = sb("epsc", [128, 1]); B_epsc = Buf("epsc")
    rstd = sb("rstd", [128, 20]); B_rstd = Buf("rstd")
    ssqt = sb("ssqt", [128, 20]); B_ssqt = Buf("ssqt")
    absw = sb("absw", [128, 17 * 16]); B_absw = Buf("absw")
    sgnw = sb("sgnw", [128, 17 * 16]); B_sgnw = Buf("sgnw")
    pTt = sb("pTt", [128, 2, NOWN], BF16); B_pTt = Buf("pTt")
    ARENA = 172032
    arena = sb("arena", [128, ARENA // 2], BF16)
    ar = {"off": 0}

    def aalloc(shape, dt=F32):
        n = 1
        for s_ in shape[1:]:
            n *= s_
        nb = n * (4 if dt == F32 else 2)
        nb = (nb + 63) // 64 * 64
        o = ar["off"]
        ar["off"] = o + nb
        assert ar["off"] <= ARENA, (ar["off"], ARENA)
        v = arena[:, o // 2:(o + n * (4 if dt == F32 else 2)) // 2]
        if dt == F32:
            v = v.bitcast(F32)
        if len(shape) == 3:
            v = v.rearrange("p (a b) -> p a b", b=shape[2])
        if shape[0] < 128:
            v = v[:shape[0]]
        return v

    def areset():
        S.barrier()
        ar["off"] = 0

    NB2 = 2
    NW = 3
    NKB = 3
    NP = 3

    psA = [ps("psA0", [128, 512]), ps("psA1", [128, 512])]; B_psA = [Buf("psA0"), Buf("psA1")]
    psB = [ps("psB0", [128, 512]), ps("psB1", [128, 512])]; B_psB = [Buf("psB0"), Buf("psB1")]
    psL = [ps("psL0", [128, 512]), ps("psL1", [128, 512])]; B_psL = [Buf("psL0"), Buf("psL1")]
    psC = ps("psC", [128, 512]); B_psC = Buf("psC")
    psT = ps("psT", [128, 1024], BF16); B_psT = Buf("psT")

    xT = aalloc([128, NCH, NOWN], BF16); B_xT = Buf("xT")
    xin = [aalloc([128, D]) for i in range(NB2)]; B_xin = [Buf(f"xin{i}") for i in range(NB2)]
    xin_junk = aalloc([128, D], BF16); B_junk = Buf("junk")
    wbf = [aalloc([128, NCH, 512], BF16) for i in range(2)]; B_wbf = [Buf(f"wbf{i}") for i in range(2)]
    ropeall = aalloc([128, 17 * 48]); B_ropeall = Buf("ropeall")
    wkA = [aalloc([128, 512]) for i in range(NW)]; B_wkA = [Buf(f"wkA{i}") for i in range(NW)]
    wkB = [aalloc([128, 512]) for i in range(NW)]; B_wkB = [Buf(f"wkB{i}") for i in range(NW)]
    wkS = [aalloc([128, 16]) for i in range(NW)]; B_wkS = [Buf(f"wkS{i}") for i in range(NW)]
    wkR = [aalloc([128, 4 * 64]) for i in range(NW)]; B_wkR = [Buf(f"wkR{i}") for i in range(NW)]
    wkH = [aalloc([128, 512], BF16) for i in range(NW)]; B_wkH = [Buf(f"wkH{i}") for i in range(NW)]
    stg = [aalloc([128, 4 * 128], BF16) for i in range(NW)]; B_stg = [Buf(f"stg{i}") for i in range(NW)]
    kstage = aalloc([128, 1024]); B_kstage = Buf("kstage")
    kistage = aalloc([128, 64]); B_kistage = Buf("kistage")
    kTs = aalloc([128, 8, 128], BF16); B_kTs = Buf("kTs")
    kiTs = aalloc([64, 128], BF16); B_kiTs = Buf("kiTs")
    vstage = aalloc([128, 1024], BF16); B_vstage = Buf("vstage")

    def bc_rows(handle, n):
        return bass.AP(handle, 0, [[0, 128], [1, n]])

    S.dma("sp", lambda e: e.dma_start(out=ident_f[:, :], in_=ident_d[:, :]), w=[B_identf])
    S.dma("pool", lambda e: e.dma_start(out=ident_b[:, :], in_=ident_d[:, :]), w=[B_identb])
    S.dma("pool", lambda e: e.dma_start(out=maskT[:, :], in_=maskT_d[:, :]), w=[B_maskT])
    S.dma("pool", lambda e: e.dma_start(out=maskqk[:, :], in_=maskqk_d[:, :]), w=[B_maskqk])
    S.dma("sp", lambda e: e.dma_start(out=gcol[:, :], in_=bass.AP(ln_g_h, 0, [[1, 128], [128, 16]]),
                                      allow_slow_non_contiguous=True), w=[B_gcol])
    S.dma("sp", lambda e: e.dma_start(out=pgcol[:, :], in_=bass.AP(ple_g_h, 0, [[1, 128], [128, 16]]),
                                      allow_slow_non_contiguous=True), w=[B_pgcol])
    for i, (nm, n) in enumerate((("q_norm_a", 64), ("k_norm_a", 64), ("q_norm_b", 128), ("k_norm_b", 128),
                                 ("k_norm_idx", 64), ("subln_a", 128))):
        S.dma("sp", lambda e, i=i, nm=nm, n=n: e.dma_start(out=gains[:, i * 128:i * 128 + n],
                                                           in_=bc_rows(vec_h[nm], n)), w=[B_gains])
    for i, nm in enumerate(("lam_q1", "lam_k1", "lam_q2", "lam_k2")):
        S.dma("sp", lambda e, i=i, nm=nm: e.dma_start(out=lamv[:, i * 64:(i + 1) * 64], in_=bc_rows(vec_h[nm], 64)),
              w=[B_lamv])
    G_QNA, G_KNA, G_QNB, G_KNB, G_KNI, G_SUB = [gains[:, i * 128:(i + 1) * 128] for i in range(6)]
    S.op("dve", lambda e: e.memset(epsc[:, :], EPS), w=[B_epsc])
    S.op("dve", lambda e: e.tensor_tensor(out=lamj[:, :], in0=lamv[:, 0:64], in1=lamv[:, 64:128], op=ALU.mult),
         r=[B_lamv], w=[B_lamj])
    S.op("dve", lambda e: e.tensor_reduce(out=lamt[:, 0:1], in_=lamj[:, :], axis=AX.X, op=ALU.add),
         r=[B_lamj], w=[B_lamt])
    S.op("dve", lambda e: e.tensor_tensor(out=lamj[:, :], in0=lamv[:, 128:192], in1=lamv[:, 192:256], op=ALU.mult),
         r=[B_lamv, B_lamt], w=[B_lamj])
    S.op("dve", lambda e: e.tensor_reduce(out=lamt[:, 1:2], in_=lamj[:, :], axis=AX.X, op=ALU.add),
         r=[B_lamj], w=[B_lamt])
    S.op("act", lambda e: e.activation(out=lamt[:, 2:4], in_=lamt[:, 0:2], func=AF.Exp), r=[B_lamt], w=[B_lamt])
    S.op("dve", lambda e: e.tensor_tensor(out=lamt[:, 4:5], in0=lamt[:, 3:4], in1=lamt[:, 2:3], op=ALU.subtract),
         r=[B_lamt], w=[B_lamt])
    S.op("dve", lambda e: e.tensor_scalar(out=lamt[:, 4:5], in0=lamt[:, 4:5], scalar1=-LAM_INIT, scalar2=None,
                                          op0=ALU.add), r=[B_lamt], w=[B_lamt])
    NEGLAM = lamt[:, 4:5]

    rot = {"w": 0, "x": 0, "wk": 0, "ps": 0, "psb": 0, "kb": 0, "p": 0, "l": 0, "r": 0, "ev": 0}

    def nxt(k, n):
        v = rot[k]
        rot[k] = (v + 1) % n
        return v

    def evac_engine():
        return "act" if nxt("ev", 2) == 0 else "dve"

    def build_xT(src, tiles, dst, B_dst, colg, B_colg, with_stats):
        for ti, (tok0, nt, c0) in enumerate(tiles):
            xi = nxt("x", NB2)
            S.dma("sp", lambda e, xi=xi, tok0=tok0, nt=nt: e.dma_start(out=xin[xi][:nt, :], in_=src[tok0:tok0 + nt, :]),
                  w=[B_xin[xi]])
            if with_stats:
                S.op("act", lambda e, xi=xi, nt=nt, ti=ti: e.activation(
                    out=xin_junk[:nt, :], in_=xin[xi][:nt, :], func=AF.Square, accum_out=ssqt[:nt, ti:ti + 1]),
                    r=[B_xin[xi]], w=[B_junk, B_ssqt])
                S.op("act", lambda e, nt=nt, ti=ti: e.activation(
                    out=rstd[:nt, ti:ti + 1], in_=ssqt[:nt, ti:ti + 1], func=AF.Sqrt, scale=1.0 / D, bias=epsc[:nt, :]),
                    r=[B_ssqt, B_epsc], w=[B_rstd])
                S.op("dve", lambda e, nt=nt, ti=ti: e.reciprocal(out=rstd[:nt, ti:ti + 1], in_=rstd[:nt, ti:ti + 1]),
                     r=[B_rstd], w=[B_rstd])
            for q4 in range(4):
                S_ps = B_psC
                for k in range(4):
                    ch = q4 * 4 + k
                    S.op("pe", lambda e, xi=xi, nt=nt, ch=ch, k=k: e.transpose(
                        out=psC[:, k * 128:k * 128 + nt], in_=xin[xi][:nt, ch * 128:(ch + 1) * 128],
                        identity=ident_f[:nt, :nt]), r=[B_xin[xi], B_identf], w=[S_ps])
                for k in range(4):
                    ch = q4 * 4 + k
                    S.op("dve", lambda e, nt=nt, ch=ch, k=k, c0=c0: e.tensor_scalar(
                        out=dst[:, ch, c0:c0 + nt], in0=psC[:, k * 128:k * 128 + nt], scalar1=colg[:, ch:ch + 1],
                        scalar2=None, op0=ALU.mult), r=[S_ps, B_colg], w=[B_dst])


    def load_w(wsrc, col0, ncols, nchk, tiles_, B_tiles):
        wi = nxt("w", 2)
        S.dma("pool", lambda e, wi=wi: e.dma_start(
            out=tiles_[wi][:, :nchk, :ncols],
            in_=wsrc[:, col0:col0 + ncols].rearrange("(ch p) c -> p ch c", p=128)), w=[B_tiles[wi]])
        return wi

    def gemm_tile(pst, B_pst, aT, B_aT, tcol0, nt, wt, B_wt, nchk, ncols):
        for ch in range(nchk):
            S.op("pe", lambda e, ch=ch: e.matmul(out=pst[:nt, :ncols], lhsT=aT[:, ch, tcol0:tcol0 + nt],
                                                 rhs=wt[:, ch, :ncols], start=(ch == 0), stop=(ch == nchk - 1)),
                 r=[B_aT, B_wt], w=[B_pst])

    def norm_rope(pst, B_pst, nt, ncols, dh, gain_ap, rcol, cs, B_cs, do_norm):
        G = ncols // dh
        r2 = dh // 8
        c0 = 0 if dh == 64 else 16
        wi_ = nxt("wk", NW)
        A, Bt, Ss, Rr = wkA[wi_], wkB[wi_], wkS[wi_], wkR[wi_]
        BA, BB, BS, BR = B_wkA[wi_], B_wkB[wi_], B_wkS[wi_], B_wkR[wi_]
        S.op("dve", lambda e: e.tensor_scalar(out=A[:nt, :ncols], in0=pst[:nt, :ncols], scalar1=rcol, scalar2=None,
                                              op0=ALU.mult), r=[B_pst, B_rstd], w=[BA])
        A3 = A[:nt, :ncols].rearrange("p (g d) -> p g d", d=dh)
        if do_norm:
            S.op("act", lambda e: e.activation(out=Bt[:nt, :ncols], in_=A[:nt, :ncols], func=AF.Square), r=[BA], w=[BB])
            S.op("dve", lambda e: e.tensor_reduce(out=Ss[:nt, 0:G], in_=Bt[:nt, :ncols].rearrange("p (g d) -> p g d", d=dh),
                                                  axis=AX.X, op=ALU.add), r=[BB], w=[BS])
            S.op("act", lambda e: e.activation(out=Ss[:nt, 8:8 + G], in_=Ss[:nt, 0:G], func=AF.Sqrt, scale=1.0 / dh,
                                               bias=epsc[:nt, :]), r=[BS, B_epsc], w=[BS])
            S.op("dve", lambda e: e.reciprocal(out=Ss[:nt, 8:8 + G], in_=Ss[:nt, 8:8 + G]), r=[BS], w=[BS])
            S.op("dve", lambda e: e.tensor_tensor(out=A3, in0=A3,
                                                  in1=Ss[:nt, 8:8 + G].unsqueeze(2).broadcast_to([nt, G, dh]),
                                                  op=ALU.mult), r=[BA, BS], w=[BA])
            S.op("dve", lambda e: e.tensor_tensor(out=A3, in0=A3,
                                                  in1=gain_ap[:nt, 0:dh].unsqueeze(1).broadcast_to([nt, G, dh]),
                                                  op=ALU.mult), r=[BA, B_gains], w=[BA])
        x1 = A3[:, :, 0:r2]
        x2 = A3[:, :, r2:2 * r2]
        cosb = cs[:nt, c0:c0 + r2].unsqueeze(1).broadcast_to([nt, G, r2])
        sinb = cs[:nt, c0 + r2:c0 + 2 * r2].unsqueeze(1).broadcast_to([nt, G, r2])
        n = G * r2
        T = [Rr[:nt, k * 64:k * 64 + n].rearrange("p (g d) -> p g d", d=r2) for k in range(4)]
        S.op("dve", lambda e: e.tensor_tensor(out=T[0], in0=x1, in1=cosb, op=ALU.mult), r=[BA, B_cs], w=[BR])
        S.op("dve", lambda e: e.tensor_tensor(out=T[1], in0=x2, in1=sinb, op=ALU.mult), r=[BA, B_cs], w=[BR])
        S.op("dve", lambda e: e.tensor_tensor(out=T[2], in0=x2, in1=cosb, op=ALU.mult), r=[BA, B_cs], w=[BR])
        S.op("dve", lambda e: e.tensor_tensor(out=T[3], in0=x1, in1=sinb, op=ALU.mult), r=[BA, B_cs], w=[BR])
        S.op("dve", lambda e: e.tensor_tensor(out=x1, in0=T[0], in1=T[1], op=ALU.subtract), r=[BR], w=[BA])
        S.op("dve", lambda e: e.tensor_tensor(out=x2, in0=T[2], in1=T[3], op=ALU.add), r=[BR], w=[BA])
        return wi_

    def to_T_scratch(src_bf, B_src, nt, ncols, dsts, stg_tiles=None):
        nk = (ncols + 127) // 128
        si = nxt("r", NW)
        if stg_tiles is None:
            stg_tiles = (stg, B_stg)
        st = stg_tiles[0][si]
        Bst = stg_tiles[1][si]
        for k in range(nk):
            cw = min(128, ncols - k * 128)
            S.op("pe", lambda e, k=k, cw=cw: e.transpose(out=psT[:cw, k * 128:k * 128 + nt],
                                                         in_=src_bf[:nt, k * 128:k * 128 + cw],
                                                         identity=ident_b[:nt, :nt]), r=[B_src, B_identb], w=[B_psT])
        rows = min(128, ncols)
        eng = evac_engine()
        if eng == "act":
            S.op("act", lambda e: e.copy(out=st[:rows, :nk * 128], in_=psT[:rows, :nk * 128]), r=[B_psT], w=[Bst])
        else:
            S.op("dve", lambda e: e.tensor_copy(out=st[:rows, :nk * 128], in_=psT[:rows, :nk * 128]),
                 r=[B_psT], w=[Bst])
        for (dap, t0, n, Bd) in dsts:
            if ncols >= 128:
                S.dma("sp", lambda e, dap=dap, t0=t0, n=n: e.dma_start(
                    out=dap.rearrange("(ch p) t -> p ch t", p=128),
                    in_=st[:, :nk * 128].rearrange("p (ch t) -> p ch t", t=128)[:, :, t0:t0 + n]),
                    r=[Bst], w=[Bd])
            else:
                S.dma("sp", lambda e, dap=dap, t0=t0, n=n: e.dma_start(out=dap, in_=st[:ncols, t0:t0 + n]),
                      r=[Bst], w=[Bd])

    def cast_bf(src_ap, B_src, nt, ncols):
        hi = nxt("l", NW)
        S.op("act", lambda e: e.copy(out=wkH[hi][:nt, :ncols], in_=src_ap), r=[B_src], w=[B_wkH[hi]])
        return hi

    SEC = {"qa": 0, "ka": 1024, "va": 2048, "ga": 3072, "qb": 4096, "kb": 5120, "vb": 6144, "gb": 7168,
           "qi": 8192, "kiwi": 9216, "mg": 9296}

    def proj_pass(src, rope_src, tiles, blocks, own, xT, B_xT):
        build_xT(src, [(t[0], t[1], t[2]) for t in tiles], xT, B_xT, gcol, B_gcol, True)
        for ti, (tok0, nt, xc0, g0) in enumerate(tiles):
            S.dma("sp", lambda e, ti=ti, tok0=tok0, nt=nt: e.dma_start(out=ropeall[:nt, ti * 48:(ti + 1) * 48],
                                                                      in_=rope_src[tok0:tok0 + nt, :]), w=[B_ropeall])
        for (sec, off, ncols) in blocks:
            col0 = SEC[sec] + off
            wi = load_w(w_in, col0, ncols, NCH, wbf, B_wbf)
            for ti, (tok0, nt, xc0, g0) in enumerate(tiles):
                pi = nxt("ps", 2)
                pst, Bp = psA[pi], B_psA[pi]
                gemm_tile(pst, Bp, xT, B_xT, xc0, nt, wbf[wi], B_wbf[wi], NCH, ncols)
                rcol = rstd[:nt, ti:ti + 1]
                cs = ropeall[:, ti * 48:(ti + 1) * 48]
                is_samp = own and nt == 64
                if sec in ("qa", "ka", "qb", "kb"):
                    dh = 64 if sec in ("qa", "ka") else 128
                    gain = {"qa": G_QNA, "ka": G_KNA, "qb": G_QNB, "kb": G_KNB}[sec]
                    wi_ = norm_rope(pst, Bp, nt, ncols, dh, gain, rcol, cs, B_ropeall, True)
                    if sec in ("ka", "kb"):
                        if own:
                            od = o_dk if sec == "ka" else o_sk
                            S.dma("sp", lambda e, od=od, wi_=wi_, g0=g0, nt=nt, off=off: e.dma_start(
                                out=od[g0:g0 + nt, off:off + ncols], in_=wkA[wi_][:nt, :ncols]),
                                r=[B_wkA[wi_]], w=[B_out])
                        if (not own) or is_samp:
                            hi = cast_bf(wkA[wi_][:nt, :ncols], B_wkA[wi_], nt, ncols)
                            if not own:
                                dT = kaT if sec == "ka" else kbT
                                dsts = [(dT[off:off + ncols, g0:g0 + nt], 0, nt, B_kv)]
                            else:
                                dT = s_kaT if sec == "ka" else s_kbT
                                dsts = [(dT[b, off:off + ncols, PAST:PAST + 16], 16 * b, 16, B_skv) for b in range(4)]
                            to_T_scratch(wkH[hi], B_wkH[hi], nt, ncols, dsts)
                    else:
                        hi = cast_bf(wkA[wi_][:nt, :ncols], B_wkA[wi_], nt, ncols)
                        dT = qaT if sec == "qa" else qbT
                        to_T_scratch(wkH[hi], B_wkH[hi], nt, ncols, [(dT[off:off + ncols, g0:g0 + nt], 0, nt, B_q)])
                elif sec in ("va", "vb"):
                    if own:
                        wi_ = nxt("wk", NW)
                        S.op("dve", lambda e, wi_=wi_, pst=pst, nt=nt, rcol=rcol: e.tensor_scalar(
                            out=wkA[wi_][:nt, :ncols], in0=pst[:nt, :ncols], scalar1=rcol, scalar2=None, op0=ALU.mult),
                            r=[Bp, B_rstd], w=[B_wkA[wi_]])
                        od = o_dv if sec == "va" else o_sv
                        S.dma("sp", lambda e, od=od, wi_=wi_, g0=g0, nt=nt, off=off: e.dma_start(
                            out=od[g0:g0 + nt, off:off + ncols], in_=wkA[wi_][:nt, :ncols]), r=[B_wkA[wi_]], w=[B_out])
                    if (not own) or is_samp:
                        hi = nxt("l", NW)
                        S.op("dve", lambda e, hi=hi, pst=pst, nt=nt, rcol=rcol: e.tensor_scalar(
                            out=wkH[hi][:nt, :ncols], in0=pst[:nt, :ncols], scalar1=rcol, scalar2=None, op0=ALU.mult),
                            r=[Bp, B_rstd], w=[B_wkH[hi]])
                        if not own:
                            dV = va if sec == "va" else vb
                            S.dma("sp", lambda e, dV=dV, hi=hi, g0=g0, nt=nt, off=off: e.dma_start(
                                out=dV[g0:g0 + nt, off:off + ncols], in_=wkH[hi][:nt, :ncols]), r=[B_wkH[hi]], w=[B_kv])
                        else:
                            dV = s_va if sec == "va" else s_vb
                            for b in range(4):
                                S.dma("sp", lambda e, dV=dV, hi=hi, b=b, off=off: e.dma_start(
                                    out=dV[b, PAST:PAST + 16, off:off + ncols], in_=wkH[hi][16 * b:16 * b + 16, :ncols]),
                                    r=[B_wkH[hi]], w=[B_skv])
                elif sec in ("ga", "gb", "mg"):
                    hi = nxt("l", NW)
                    fn = AF.Silu if sec != "mg" else AF.Sigmoid
                    wi_ = nxt("wk", NW)
                    S.op("dve", lambda e, wi_=wi_, pst=pst, nt=nt, rcol=rcol: e.tensor_scalar(
                        out=wkA[wi_][:nt, :ncols], in0=pst[:nt, :ncols], scalar1=rcol, scalar2=None, op0=ALU.mult),
                        r=[Bp, B_rstd], w=[B_wkA[wi_]])
                    S.op("act", lambda e, hi=hi, wi_=wi_, nt=nt, fn=fn: e.activation(
                        out=wkH[hi][:nt, :ncols], in_=wkA[wi_][:nt, :ncols], func=fn),
                        r=[B_wkA[wi_]], w=[B_wkH[hi]])
                    dG = {"ga": sga, "gb": sgb, "mg": smg}[sec]
                    S.dma("sp", lambda e, dG=dG, hi=hi, g0=g0, nt=nt, off=off: e.dma_start(
                        out=dG[g0:g0 + nt, off:off + ncols], in_=wkH[hi][:nt, :ncols]), r=[B_wkH[hi]], w=[B_g])
                elif sec == "kiwi":
                    wi_ = norm_rope(pst, Bp, nt, 64, 64, G_KNI, rcol, cs, B_ropeall, True)
                    if own:
                        S.dma("sp", lambda e, wi_=wi_, g0=g0, nt=nt: e.dma_start(
                            out=o_ik[g0:g0 + nt, :], in_=wkA[wi_][:nt, 0:64]), r=[B_wkA[wi_]], w=[B_out])
                        S.op("dve", lambda e, ti=ti, pst=pst, nt=nt, rcol=rcol: e.tensor_scalar(
                            out=absw[:nt, ti * 16:(ti + 1) * 16], in0=pst[:nt, 64:80], scalar1=rcol, scalar2=1.0 / 32.0,
                            op0=ALU.mult, op1=ALU.mult), r=[Bp, B_rstd], w=[B_absw])
                        S.op("dve", lambda e, ti=ti, nt=nt: e.scalar_tensor_tensor(
                            out=absw[:nt, ti * 16:(ti + 1) * 16], in0=absw[:nt, ti * 16:(ti + 1) * 16],
                            scalar=-1.0, in1=absw[:nt, ti * 16:(ti + 1) * 16], op0=ALU.mult, op1=ALU.max),
                            r=[B_absw], w=[B_absw])
                        S.op("act", lambda e, ti=ti, pst=pst, nt=nt: e.activation(
                            out=sgnw[:nt, ti * 16:(ti + 1) * 16], in_=pst[:nt, 64:80], func=AF.Sign),
                            r=[Bp], w=[B_sgnw])
                        S.dma("sp", lambda e, ti=ti, g0=g0, nt=nt: e.dma_start(
                            out=sgn_scr[g0:g0 + nt, :], in_=sgnw[:nt, ti * 16:(ti + 1) * 16]), r=[B_sgnw], w=[B_q])
                    if (not own) or is_samp:
                        hi = cast_bf(wkA[wi_][:nt, 0:64], B_wkA[wi_], nt, 64)
                        if not own:
                            dsts = [(kiT[:, g0:g0 + nt], 0, nt, B_kv)]
                        else:
                            dsts = [(s_kiT[b, :, PAST:PAST + 16], 16 * b, 16, B_skv) for b in range(4)]
                        to_T_scratch(wkH[hi], B_wkH[hi], nt, 64, dsts)
                elif sec == "qi":
                    wi_ = norm_rope(pst, Bp, nt, ncols, 64, None, rcol, cs, B_ropeall, False)
                    A3 = wkA[wi_][:nt, :ncols].rearrange("p (g d) -> p g d", d=64)
                    h0 = off // 64
                    S.op("dve", lambda e, A3=A3, ti=ti, nt=nt, h0=h0: e.tensor_tensor(
                        out=A3, in0=A3,
                        in1=absw[:nt, ti * 16 + h0:ti * 16 + h0 + 8].unsqueeze(2).broadcast_to([nt, 8, 64]),
                        op=ALU.mult), r=[B_wkA[wi_], B_absw], w=[B_wkA[wi_]])
                    hi = cast_bf(wkA[wi_][:nt, :ncols], B_wkA[wi_], nt, ncols)
                    to_T_scratch(wkH[hi], B_wkH[hi], nt, ncols, [(qiT[off:off + ncols, g0:g0 + nt], 0, nt, B_q)])

    own_tiles = [(128 * j, 128, 128 * j, 128 * j) for j in range(NBLK)] + [(2048, 64, 2048, 2048)]
    own_blocks = [("kiwi", 0, 80)]
    for sec in ("qa", "ka", "va", "ga", "qb", "kb", "vb", "gb", "qi"):
        own_blocks += [(sec, 0, 512), (sec, 512, 512)]
    own_blocks += [("mg", 512 * k, 512) for k in range(8)]
    if KSEC:
        own_blocks = [b_ for b_ in own_blocks if b_[0] in KSEC.split(',')]
    proj_pass(x_own, rope_own, own_tiles, own_blocks, True, xT, B_xT)

    for ti, (tok0, nt, xc0, g0) in enumerate(own_tiles if KSTOP >= 2 else []):
        xi = nxt("x", NB2)
        S.dma("sp", lambda e, xi=xi, tok0=tok0, nt=nt: e.dma_start(out=xin[xi][:nt, 0:256], in_=p_own[tok0:tok0 + nt, :]),
              w=[B_xin[xi]])
        for k in range(2):
            S.op("pe", lambda e, xi=xi, nt=nt, k=k: e.transpose(out=psC[:, k * 128:k * 128 + nt],
                                                               in_=xin[xi][:nt, k * 128:(k + 1) * 128],
                                                               identity=ident_f[:nt, :nt]),
                 r=[B_xin[xi], B_identf], w=[B_psC])
        for k in range(2):
            S.op("dve", lambda e, nt=nt, k=k, xc0=xc0: e.tensor_copy(out=pTt[:, k, xc0:xc0 + nt],
                                                                     in_=psC[:, k * 128:k * 128 + nt]),
                 r=[B_psC], w=[B_pTt])

    kv_blocks = [("kiwi", 0, 80)]
    for sec in ("ka", "va", "kb", "vb"):
        kv_blocks += [(sec, 0, 512), (sec, 512, 512)]
    for g in range(0 if KSTOP < 2 else (1 if DBG else 8)):
        tiles = [(2048 * g + 128 * t, 128, 128 * t, 2048 * g + 128 * t) for t in range(16)]
        proj_pass(x_all, rope_all, tiles, kv_blocks, False, xT, B_xT)

    for b in range(4 if KSTOP >= 2 else 0):
        for (csrc, dst) in ((c_dv, s_va), (c_sv, s_vb)):
            for kt in range(32):
                S.dma("pool", lambda e, csrc=csrc, b=b, kt=kt: e.dma_start(
                    out=vstage[:, :], in_=csrc[b, kt * 128:(kt + 1) * 128, :]), w=[B_vstage])
                S.dma("sp", lambda e, dst=dst, b=b, kt=kt: e.dma_start(
                    out=dst[b, kt * 128:(kt + 1) * 128, :], in_=vstage[:, :]), r=[B_vstage], w=[B_skv])
        for (csrc, dst) in ((c_dk, s_kaT), (c_sk, s_kbT)):
            for kt in range(32):
                S.dma("sp", lambda e, csrc=csrc, b=b, kt=kt: e.dma_start(
                    out=kstage[:, :], in_=csrc[b, kt * 128:(kt + 1) * 128, :]), w=[B_kstage])
                for half in range(2):
                    pi = nxt("ps", 2)
                    for k in range(4):
                        ch = half * 4 + k
                        S.op("pe", lambda e, ch=ch, k=k, pi=pi: e.transpose(
                            out=psA[pi][:, k * 128:(k + 1) * 128], in_=kstage[:, ch * 128:(ch + 1) * 128],
                            identity=ident_f[:, :]), r=[B_kstage, B_identf], w=[B_psA[pi]])
                    eng = evac_engine()
                    if eng == "act":
                        S.op("act", lambda e, half=half, pi=pi: e.copy(
                            out=kTs[:, half * 4:(half + 1) * 4, :].rearrange("p a b -> p (a b)"), in_=psA[pi][:, :]),
                            r=[B_psA[pi]], w=[B_kTs])
                    else:
                        S.op("dve", lambda e, half=half, pi=pi: e.tensor_copy(
                            out=kTs[:, half * 4:(half + 1) * 4, :].rearrange("p a b -> p (a b)"), in_=psA[pi][:, :]),
                            r=[B_psA[pi]], w=[B_kTs])
                S.dma("sp", lambda e, dst=dst, b=b, kt=kt: e.dma_start(
                    out=dst[b, :, kt * 128:(kt + 1) * 128].rearrange("(ch p) t -> p ch t", p=128), in_=kTs[:, :, :]),
                    r=[B_kTs], w=[B_skv])
        for kt in range(32):
            S.dma("sp", lambda e, b=b, kt=kt: e.dma_start(out=kistage[:, :], in_=c_ik[b, kt * 128:(kt + 1) * 128, :]),
                  w=[B_kistage])
            S.op("pe", lambda e: e.transpose(out=psC[:64, 0:128], in_=kistage[:, :], identity=ident_f[:, :]),
                 r=[B_kistage, B_identf], w=[B_psC])
            S.op("dve", lambda e: e.tensor_copy(out=kiTs[:, :], in_=psC[:64, 0:128]), r=[B_psC], w=[B_kiTs])
            S.dma("sp", lambda e, b=b, kt=kt: e.dma_start(out=s_kiT[b, :, kt * 128:(kt + 1) * 128], in_=kiTs[:, :]),
                  r=[B_kiTs], w=[B_skv])

    areset()
    scores = aalloc([128, SEQ]); B_scores = Buf("scores")
    mbias = aalloc([128, SEQ], BF16); B_mbias = Buf("mbias")
    ktc = [aalloc([128, 2048], BF16) for i in range(NKB)]; B_ktc = [Buf(f"ktc{i}") for i in range(NKB)]
    vtc = [aalloc([128, 16, 130], BF16) for i in range(NKB)]; B_vtc = [Buf(f"vtc{i}") for i in range(NKB)]
    kic = [aalloc([128, 512], BF16) for i in range(NKB)]; B_kic = [Buf(f"kic{i}") for i in range(NKB)]
    qa_g = aalloc([128, 2 * 8 * 128], BF16).rearrange("p (c h t) -> p c h t", c=2, h=8); B_qa_g = Buf("qa_g")
    qb_g = aalloc([128, 8, 128], BF16); B_qb_g = Buf("qb_g")
    qi_g = aalloc([128, 16 * 128], BF16).rearrange("p (a two t) -> p a two t", two=2, t=128); B_qi_g = Buf("qi_g")
    sga_g = aalloc([128, 1024], BF16); B_sga_g = Buf("sga_g")
    sgb_g = aalloc([128, 1024], BF16); B_sgb_g = Buf("sgb_g")
    sgn_g = aalloc([128, 16]); B_sgn_g = Buf("sgn_g")
    dg = aalloc([128, 16, 128], BF16); B_dg = Buf("dg")
    u_g = aalloc([128, D], BF16); B_u_g = Buf("u_g")
    ustg = aalloc([128, NCH, 128], BF16); B_ustg = Buf("ustg")
    pT = [aalloc([128, 512], BF16) for i in range(NP)]; B_pT = [Buf(f"pT{i}") for i in range(NP)]
    rl = [aalloc([128, 512], BF16) for i in range(NP)]; B_rl = [Buf(f"rl{i}") for i in range(NP)]
    bis = aalloc([128, 8]); B_bis = Buf("bis")
    ep = aalloc([128, 16]); B_ep = Buf("ep")
    eo = aalloc([128, 256]); B_eo = Buf("eo")
    for i in range(NKB):
        S.op("dve", lambda e, i=i: e.memset(vtc[i][:, :, 128:129], 1.0), w=[B_vtc[i]])
    S.op("dve", lambda e: e.memset(qa_g.rearrange("p c h t -> p (c h t)"), 0.0), w=[B_qa_g])
    S.op("dve", lambda e: e.memset(qi_g.rearrange("p a two t -> p (a two t)"), 0.0), w=[B_qi_g])

    def attend(qg):
        nq, tok0, KS, B_KS, nkeys, masked = qg["nq"], qg["tok0"], qg["ks"], qg["bks"], qg["nkeys"], qg["masked"]
        k_aT, v_a, k_bT, v_b, k_iT = KS
        ti_w = qg["ti"]
        wrow0 = qg["wrow0"]
        ntiles_full = nkeys // 128
        rem = nkeys - ntiles_full * 128
        for c_ in range(2):
            S.dma("sp", lambda e, c_=c_: e.dma_start(
                out=qa_g[c_ * 64:(c_ + 1) * 64, c_, :, :nq],
                in_=qaT[:, tok0:tok0 + nq].rearrange("(h p) t -> p h t", p=128)[c_ * 64:(c_ + 1) * 64]),
                r=[B_q], w=[B_qa_g])
        S.dma("sp", lambda e: e.dma_start(out=qb_g[:, :, :nq], in_=qbT[:, tok0:tok0 + nq].rearrange("(h p) t -> p h t", p=128)),
              r=[B_q], w=[B_qb_g])
        for c_ in range(2):
            S.dma("sp", lambda e, c_=c_: e.dma_start(
                out=qi_g[c_ * 64:(c_ + 1) * 64, :, c_, :nq],
                in_=qiT[:, tok0:tok0 + nq].rearrange("(h p) t -> p h t", p=128)[c_ * 64:(c_ + 1) * 64]),
                r=[B_q], w=[B_qi_g])
        S.dma("sp", lambda e: e.dma_start(out=sga_g[:nq, :], in_=sga[tok0:tok0 + nq, :]), r=[B_g], w=[B_sga_g])
        S.dma("sp", lambda e: e.dma_start(out=sgb_g[:nq, :], in_=sgb[tok0:tok0 + nq, :]), r=[B_g], w=[B_sgb_g])
        S.dma("sp", lambda e: e.dma_start(out=sgn_g[:nq, :], in_=sgn_scr[tok0:tok0 + nq, :]), r=[B_q], w=[B_sgn_g])
        for h in range(16):
            S.op("dve", lambda e, h=h: e.tensor_scalar(
                out=dg[:nq, h, :nq], in0=ident_b[:nq, :nq],
                scalar1=sgn_g[:nq, h:h + 1], scalar2=None, op0=ALU.mult),
                r=[B_identb, B_sgn_g], w=[B_dg])
        nchunks = (nkeys + 511) // 512
        for kc in range(nchunks):
            k0 = kc * 512
            cw = min(512, nkeys - k0)
            ci = nxt("kb", NKB)
            for half in range(2):
                S.dma("sp", lambda e, ci=ci, half=half, k0=k0, cw=cw: e.dma_start(
                    out=kic[ci][half * 64:(half + 1) * 64, :cw], in_=k_iT[:, k0:k0 + cw]), r=[B_KS], w=[B_kic[ci]])
            is_tail = masked and kc >= nchunks - 2
            for h in range(16):
                li = nxt("psb", 2)
                hp = (h % 2) * 64
                S.op("pe", lambda e, h=h, li=li, hp=hp, ci=ci, cw=cw: e.matmul(
                    out=psL[li][:nq, :cw], lhsT=qi_g[:, h // 2, h % 2, :nq], rhs=kic[ci][:, :cw],
                    start=True, stop=True), r=[B_qi_g, B_kic[ci]], w=[B_psL[li]])
                ri = nxt("p", NP)
                eng = evac_engine()
                if eng == "act":
                    S.op("act", lambda e, li=li, ri=ri, cw=cw: e.activation(out=rl[ri][:nq, :cw], in_=psL[li][:nq, :cw],
                                                                          func=AF.Relu), r=[B_psL[li]], w=[B_rl[ri]])
                else:
                    S.op("dve", lambda e, li=li, ri=ri, cw=cw: e.tensor_scalar(
                        out=rl[ri][:nq, :cw], in0=psL[li][:nq, :cw], scalar1=0.0, scalar2=None, op0=ALU.max),
                        r=[B_psL[li]], w=[B_rl[ri]])
                S.op("pe", lambda e, h=h, ri=ri, cw=cw, is_tail=is_tail: e.matmul(
                    out=psC[:nq, :cw], lhsT=dg[:nq, h, :nq], rhs=rl[ri][:nq, :cw], start=(h == 0),
                    stop=(h == 15 and not is_tail)), r=[B_dg, B_rl[ri]], w=[B_psC])
            if is_tail:
                mo = (kc - (nchunks - 2)) * 512
                S.op("pe", lambda e, mo=mo, cw=cw: e.matmul(out=psC[:nq, :cw], lhsT=ident_b[:nq, :nq],
                                                            rhs=maskqk[:nq, mo:mo + cw], start=False, stop=True),
                     r=[B_identb, B_maskqk], w=[B_psC])
            S.op("act", lambda e, k0=k0, cw=cw: e.copy(out=scores[:nq, k0:k0 + cw], in_=psC[:nq, :cw]),
                 r=[B_psC], w=[B_scores])
        S.op("dve", lambda e: e.memset(bis[:nq, 0:1], BIS_LO), w=[B_bis])
        for it in range(NBIS):
            wd = BIS_W / (2.0 ** (it + 1))
            S.op("dve", lambda e, wd=wd: e.tensor_scalar(out=bis[:nq, 1:2], in0=bis[:nq, 0:1], scalar1=wd, scalar2=None,
                                                         op0=ALU.add), r=[B_bis], w=[B_bis])
            S.op("dve", lambda e: e.tensor_scalar(out=mbias[:nq, :nkeys], in0=scores[:nq, :nkeys], scalar1=bis[:nq, 1:2],
                                                  scalar2=0.0, op0=ALU.is_ge, op1=ALU.add, accum_out=bis[:nq, 2:3]),
                 r=[B_scores, B_bis], w=[B_mbias, B_bis])
            S.op("dve", lambda e: e.tensor_scalar(out=bis[:nq, 3:4], in0=bis[:nq, 2:3], scalar1=float(TOPK) - 0.5,
                                                  scalar2=None, op0=ALU.is_ge), r=[B_bis], w=[B_bis])
            S.op("dve", lambda e, wd=wd: e.scalar_tensor_tensor(out=bis[:nq, 0:1], in0=bis[:nq, 3:4], scalar=wd,
                                                                in1=bis[:nq, 0:1], op0=ALU.mult, op1=ALU.add),
                 r=[B_bis], w=[B_bis])
        S.op("dve", lambda e: e.tensor_scalar(out=mbias[:nq, :nkeys], in0=scores[:nq, :nkeys], scalar1=bis[:nq, 0:1],
                                              scalar2=NEG, op0=ALU.is_lt, op1=ALU.mult), r=[B_scores, B_bis], w=[B_mbias])

        def branch(is_diff):
            kT_d, v_d = (k_aT, v_a) if is_diff else (k_bT, v_b)
            q_g, Bq_g = (qa_g, B_qa_g) if is_diff else (qb_g, B_qb_g)
            ncomp = 2 if is_diff else 1
            per = 2 if is_diff else 4
            scale = (64.0 ** -0.5) if is_diff else (128.0 ** -0.5)
            tl = [(t, 128) for t in range(ntiles_full)] + ([(ntiles_full, rem)] if rem else [])
            ntl = len(tl)
            for h in range(8):
                oi = nxt("ps", 2)
                pso, Bpso = psB[oi], B_psB[oi]
                if is_diff:
                    pso_c = [psB[0], psB[1]]
                    Bpso_c = [B_psB[0], B_psB[1]]
                else:
                    pso_c = [pso]
                    Bpso_c = [Bpso]
                cur = {"chunk": -1, "ci": 0}

                def need_chunk(t):
                    c = t // 16
                    if c == cur["chunk"]:
                        return cur["ci"]
                    ci = nxt("kb", NKB)
                    k0 = c * 2048
                    cwk = min(2048, nkeys - k0)
                    nfull = min(16, ntiles_full - c * 16)
                    S.dma("sp", lambda e, ci=ci, k0=k0, cwk=cwk, h=h: e.dma_start(
                        out=ktc[ci][:, :cwk], in_=kT_d[h * 128:(h + 1) * 128, k0:k0 + cwk]), r=[B_KS], w=[B_ktc[ci]])
                    if nfull > 0:
                        S.dma("sp", lambda e, ci=ci, k0=k0, nfull=nfull, h=h: e.dma_start(
                            out=vtc[ci][:, :nfull, 0:128],
                            in_=v_d[k0:k0 + nfull * 128, h * 128:(h + 1) * 128].rearrange("(t p) d -> p t d", p=128)),
                            r=[B_KS], w=[B_vtc[ci]])
                    if rem and c == ntiles_full // 16:
                        tt = ntiles_full - c * 16
                        S.dma("sp", lambda e, ci=ci, tt=tt, h=h: e.dma_start(
                            out=vtc[ci][:rem, tt, 0:128],
                            in_=v_d[ntiles_full * 128:ntiles_full * 128 + rem, h * 128:(h + 1) * 128]),
                            r=[B_KS], w=[B_vtc[ci]])
                    cur["chunk"] = c
                    cur["ci"] = ci
                    return ci

                i = 0
                first = True
                while i < ntl:
                    batch = [tl[i]]
                    while (len(batch) < per and i + len(batch) < ntl and tl[i + len(batch)][1] == batch[0][1]
                           and tl[i + len(batch)][0] // 16 == batch[0][0] // 16):
                        batch.append(tl[i + len(batch)])
                    i += len(batch)
                    ksz = batch[0][1]
                    ci = need_chunk(batch[0][0])
                    si_ = nxt("psb", 2)
                    pss, Bpss = psA[si_], B_psA[si_]
                    nsl = len(batch) * ncomp
                    for bi, (t, _) in enumerate(batch):
                        tc_ = t % 16
                        mt = (t - (ntiles_full - 8)) if (masked and t >= ntiles_full - 8) else -1
                        for c in range(ncomp):
                            sl = bi * ncomp + c
                            if is_diff:
                                lhs = ktc[ci][:, tc_ * 128:tc_ * 128 + ksz]
                                rhs = q_g[:, c, h, :nq]
                            else:
                                lhs = ktc[ci][:, tc_ * 128:tc_ * 128 + ksz]
                                rhs = q_g[:, h, :nq]
                            has_mask = (mt >= 0) if is_diff else True
                            S.op("pe", lambda e, lhs=lhs, rhs=rhs, sl=sl, has_mask=has_mask, ksz=ksz, pss=pss: e.matmul(
                                out=pss[:ksz, sl * nq:(sl + 1) * nq], lhsT=lhs, rhs=rhs, start=True, stop=not has_mask),
                                r=[B_ktc[ci], Bq_g], w=[Bpss])
                            if is_diff and mt >= 0:
                                S.op("pe", lambda e, sl=sl, mt=mt, ksz=ksz, pss=pss: e.matmul(
                                    out=pss[:ksz, sl * nq:(sl + 1) * nq], lhsT=ident_b[:, :ksz],
                                    rhs=maskT[:, mt * 128:mt * 128 + nq], start=False, stop=True),
                                    r=[B_identb, B_maskT], w=[Bpss])
                            if not is_diff:
                                S.op("pe", lambda e, sl=sl, t=t, ksz=ksz, pss=pss: e.matmul(
                                    out=pss[:ksz, sl * nq:(sl + 1) * nq], lhsT=mbias[:nq, t * 128:t * 128 + ksz],
                                    rhs=ident_b[:nq, :nq], start=False, stop=True), r=[B_mbias, B_identb], w=[Bpss])
                    pi_ = nxt("p", NP)
                    S.op("act", lambda e, pi_=pi_, nsl=nsl, ksz=ksz, pss=pss: e.activation(
                        out=pT[pi_][:ksz, :nsl * nq], in_=pss[:ksz, :nsl * nq], func=AF.Exp, scale=scale),
                        r=[Bpss], w=[B_pT[pi_]])
                    for bi, (t, _) in enumerate(batch):
                        tc_ = t % 16
                        lastt = (t == tl[-1][0])
                        for c in range(ncomp):
                            sl = bi * ncomp + c
                            S.op("pe", lambda e, pi_=pi_, sl=sl, c=c, tc_=tc_, first=first, lastt=lastt, ksz=ksz, po=pso_c[c], ci=ci: e.matmul(
                                out=po[:nq, 0:129], lhsT=pT[pi_][:ksz, sl * nq:(sl + 1) * nq],
                                rhs=vtc[ci][:ksz, tc_, 0:129], start=first, stop=lastt), r=[B_pT[pi_], B_vtc[ci]], w=[Bpso_c[c]])
                        first = False
                if is_diff:
                    P0, P1 = psB[0], psB[1]
                    S.op("dve", lambda e: e.reciprocal(out=ep[:nq, 0:1], in_=P0[:nq, 128:129]), r=[B_psB[0]], w=[B_ep])
                    S.op("dve", lambda e: e.reciprocal(out=ep[:nq, 1:2], in_=P1[:nq, 128:129]), r=[B_psB[1]], w=[B_ep])
                    S.op("dve", lambda e: e.tensor_tensor(out=ep[:nq, 1:2], in0=ep[:nq, 1:2], in1=NEGLAM[:nq, :],
                                                          op=ALU.mult), r=[B_ep, B_lamt], w=[B_ep])
                    S.op("dve", lambda e: e.tensor_scalar(out=eo[:nq, 0:128], in0=P0[:nq, 0:128], scalar1=ep[:nq, 0:1],
                                                          scalar2=None, op0=ALU.mult), r=[B_psB[0], B_ep], w=[B_eo])
                    S.op("dve", lambda e: e.scalar_tensor_tensor(out=eo[:nq, 0:128], in0=P1[:nq, 0:128],
                                                                 scalar=ep[:nq, 1:2], in1=eo[:nq, 0:128],
                                                                 op0=ALU.mult, op1=ALU.add), r=[B_psB[1], B_ep, B_eo], w=[B_eo])
                    S.op("act", lambda e: e.activation(out=eo[:nq, 128:256], in_=eo[:nq, 0:128], func=AF.Square,
                                                       accum_out=ep[:nq, 2:3]), r=[B_eo], w=[B_eo, B_ep])
                    S.op("act", lambda e: e.activation(out=ep[:nq, 3:4], in_=ep[:nq, 2:3], func=AF.Sqrt, scale=1.0 / 128,
                                                       bias=epsc[:nq, :]), r=[B_ep, B_epsc], w=[B_ep])
                    S.op("dve", lambda e: e.reciprocal(out=ep[:nq, 3:4], in_=ep[:nq, 3:4]), r=[B_ep], w=[B_ep])
                    S.op("dve", lambda e: e.tensor_scalar(out=eo[:nq, 0:128], in0=eo[:nq, 0:128], scalar1=ep[:nq, 3:4],
                                                          scalar2=1.0 - LAM_INIT, op0=ALU.mult, op1=ALU.mult),
                         r=[B_eo, B_ep], w=[B_eo])
                    S.op("dve", lambda e: e.tensor_tensor(out=eo[:nq, 0:128], in0=eo[:nq, 0:128], in1=G_SUB[:nq, :],
                                                          op=ALU.mult), r=[B_eo, B_gains], w=[B_eo])
                    S.op("dve", lambda e, h=h: e.tensor_tensor(out=u_g[:nq, h * 128:(h + 1) * 128], in0=eo[:nq, 0:128],
                                                               in1=sga_g[:nq, h * 128:(h + 1) * 128], op=ALU.mult),
                         r=[B_eo, B_sga_g], w=[B_u_g])
                else:
                    S.op("dve", lambda e, pso=pso: e.reciprocal(out=ep[:nq, 0:1], in_=pso[:nq, 128:129]), r=[Bpso], w=[B_ep])
                    S.op("dve", lambda e, h=h, pso=pso: e.scalar_tensor_tensor(
                        out=u_g[:nq, 1024 + h * 128:1024 + (h + 1) * 128], in0=pso[:nq, 0:128], scalar=ep[:nq, 0:1],
                        in1=sgb_g[:nq, h * 128:(h + 1) * 128], op0=ALU.mult, op1=ALU.mult),
                        r=[Bpso, B_ep, B_sgb_g], w=[B_u_g])

        branch(True)
        if DBG and tok0 == 0:
            S.dma("sp", lambda e: e.dma_start(out=dbg_eo[:, :], in_=eo[:, :]), r=[B_eo], w=[B_out])
            S.dma("sp", lambda e: e.dma_start(out=dbg_ep[:, :], in_=ep[:, :]), r=[B_ep], w=[B_out])
            for i_ in range(2):
                S.op("dve", lambda e, i_=i_: e.tensor_copy(out=scores[:, i_ * 512:(i_ + 1) * 512], in_=psB[i_][:, :]),
                     r=[B_psB[i_]], w=[B_scores])
            S.dma("sp", lambda e: e.dma_start(out=dbg_ps[:, :], in_=scores[:, 0:1024]), r=[B_scores], w=[B_out])
            S.dma("sp", lambda e: e.dma_start(out=dbg_lam[:, :], in_=lamt[:, :]), r=[B_lamt], w=[B_out])
        branch(False)
        for q2 in range(2):
            for k in range(8):
                ch = q2 * 8 + k
                S.op("pe", lambda e, ch=ch, k=k: e.transpose(out=psT[:, k * 128:k * 128 + nq],
                                                             in_=u_g[:nq, ch * 128:(ch + 1) * 128],
                                                             identity=ident_b[:nq, :nq]), r=[B_u_g, B_identb], w=[B_psT])
            S.op("act", lambda e, q2=q2: e.copy(out=ustg[:, q2 * 8:(q2 + 1) * 8, :].rearrange("p a b -> p (a b)"),
                                                in_=psT[:, :]), r=[B_psT], w=[B_ustg])
        S.dma("sp", lambda e: e.dma_start(out=uT[:, tok0:tok0 + nq].rearrange("(ch p) t -> p ch t", p=128),
                                          in_=ustg[:, :, :nq]), r=[B_ustg], w=[B_u])

    qgs = []
    for j in range(NBLK):
        qgs.append(dict(nq=128, tok0=128 * j, ks=(kaT, va, kbT, vb, kiT), bks=B_kv, nkeys=(8 * j + 8) * 128,
                        masked=True, ti=j, wrow0=0))
    for b in range(4):
        qgs.append(dict(nq=16, tok0=2048 + 16 * b, ks=(s_kaT[b], s_va[b], s_kbT[b], s_vb[b], s_kiT[b]), bks=B_skv,
                        nkeys=NKS, masked=False, ti=16, wrow0=16 * b))
    KSAMP = int(os.environ.get('KSAMP', '1'))
    for qg in ([] if KSTOP < 3 else ([qgs[0]] + ([qgs[16]] if KSAMP else []) if DBG else (qgs if KSAMP else qgs[:16]))):
        attend(qg)

    areset()
    uTs = aalloc([128, NCH, NOWN], BF16); B_uTs = Buf("uTs")
    hTs, B_hTs = uTs, B_uTs
    wbfD = [aalloc([128, NCH, 512], BF16) for i in range(2)]; B_wbfD = [Buf(f"wbfD{i}") for i in range(2)]
    wbf2 = [aalloc([128, 8, 512], BF16) for i in range(2)]; B_wbf2 = [Buf(f"wbf2{i}") for i in range(2)]
    mTt = [aalloc([128, NCH, 128], BF16) for i in range(2)]; B_mTt = [Buf(f"mTt{i}") for i in range(2)]
    mab = [aalloc([128, 1024], BF16) for i in range(2)]; B_mab = [Buf(f"mab{i}") for i in range(2)]
    hld = [aalloc([128, 512]) for i in range(2)]; B_hld = [Buf(f"hld{i}") for i in range(2)]
    wkAD = [aalloc([128, 512]) for i in range(NW)]; B_wkAD = [Buf(f"wkAD{i}") for i in range(NW)]
    wkBD = [aalloc([128, 512]) for i in range(NW)]; B_wkBD = [Buf(f"wkBD{i}") for i in range(NW)]
    wkHD = [aalloc([128, 512], BF16) for i in range(NW)]; B_wkHD = [Buf(f"wkHD{i}") for i in range(NW)]
    stgD = [aalloc([128, 4 * 128], BF16) for i in range(NW)]; B_stgD = [Buf(f"stgD{i}") for i in range(NW)]
    B_m = Buf("m_scr")
    if KSTOP >= 4:
        S.dma("sp", lambda e: e.dma_start(out=uTs[:, :, :], in_=uT[:, :].rearrange("(ch p) t -> p ch t", p=128)),
              r=[B_u], w=[B_uTs])
    for n in range(4 if KSTOP >= 4 else 0):
        wa = load_w(w_ba, n * 512, 512, 8, wbf2, B_wbf2)
        wb_ = load_w(w_bb, n * 512, 512, 8, wbf2, B_wbf2)
        for ti, (tok0, nt, xc0, g0) in enumerate(own_tiles):
            mi = nxt("x", 2)
            S.dma("sp", lambda e, mi=mi, tok0=tok0, nt=nt, n=n: e.dma_start(
                out=mab[mi][:nt, :].rearrange("p (a c) -> p a c", a=2),
                in_=smg[tok0:tok0 + nt, :].rearrange("t (a c) -> t a c", a=2)[:, :, n * 512:(n + 1) * 512]),
                r=[B_g], w=[B_mab[mi]])
            pa, Bpa = psA[0], B_psA[0]
            pb, Bpb = psA[1], B_psA[1]
            for ch in range(8):
                S.op("pe", lambda e, ch=ch, xc0=xc0, nt=nt, wa=wa: e.matmul(
                    out=pa[:nt, :], lhsT=uTs[:, ch, xc0:xc0 + nt], rhs=wbf2[wa][:, ch, :], start=(ch == 0),
                    stop=(ch == 7)), r=[B_uTs, B_wbf2[wa]], w=[Bpa])
            for ch in range(8):
                S.op("pe", lambda e, ch=ch, xc0=xc0, nt=nt, wb_=wb_: e.matmul(
                    out=pb[:nt, :], lhsT=uTs[:, 8 + ch, xc0:xc0 + nt], rhs=wbf2[wb_][:, ch, :], start=(ch == 0),
                    stop=(ch == 7)), r=[B_uTs, B_wbf2[wb_]], w=[Bpb])
            wi_ = nxt("wk", NW)
            S.op("dve", lambda e, wi_=wi_, mi=mi, nt=nt: e.tensor_tensor(out=wkAD[wi_][:nt, :], in0=pa[:nt, :],
                                                                          in1=mab[mi][:nt, 0:512], op=ALU.mult),
                 r=[Bpa, B_mab[mi]], w=[B_wkAD[wi_]])
            S.op("dve", lambda e, wi_=wi_, mi=mi, nt=nt: e.tensor_tensor(out=wkBD[wi_][:nt, :], in0=pb[:nt, :],
                                                                          in1=mab[mi][:nt, 512:1024], op=ALU.mult),
                 r=[Bpb, B_mab[mi]], w=[B_wkBD[wi_]])
            hi = nxt("l", NW)
            S.op("dve", lambda e, wi_=wi_, hi=hi, nt=nt: e.tensor_tensor(out=wkHD[hi][:nt, :], in0=wkAD[wi_][:nt, :],
                                                                          in1=wkBD[wi_][:nt, :], op=ALU.add),
                 r=[B_wkAD[wi_], B_wkBD[wi_]], w=[B_wkHD[hi]])
            to_T_scratch(wkHD[hi], B_wkHD[hi], nt, 512, [(mT_scr[n * 512:(n + 1) * 512, tok0:tok0 + nt], 0, nt, B_m)], (stgD, B_stgD))
    S.op("dve", lambda e: e.memset(absw[:, :], 0.0), r=[], w=[B_absw])
    for n in range(4 if KSTOP >= 4 else 0):
        wo = load_w(w_out, n * 512, 512, NCH, wbfD, B_wbfD)
        for ti, (tok0, nt, xc0, g0) in enumerate(own_tiles):
            hi_ = nxt("x", 2)
            S.dma("sp", lambda e, hi_=hi_, tok0=tok0, nt=nt, n=n: e.dma_start(
                out=hld[hi_][:nt, :], in_=x_own[tok0:tok0 + nt, n * 512:(n + 1) * 512]), w=[B_hld[hi_]])
            S.dma("sp", lambda e, hi_=hi_, tok0=tok0, nt=nt: e.dma_start(
                out=mTt[hi_][:, :, :nt], in_=mT_scr[:, tok0:tok0 + nt].rearrange("(ch p) t -> p ch t", p=128)),
                r=[B_m], w=[B_mTt[hi_]])
            pi = nxt("ps", 2)
            gemm_tile(psA[pi], B_psA[pi], mTt[hi_], B_mTt[hi_], 0, nt, wbfD[wo], B_wbfD[wo], NCH, 512)
            wi_ = nxt("wk", NW)
            S.op("dve", lambda e, wi_=wi_, hi_=hi_, pi=pi, nt=nt: e.tensor_tensor(
                out=wkAD[wi_][:nt, :], in0=psA[pi][:nt, :], in1=hld[hi_][:nt, :], op=ALU.add),
                r=[B_psA[pi], B_hld[hi_]], w=[B_wkAD[wi_]])
            S.dma("sp", lambda e, wi_=wi_, tok0=tok0, nt=nt, n=n: e.dma_start(
                out=hbuf[tok0:tok0 + nt, n * 512:(n + 1) * 512], in_=wkAD[wi_][:nt, :]), r=[B_wkAD[wi_]], w=[B_h])
            S.op("act", lambda e, wi_=wi_, nt=nt, ti=ti, n=n: e.activation(
                out=wkBD[wi_][:nt, :], in_=wkAD[wi_][:nt, :], func=AF.Square,
                accum_out=absw[:nt, ti * 16 + n:ti * 16 + n + 1]), r=[B_wkAD[wi_]], w=[B_wkBD[wi_], B_absw])
            for k in range(4):
                S.op("pe", lambda e, wi_=wi_, k=k, nt=nt: e.transpose(out=psC[:, k * 128:k * 128 + nt],
                                                                     in_=wkAD[wi_][:nt, k * 128:(k + 1) * 128],
                                                                     identity=ident_f[:nt, :nt]),
                     r=[B_wkAD[wi_], B_identf], w=[B_psC])
            for k in range(4):
                S.op("dve", lambda e, k=k, n=n, xc0=xc0, nt=nt: e.tensor_scalar(
                    out=hTs[:, 4 * n + k, xc0:xc0 + nt], in0=psC[:, k * 128:k * 128 + nt],
                    scalar1=pgcol[:, 4 * n + k:4 * n + k + 1], scalar2=None, op0=ALU.mult),
                    r=[B_psC, B_pgcol], w=[B_hTs])
    for ti, (tok0, nt, xc0, g0) in enumerate(own_tiles if KSTOP >= 4 else []):
        S.op("dve", lambda e, ti=ti, nt=nt: e.tensor_reduce(out=ssqt[:nt, ti:ti + 1], in_=absw[:nt, ti * 16:ti * 16 + 4],
                                                            axis=AX.X, op=ALU.add), r=[B_absw], w=[B_ssqt])
        S.op("act", lambda e, ti=ti, nt=nt: e.activation(out=rstd[:nt, ti:ti + 1], in_=ssqt[:nt, ti:ti + 1], func=AF.Sqrt,
                                                         scale=1.0 / D, bias=epsc[:nt, :]), r=[B_ssqt, B_epsc], w=[B_rstd])
        S.op("dve", lambda e, ti=ti, nt=nt: e.reciprocal(out=rstd[:nt, ti:ti + 1], in_=rstd[:nt, ti:ti + 1]),
             r=[B_rstd], w=[B_rstd])
    for n in range(4 if KSTOP >= 4 else 0):
        wg = load_w(w_pg, n * 512, 512, NCH, wbfD, B_wbfD)
        wp = load_w(w_ple, n * 512, 512, 2, wbf2, B_wbf2)
        for ti, (tok0, nt, xc0, g0) in enumerate(own_tiles):
            hi_ = nxt("x", 2)
            S.dma("sp", lambda e, hi_=hi_, tok0=tok0, nt=nt, n=n: e.dma_start(
                out=hld[hi_][:nt, :], in_=hbuf[tok0:tok0 + nt, n * 512:(n + 1) * 512]), r=[B_h], w=[B_hld[hi_]])
            pi = nxt("ps", 2)
            gemm_tile(psA[pi], B_psA[pi], hTs, B_hTs, xc0, nt, wbfD[wg], B_wbfD[wg], NCH, 512)
            gemm_tile(psB[pi], B_psB[pi], pTt, B_pTt, xc0, nt, wbf2[wp], B_wbf2[wp], 2, 512)
            wi_ = nxt("wk", NW)
            S.op("dve", lambda e, wi_=wi_, pi=pi, nt=nt, ti=ti: e.tensor_scalar(
                out=wkAD[wi_][:nt, :], in0=psA[pi][:nt, :], scalar1=rstd[:nt, ti:ti + 1], scalar2=None, op0=ALU.mult),
                r=[B_psA[pi], B_rstd], w=[B_wkAD[wi_]])
            S.op("act", lambda e, wi_=wi_, nt=nt: e.activation(
                out=wkAD[wi_][:nt, :], in_=wkAD[wi_][:nt, :], func=AF.Sigmoid),
                r=[B_wkAD[wi_]], w=[B_wkAD[wi_]])
            S.op("dve", lambda e, wi_=wi_, pi=pi, nt=nt: e.tensor_tensor(out=wkBD[wi_][:nt, :], in0=wkAD[wi_][:nt, :],
                                                                          in1=psB[pi][:nt, :], op=ALU.mult),
                 r=[B_wkAD[wi_], B_psB[pi]], w=[B_wkBD[wi_]])
            S.op("dve", lambda e, wi_=wi_, hi_=hi_, nt=nt: e.tensor_tensor(out=wkBD[wi_][:nt, :], in0=wkBD[wi_][:nt, :],
                                                                            in1=hld[hi_][:nt, :], op=ALU.add),
                 r=[B_wkBD[wi_], B_hld[hi_]], w=[B_wkBD[wi_]])
            S.dma("sp", lambda e, wi_=wi_, tok0=tok0, nt=nt, n=n: e.dma_start(
                out=y_own[tok0:tok0 + nt, n * 512:(n + 1) * 512], in_=wkBD[wi_][:nt, :]), r=[B_wkBD[wi_]], w=[B_out])

    with nc.Block() as block:
        @block.tensor
        def _(e):
            S.emit("pe", e)

        @block.scalar
        def _(e):
            S.emit("act", e)

        @block.vector
        def _(e):
            S.emit("dve", e)

        @block.gpsimd
        def _(e):
            S.emit("pool", e)

        @block.sync
        def _(e):
            S.emit("sp", e)
            S.final_waits(e)
    es.close()
    return nc, S


def _rope_table(pos):
    out = np.zeros((len(pos), 48), np.float32)
    p = pos.astype(np.float32)[:, None]
    for (r, c0) in ((16, 0), (32, 16)):
        inv = (np.float32(500000.0) ** (-np.arange(0, r, 2, dtype=np.float32) / np.float32(r))).astype(np.float32)
        ang = (p * inv[None, :]).astype(np.float32)
        out[:, c0:c0 + r // 2] = np.cos(ang)
        out[:, c0 + r // 2:c0 + r] = np.sin(ang)
    return out


_CACHE = {}


def kernel(x_prompt, x_sample, p_prompt, p_sample, cache_diff_k, cache_diff_v, cache_dsa_k, cache_dsa_v,
           cache_idx_k, ln_g, w_in, q_norm_a, k_norm_a, lam_q1, lam_k1, lam_q2, lam_k2, subln_a,
           q_norm_b, k_norm_b, k_norm_idx, w_branch_a, w_branch_b, w_out, ple_norm, w_ple_gate, w_ple):
    f = lambda a: np.ascontiguousarray(np.asarray(a, dtype=np.float32))
    xp = f(x_prompt)[0]
    xs = f(x_sample).reshape(512, D)
    pp = f(p_prompt)[0, 0]
    psm = f(p_sample)[0].reshape(512, 256)
    cdk = f(cache_diff_k)[0].reshape(32, PAST, 1024)
    cdv = f(cache_diff_v)[0].reshape(32, PAST, 1024)
    csk = f(cache_dsa_k)[0].reshape(32, PAST, 1024)
    csv = f(cache_dsa_v)[0].reshape(32, PAST, 1024)
    cik = f(cache_idx_k)[0]
    if "nc" not in _CACHE:
        _CACHE["nc"] = build_program()
    nc, S = _CACHE["nc"]
    rope_all = _rope_table(np.arange(SEQ))
    rope_s = _rope_table(PAST + np.arange(16))
    ident = np.eye(128, dtype=np.float32)
    shared = dict(x_all=xp[:XROWS], rope_all=rope_all[:XROWS], w_in=f(w_in)[0], ln_g=f(ln_g)[0], q_norm_a=f(q_norm_a)[0],
                  k_norm_a=f(k_norm_a)[0], lam_q1=f(lam_q1)[0], lam_k1=f(lam_k1)[0], lam_q2=f(lam_q2)[0],
                  lam_k2=f(lam_k2)[0], subln_a=f(subln_a)[0], q_norm_b=f(q_norm_b)[0], k_norm_b=f(k_norm_b)[0],
                  k_norm_idx=f(k_norm_idx)[0], w_branch_a=f(w_branch_a)[0], w_branch_b=f(w_branch_b)[0],
                  w_out=f(w_out)[0], ple_norm=f(ple_norm)[0], w_ple_gate=f(w_ple_gate)[0], w_ple=f(w_ple)[0],
                  ident=ident)
    in_maps = []
    rows_all = []
    kk = np.arange(128)[:, None]
    qq = np.arange(128)[None, :]
    diag_vis = (kk < 64) | (qq >= 64)
    for c in range(NCORES):
        rows = np.concatenate([np.arange((8 * j + c) * 128, (8 * j + c + 1) * 128) for j in range(NBLK)])
        rows_all.append(rows)
        x_own = np.concatenate([xp[rows], xs[64 * c:64 * (c + 1)]], 0)
        p_own = np.concatenate([pp[rows], psm[64 * c:64 * (c + 1)]], 0)
        rope_own = np.concatenate([rope_all[rows], np.tile(rope_s, (4, 1))], 0)
        mT = np.zeros((128, 8, 128), np.float32)
        for m in range(8):
            if m > c:
                mT[:, m, :] = NEG
            elif m == c:
                mT[:, m, :] = np.where(diag_vis, 0.0, NEG)
        mqk = np.ascontiguousarray(mT.transpose(2, 1, 0))
        d = dict(shared)
        d.update(x_own=np.ascontiguousarray(x_own), p_own=np.ascontiguousarray(p_own),
                 rope_own=np.ascontiguousarray(rope_own),
                 c_dk=np.ascontiguousarray(cdk[4 * c:4 * c + 4, :CROWS]), c_dv=np.ascontiguousarray(cdv[4 * c:4 * c + 4, :CROWS]),
                 c_sk=np.ascontiguousarray(csk[4 * c:4 * c + 4, :CROWS]), c_sv=np.ascontiguousarray(csv[4 * c:4 * c + 4, :CROWS]),
                 c_ik=np.ascontiguousarray(cik[4 * c:4 * c + 4, :CROWS]),
                 maskT=mT.reshape(128, 1024), maskqk=mqk.reshape(128, 1024))
        in_maps.append(d)
    res = run_bass_kernel_spmd(nc, in_maps[:KCORES], core_ids=list(range(KCORES)))
    R = res.results
    _CACHE['dbg'] = R[0]
    y_p = np.zeros((1, SEQ, D), np.float32)
    y_s = np.zeros((32, 16, D), np.float32)
    outs_p = {k: np.zeros((SEQ, n), np.float32) for k, n in (("o_dk", 1024), ("o_dv", 1024), ("o_sk", 1024),
                                                                ("o_sv", 1024), ("o_ik", 64))}
    outs_s = {k: np.zeros((512, n), np.float32) for k, n in (("o_dk", 1024), ("o_dv", 1024), ("o_sk", 1024),
                                                               ("o_sv", 1024), ("o_ik", 64))}
    for c in range(KCORES):
        r = R[c]
        yo = np.asarray(r["y_own"])
        y_p[0, rows_all[c]] = yo[:2048]
        y_s[4 * c:4 * c + 4] = yo[2048:].reshape(4, 16, D)
        for k in outs_p:
            a = np.asarray(r[k])
            outs_p[k][rows_all[c]] = a[:2048]
            outs_s[k][64 * c:64 * (c + 1)] = a[2048:]
    return (y_p, y_s,
            outs_p["o_dk"].reshape(1, 1, SEQ, 8, 2, 64), outs_p["o_dv"].reshape(1, 1, SEQ, 8, 128),
            outs_p["o_sk"].reshape(1, 1, SEQ, 8, 128), outs_p["o_sv"].reshape(1, 1, SEQ, 8, 128),
            outs_p["o_ik"].reshape(1, 1, SEQ, 64),
            outs_s["o_dk"].reshape(1, 32, 16, 8, 2, 64), outs_s["o_dv"].reshape(1, 32, 16, 8, 128),
            outs_s["o_sk"].reshape(1, 32, 16, 8, 128), outs_s["o_sv"].reshape(1, 32, 16, 8, 128),
            outs_s["o_ik"].reshape(1, 32, 16, 64))
```

```python
import contextlib
import numpy as np
import concourse.bass as bass
import concourse.mybir as mybir
from concourse.bass_utils import run_bass_kernel_spmd

F32 = mybir.dt.float32
BF16 = mybir.dt.bfloat16
ALU = mybir.AluOpType
AF = mybir.ActivationFunctionType
AX = mybir.AxisListType

NCORES = 8
D = 2048
NCH = 16
SEQ = 16384
NBLK = 16
NOWN = NBLK * 128 + 64
PAST = 4096
NKS = PAST + 16
N_IN = 13392
EPS = 1e-6
import os
DBG = bool(int(os.environ.get('KDBG', '0')))
KSTOP = int(os.environ.get('KSTOP', '9'))
SMALL = DBG and KSTOP < 2
KCORES = int(os.environ.get('KCORES', '8'))
KMAXOPS = int(os.environ.get('KMAXOPS', '100000000'))
KSEC = os.environ.get('KSEC', '')
KTRACE = [int(v) for v in os.environ.get('KTRACE', '').split(',')] if os.environ.get('KTRACE') else None
XROWS = 128 if SMALL else 16384
CROWS = 128 if SMALL else 4096
NEG = -30000.0
NBIS = 20
BIS_LO = -32.0
BIS_W = 64.0
TOPK = 256
LAM_INIT = 0.2


class Buf:
    __slots__ = ("name", "w", "r")

    def __init__(self, name):
        self.name = name
        self.w = None
        self.r = []


class Sched:
    def __init__(self, sems):
        self.free = list(sems)
        self.q = {e: [] for e in ("pe", "act", "dve", "pool", "sp")}
        self.cnt = {}
        self.sem = {}
        for e in ("pe", "act", "dve"):
            self.sem[e] = self.free.pop()
            self.cnt[e] = 0
        self.dsem = {"sp": [self.free.pop() for _ in range(20)],
                     "pool": [self.free.pop() for _ in range(20)]}
        self.dval = {}
        self.drot = {"sp": 0, "pool": 0}
        self.known = {e: {} for e in self.q}
        self.nops = 0

    def _waits(self, eng, r, w):
        need = {}
        for b in r:
            if b.w is not None:
                s, v = b.w
                need[s] = max(need.get(s, 0), v)
        for b in w:
            if b.w is not None:
                s, v = b.w
                need[s] = max(need.get(s, 0), v)
            for (s, v) in b.r:
                need[s] = max(need.get(s, 0), v)
        out = []
        kn = self.known[eng]
        for s, v in need.items():
            if eng == "pe" and s is self.sem.get("pe"):
                continue
            if kn.get(id(s), 0) >= v:
                continue
            kn[id(s)] = v
            out.append((s, v))
        return out

    def _commit(self, ev, r, w):
        for b in r:
            b.r.append(ev)
            if len(b.r) > 64:
                m = {}
                for (s, v) in b.r:
                    k = id(s)
                    if k not in m or m[k][1] < v:
                        m[k] = (s, v)
                b.r = list(m.values())
        for b in w:
            b.w = ev
            b.r = []

    def op(self, eng, fn, r=(), w=()):
        if self.nops >= KMAXOPS:
            return
        waits = self._waits(eng, r, w)
        if self.cnt[eng] >= 30000:
            self.sem[eng] = self.free.pop()
            self.cnt[eng] = 0
        self.cnt[eng] += 1
        s = self.sem[eng]
        ev = (s, self.cnt[eng])
        self.q[eng].append((waits, fn, s, 1))
        self._commit(ev, r, w)
        self.nops += 1
        if KTRACE and KTRACE[0] <= self.nops <= KTRACE[1]:
            print("OP", self.nops, eng, fn.__code__.co_firstlineno, flush=True)

    def dma(self, qe, fn, r=(), w=()):
        if self.nops >= KMAXOPS:
            return
        waits = self._waits(qe, r, w)
        i = self.drot[qe]
        self.drot[qe] = (i + 1) % len(self.dsem[qe])
        s = self.dsem[qe][i]
        prev = self.dval.get(id(s), 0)
        kn = self.known[qe]
        if prev > 0 and kn.get(id(s), 0) < prev:
            kn[id(s)] = prev
            waits.append((s, prev))
        self.dval[id(s)] = prev + 16
        ev = (s, prev + 16)
        self.q[qe].append((waits, fn, s, 16))
        self._commit(ev, r, w)
        self.nops += 1
        if KTRACE and KTRACE[0] <= self.nops <= KTRACE[1]:
            print("DMA", self.nops, qe, fn.__code__.co_firstlineno, flush=True)

    def barrier(self):
        evs = []
        for en in ("pe", "act", "dve"):
            if self.cnt[en] > 0:
                evs.append((self.sem[en], self.cnt[en]))
        for qe in ("sp", "pool"):
            for s in self.dsem[qe]:
                v = self.dval.get(id(s), 0)
                if v > 0:
                    evs.append((s, v))
        for eng in self.q:
            kn = self.known[eng]
            waits = []
            for (s, v) in evs:
                if kn.get(id(s), 0) >= v:
                    continue
                kn[id(s)] = v
                waits.append((s, v))
            self.q[eng].append((waits, None, None, 0))

    def emit(self, qe, e):
        for (waits, fn, s, inc) in self.q[qe]:
            for (ws, wv) in waits:
                e.wait_ge(ws, wv)
            if fn is not None:
                fn(e).then_inc(s, inc)

    def final_waits(self, e):
        for qe in ("sp", "pool"):
            for s in self.dsem[qe]:
                v = self.dval.get(id(s), 0)
                if v > 0:
                    e.wait_ge(s, v)
        for en in ("pe", "act", "dve"):
            if self.cnt[en] > 0:
                e.wait_ge(self.sem[en], self.cnt[en])


def build_program():
    nc = bass.Bass("TRN2", target_bir_lowering=False)
    es = contextlib.ExitStack()

    def din(name, shape, dt=F32):
        return nc.dram_tensor(name, list(shape), dt, kind="ExternalInput")

    def dout(name, shape, dt=F32):
        return nc.dram_tensor(name, list(shape), dt, kind="ExternalOutput")

    def dscr(name, shape, dt=BF16):
        if DBG and name in ("uT", "hbuf", "mT_scr", "qaT", "qbT", "qiT", "sga", "sgn_scr", "kiT", "kaT", "va"):
            return nc.dram_tensor(name, list(shape), dt, kind="ExternalOutput")
        return nc.dram_tensor(name, list(shape), dt, kind="Internal")

    x_all = din("x_all", [XROWS, D]).ap()
    x_own = din("x_own", [NOWN, D]).ap()
    p_own = din("p_own", [NOWN, 256]).ap()
    rope_all = din("rope_all", [XROWS, 48]).ap()
    rope_own = din("rope_own", [NOWN, 48]).ap()
    c_dk = din("c_dk", [4, CROWS, 1024]).ap()
    c_dv = din("c_dv", [4, CROWS, 1024]).ap()
    c_sk = din("c_sk", [4, CROWS, 1024]).ap()
    c_sv = din("c_sv", [4, CROWS, 1024]).ap()
    c_ik = din("c_ik", [4, CROWS, 64]).ap()
    w_in = din("w_in", [D, N_IN]).ap()
    ln_g_h = din("ln_g", [D])
    vec_h = {}
    for nm, n in (("q_norm_a", 64), ("k_norm_a", 64), ("lam_q1", 64), ("lam_k1", 64), ("lam_q2", 64),
                  ("lam_k2", 64), ("subln_a", 128), ("q_norm_b", 128), ("k_norm_b", 128), ("k_norm_idx", 64)):
        vec_h[nm] = din(nm, [n])
    w_ba = din("w_branch_a", [1024, D]).ap()
    w_bb = din("w_branch_b", [1024, D]).ap()
    w_out = din("w_out", [D, D]).ap()
    ple_g_h = din("ple_norm", [D])
    w_pg = din("w_ple_gate", [D, D]).ap()
    w_ple = din("w_ple", [256, D]).ap()
    ident_d = din("ident", [128, 128]).ap()
    maskT_d = din("maskT", [128, 8 * 128]).ap()
    maskqk_d = din("maskqk", [128, 8 * 128]).ap()

    if DBG:
        dbg_eo = dout("dbg_eo", [128, 256]).ap()
        dbg_ep = dout("dbg_ep", [128, 16]).ap()
        dbg_ps = dout("dbg_ps", [128, 1024]).ap()
        dbg_lam = dout("dbg_lam", [128, 8]).ap()
    y_own = dout("y_own", [NOWN, D]).ap()
    o_dk = dout("o_dk", [NOWN, 1024]).ap()
    o_dv = dout("o_dv", [NOWN, 1024]).ap()
    o_sk = dout("o_sk", [NOWN, 1024]).ap()
    o_sv = dout("o_sv", [NOWN, 1024]).ap()
    o_ik = dout("o_ik", [NOWN, 64]).ap()

    kaT = dscr("kaT", [1024, SEQ]).ap()
    va = dscr("va", [SEQ, 1024]).ap()
    kbT = dscr("kbT", [1024, SEQ]).ap()
    vb = dscr("vb", [SEQ, 1024]).ap()
    kiT = dscr("kiT", [64, SEQ]).ap()
    s_kaT = dscr("s_kaT", [4, 1024, NKS]).ap()
    s_va = dscr("s_va", [4, NKS, 1024]).ap()
    s_kbT = dscr("s_kbT", [4, 1024, NKS]).ap()
    s_vb = dscr("s_vb", [4, NKS, 1024]).ap()
    s_kiT = dscr("s_kiT", [4, 64, NKS]).ap()
    qaT = dscr("qaT", [1024, NOWN]).ap()
    qbT = dscr("qbT", [1024, NOWN]).ap()
    qiT = dscr("qiT", [1024, NOWN]).ap()
    sga = dscr("sga", [NOWN, 1024]).ap()
    sgb = dscr("sgb", [NOWN, 1024]).ap()
    smg = dscr("smg", [NOWN, 4096]).ap()
    uT = dscr("uT", [D, NOWN]).ap()
    hbuf = dscr("hbuf", [NOWN, D], F32).ap()
    sgn_scr = dscr("sgn_scr", [NOWN, 16], F32).ap()
    mT_scr = dscr("mT_scr", [D, NOWN]).ap()
    B_kv = Buf("kv_scr")
    B_skv = Buf("skv_scr")
    B_q = Buf("q_scr")
    B_g = Buf("g_scr")
    B_u = Buf("u_scr")
    B_h = Buf("h_scr")
    B_out = Buf("outs")

    def sb(name, shape, dt=F32):
        return es.enter_context(nc.sbuf_tensor(name, list(shape), dt))

    def ps(name, shape, dt=F32):
        return es.enter_context(nc.psum_tensor(name, list(shape), dt))

    sems = [es.enter_context(nc.semaphore(f"s{i}")) for i in range(90)]
    S = Sched(sems)

    ident_f = sb("ident_fsb", [128, 128]); B_identf = Buf("identf")
    ident_b = sb("ident_b", [128, 128], BF16); B_identb = Buf("identb")
    maskT = sb("maskT_sb", [128, 1024], BF16); B_maskT = Buf("maskT")
    maskqk = sb("maskqk_sb", [128, 1024], BF16); B_maskqk = Buf("maskqk")
    gcol = sb("gcol", [128, 16]); B_gcol = Buf("gcol")
    pgcol = sb("pgcol", [128, 16]); B_pgcol = Buf("pgcol")
    gains = sb("gains", [128, 6 * 128]); B_gains = Buf("gains")
    lamv = sb("lamv", [128, 4 * 64]); B_lamv = Buf("lamv")
    lamt = sb("lamt", [128, 8]); B_lamt = Buf("lamt")
    lamj = sb("lamj", [128, 64]); B_lamj = Buf("lamj")
    epsc = sb("epsc", [128, 1]); B_epsc = Buf("epsc")
    rstd = sb("rstd", [128, 20]); B_rstd = Buf("rstd")
    ssqt = sb("ssqt", [128, 20]); B_ssqt = Buf("ssqt")
    absw = sb("absw", [128, 17 * 16]); B_absw = Buf("absw")
    sgnw = sb("sgnw", [128, 17 * 16]); B_sgnw = Buf("sgnw")
    pTt = sb("pTt", [128, 2, NOWN], BF16); B_pTt = Buf("pTt")
    ARENA = 180224
    arena = sb("arena", [128, ARENA // 2], BF16)
    ar = {"off": 0}

    def aalloc(shape, dt=F32):
        n = 1
        for s_ in shape[1:]:
            n *= s_
        nb = n * (4 if dt == F32 else 2)
        nb = (nb + 63) // 64 * 64
        o = ar["off"]
        ar["off"] = o + nb
        assert ar["off"] <= ARENA, (ar["off"], ARENA)
        v = arena[:, o // 2:(o + n * (4 if dt == F32 else 2)) // 2]
        if dt == F32:
            v = v.bitcast(F32)
        if len(shape) == 3:
            v = v.rearrange("p (a b) -> p a b", b=shape[2])
        if shape[0] < 128:
            v = v[:shape[0]]
        return v

    def areset():
        S.barrier()
        ar["off"] = 0

    NB2 = 2
    NW = 3
    NKB = 3
    NP = 3

    psA = [ps("psA0", [128, 512]), ps("psA1", [128, 512])]; B_psA = [Buf("psA0"), Buf("psA1")]
    psB = [ps("psB0", [128, 512]), ps("psB1", [128, 512])]; B_psB = [Buf("psB0"), Buf("psB1")]
    psL = [ps("psL0", [128, 512]), ps("psL1", [128, 512])]; B_psL = [Buf("psL0"), Buf("psL1")]
    psC = ps("psC", [128, 512]); B_psC = Buf("psC")
    psT = ps("psT", [128, 1024], BF16); B_psT = Buf("psT")

    xT = aalloc([128, NCH, NOWN], BF16); B_xT = Buf("xT")
    xin = [aalloc([128, D]) for i in range(NB2)]; B_xin = [Buf(f"xin{i}") for i in range(NB2)]
    xin_junk = aalloc([128, D], BF16); B_junk = Buf("junk")
    wbf = [aalloc([128, NCH, 512], BF16) for i in range(2)]; B_wbf = [Buf(f"wbf{i}") for i in range(2)]
    ropeall = aalloc([128, 17 * 48]); B_ropeall = Buf("ropeall")
    wkA = [aalloc([128, 512]) for i in range(NW)]; B_wkA = [Buf(f"wkA{i}") for i in range(NW)]
    wkB = [aalloc([128, 512]) for i in range(NW)]; B_wkB = [Buf(f"wkB{i}") for i in range(NW)]
    wkS = [aalloc([128, 16]) for i in range(NW)]; B_wkS = [Buf(f"wkS{i}") for i in range(NW)]
    wkR = [aalloc([128, 4 * 64]) for i in range(NW)]; B_wkR = [Buf(f"wkR{i}") for i in range(NW)]
    wkH = [aalloc([128, 512], BF16) for i in range(NW)]; B_wkH = [Buf(f"wkH{i}") for i in range(NW)]
    stg = [aalloc([128, 4 * 128], BF16) for i in range(NW)]; B_stg = [Buf(f"stg{i}") for i in range(NW)]
    kstage = [aalloc([128, 1024]) for i in range(2)]; B_kstage = [Buf(f"kstage{i}") for i in range(2)]
    kistage = [aalloc([128, 64]) for i in range(2)]; B_kistage = [Buf(f"kistage{i}") for i in range(2)]
    kTs = [aalloc([128, 8, 128], BF16) for i in range(2)]; B_kTs = [Buf(f"kTs{i}") for i in range(2)]
    kiTs = [aalloc([64, 128], BF16) for i in range(2)]; B_kiTs = [Buf(f"kiTs{i}") for i in range(2)]
    vstage = [aalloc([128, 1024], BF16) for i in range(3)]; B_vstage = [Buf(f"vstage{i}") for i in range(3)]

    def bc_rows(handle, n):
        return bass.AP(handle, 0, [[0, 128], [1, n]])

    S.dma("sp", lambda e: e.dma_start(out=ident_f[:, :], in_=ident_d[:, :]), w=[B_identf])
    S.dma("pool", lambda e: e.dma_start(out=ident_b[:, :], in_=ident_d[:, :]), w=[B_identb])
    S.dma("pool", lambda e: e.dma_start(out=maskT[:, :], in_=maskT_d[:, :]), w=[B_maskT])
    S.dma("pool", lambda e: e.dma_start(out=maskqk[:, :], in_=maskqk_d[:, :]), w=[B_maskqk])
    S.dma("sp", lambda e: e.dma_start(out=gcol[:, :], in_=bass.AP(ln_g_h, 0, [[1, 128], [128, 16]]),
                                      allow_slow_non_contiguous=True), w=[B_gcol])
    S.dma("sp", lambda e: e.dma_start(out=pgcol[:, :], in_=bass.AP(ple_g_h, 0, [[1, 128], [128, 16]]),
                                      allow_slow_non_contiguous=True), w=[B_pgcol])
    for i, (nm, n) in enumerate((("q_norm_a", 64), ("k_norm_a", 64), ("q_norm_b", 128), ("k_norm_b", 128),
                                 ("k_norm_idx", 64), ("subln_a", 128))):
        S.dma("sp", lambda e, i=i, nm=nm, n=n: e.dma_start(out=gains[:, i * 128:i * 128 + n],
                                                           in_=bc_rows(vec_h[nm], n)), w=[B_gains])
    for i, nm in enumerate(("lam_q1", "lam_k1", "lam_q2", "lam_k2")):
        S.dma("sp", lambda e, i=i, nm=nm: e.dma_start(out=lamv[:, i * 64:(i + 1) * 64], in_=bc_rows(vec_h[nm], 64)),
              w=[B_lamv])
    G_QNA, G_KNA, G_QNB, G_KNB, G_KNI, G_SUB = [gains[:, i * 128:(i + 1) * 128] for i in range(6)]
    S.op("dve", lambda e: e.memset(epsc[:, :], EPS), w=[B_epsc])
    S.op("dve", lambda e: e.tensor_tensor(out=lamj[:, :], in0=lamv[:, 0:64], in1=lamv[:, 64:128], op=ALU.mult),
         r=[B_lamv], w=[B_lamj])
    S.op("dve", lambda e: e.tensor_reduce(out=lamt[:, 0:1], in_=lamj[:, :], axis=AX.X, op=ALU.add),
         r=[B_lamj], w=[B_lamt])
    S.op("dve", lambda e: e.tensor_tensor(out=lamj[:, :], in0=lamv[:, 128:192], in1=lamv[:, 192:256], op=ALU.mult),
         r=[B_lamv, B_lamt], w=[B_lamj])
    S.op("dve", lambda e: e.tensor_reduce(out=lamt[:, 1:2], in_=lamj[:, :], axis=AX.X, op=ALU.add),
         r=[B_lamj], w=[B_lamt])
    S.op("act", lambda e: e.activation(out=lamt[:, 2:4], in_=lamt[:, 0:2], func=AF.Exp), r=[B_lamt], w=[B_lamt])
    S.op("dve", lambda e: e.tensor_tensor(out=lamt[:, 4:5], in0=lamt[:, 3:4], in1=lamt[:, 2:3], op=ALU.subtract),
         r=[B_lamt], w=[B_lamt])
    S.op("dve", lambda e: e.tensor_scalar(out=lamt[:, 4:5], in0=lamt[:, 4:5], scalar1=-LAM_INIT, scalar2=None,
                                          op0=ALU.add), r=[B_lamt], w=[B_lamt])
    NEGLAM = lamt[:, 4:5]

    rot = {"w": 0, "x": 0, "wk": 0, "ps": 0, "psb": 0, "kb": 0, "p": 0, "l": 0, "r": 0, "ev": 0, "vst": 0, "kst": 0, "kist": 0}

    def nxt(k, n):
        v = rot[k]
        rot[k] = (v + 1) % n
        return v

    def evac_engine():
        return "act" if nxt("ev", 2) == 0 else "dve"

    def build_xT(src, tiles, dst, B_dst, colg, B_colg, with_stats):
        for ti, (tok0, nt, c0) in enumerate(tiles):
            xi = nxt("x", NB2)
            S.dma("sp", lambda e, xi=xi, tok0=tok0, nt=nt: e.dma_start(out=xin[xi][:nt, :], in_=src[tok0:tok0 + nt, :]),
                  w=[B_xin[xi]])
            if with_stats:
                S.op("act", lambda e, xi=xi, nt=nt, ti=ti: e.activation(
                    out=xin_junk[:nt, :], in_=xin[xi][:nt, :], func=AF.Square, accum_out=ssqt[:nt, ti:ti + 1]),
                    r=[B_xin[xi]], w=[B_junk, B_ssqt])
                S.op("act", lambda e, nt=nt, ti=ti: e.activation(
                    out=rstd[:nt, ti:ti + 1], in_=ssqt[:nt, ti:ti + 1], func=AF.Sqrt, scale=1.0 / D, bias=epsc[:nt, :]),
                    r=[B_ssqt, B_epsc], w=[B_rstd])
                S.op("dve", lambda e, nt=nt, ti=ti: e.reciprocal(out=rstd[:nt, ti:ti + 1], in_=rstd[:nt, ti:ti + 1]),
                     r=[B_rstd], w=[B_rstd])
            for q4 in range(4):
                S_ps = B_psC
                for k in range(4):
                    ch = q4 * 4 + k
                    S.op("pe", lambda e, xi=xi, nt=nt, ch=ch, k=k: e.transpose(
                        out=psC[:, k * 128:k * 128 + nt], in_=xin[xi][:nt, ch * 128:(ch + 1) * 128],
                        identity=ident_f[:nt, :nt]), r=[B_xin[xi], B_identf], w=[S_ps])
                for k in range(4):
                    ch = q4 * 4 + k
                    S.op("dve", lambda e, nt=nt, ch=ch, k=k, c0=c0: e.tensor_scalar(
                        out=dst[:, ch, c0:c0 + nt], in0=psC[:, k * 128:k * 128 + nt], scalar1=colg[:, ch:ch + 1],
                        scalar2=None, op0=ALU.mult), r=[S_ps, B_colg], w=[B_dst])


    def load_w(wsrc, col0, ncols, nchk, tiles_, B_tiles):
        wi = nxt("w", 2)
        S.dma("pool", lambda e, wi=wi: e.dma_start(
            out=tiles_[wi][:, :nchk, :ncols],
            in_=wsrc[:, col0:col0 + ncols].rearrange("(ch p) c -> p ch c", p=128)), w=[B_tiles[wi]])
        return wi

    def gemm_tile(pst, B_pst, aT, B_aT, tcol0, nt, wt, B_wt, nchk, ncols):
        for ch in range(nchk):
            S.op("pe", lambda e, ch=ch: e.matmul(out=pst[:nt, :ncols], lhsT=aT[:, ch, tcol0:tcol0 + nt],
                                                 rhs=wt[:, ch, :ncols], start=(ch == 0), stop=(ch == nchk - 1)),
                 r=[B_aT, B_wt], w=[B_pst])

    def norm_rope(pst, B_pst, nt, ncols, dh, gain_ap, rcol, cs, B_cs, do_norm):
        G = ncols // dh
        r2 = dh // 8
        c0 = 0 if dh == 64 else 16
        wi_ = nxt("wk", NW)
        A, Bt, Ss, Rr = wkA[wi_], wkB[wi_], wkS[wi_], wkR[wi_]
        BA, BB, BS, BR = B_wkA[wi_], B_wkB[wi_], B_wkS[wi_], B_wkR[wi_]
        S.op("dve", lambda e: e.tensor_scalar(out=A[:nt, :ncols], in0=pst[:nt, :ncols], scalar1=rcol, scalar2=None,
                                              op0=ALU.mult), r=[B_pst, B_rstd], w=[BA])
        A3 = A[:nt, :ncols].rearrange("p (g d) -> p g d", d=dh)
        if do_norm:
            S.op("act", lambda e: e.activation(out=Bt[:nt, :ncols], in_=A[:nt, :ncols], func=AF.Square), r=[BA], w=[BB])
            S.op("dve", lambda e: e.tensor_reduce(out=Ss[:nt, 0:G], in_=Bt[:nt, :ncols].rearrange("p (g d) -> p g d", d=dh),
                                                  axis=AX.X, op=ALU.add), r=[BB], w=[BS])
            S.op("act", lambda e: e.activation(out=Ss[:nt, 8:8 + G], in_=Ss[:nt, 0:G], func=AF.Sqrt, scale=1.0 / dh,
                                               bias=epsc[:nt, :]), r=[BS, B_epsc], w=[BS])
            S.op("dve", lambda e: e.reciprocal(out=Ss[:nt, 8:8 + G], in_=Ss[:nt, 8:8 + G]), r=[BS], w=[BS])
            S.op("dve", lambda e: e.tensor_tensor(out=A3, in0=A3,
                                                  in1=Ss[:nt, 8:8 + G].unsqueeze(2).broadcast_to([nt, G, dh]),
                                                  op=ALU.mult), r=[BA, BS], w=[BA])
            S.op("dve", lambda e: e.tensor_tensor(out=A3, in0=A3,
                                                  in1=gain_ap[:nt, 0:dh].unsqueeze(1).broadcast_to([nt, G, dh]),
                                                  op=ALU.mult), r=[BA, B_gains], w=[BA])
        x1 = A3[:, :, 0:r2]
        x2 = A3[:, :, r2:2 * r2]
        cosb = cs[:nt, c0:c0 + r2].unsqueeze(1).broadcast_to([nt, G, r2])
        sinb = cs[:nt, c0 + r2:c0 + 2 * r2].unsqueeze(1).broadcast_to([nt, G, r2])
        n = G * r2
        T = [Rr[:nt, k * 64:k * 64 + n].rearrange("p (g d) -> p g d", d=r2) for k in range(4)]
        S.op("dve", lambda e: e.tensor_tensor(out=T[0], in0=x1, in1=cosb, op=ALU.mult), r=[BA, B_cs], w=[BR])
        S.op("dve", lambda e: e.tensor_tensor(out=T[1], in0=x2, in1=sinb, op=ALU.mult), r=[BA, B_cs], w=[BR])
        S.op("dve", lambda e: e.tensor_tensor(out=T[2], in0=x2, in1=cosb, op=ALU.mult), r=[BA, B_cs], w=[BR])
        S.op("dve", lambda e: e.tensor_tensor(out=T[3], in0=x1, in1=sinb, op=ALU.mult), r=[BA, B_cs], w=[BR])
        S.op("dve", lambda e: e.tensor_tensor(out=x1, in0=T[0], in1=T[1], op=ALU.subtract), r=[BR], w=[BA])
        S.op("dve", lambda e: e.tensor_tensor(out=x2, in0=T[2], in1=T[3], op=ALU.add), r=[BR], w=[BA])
        return wi_

    def to_T_scratch(src_bf, B_src, nt, ncols, dsts, stg_tiles=None):
        nk = (ncols + 127) // 128
        si = nxt("r", NW)
        if stg_tiles is None:
            stg_tiles = (stg, B_stg)
        st = stg_tiles[0][si]
        Bst = stg_tiles[1][si]
        for k in range(nk):
            cw = min(128, ncols - k * 128)
            S.op("pe", lambda e, k=k, cw=cw: e.transpose(out=psT[:cw, k * 128:k * 128 + nt],
                                                         in_=src_bf[:nt, k * 128:k * 128 + cw],
                                                         identity=ident_b[:nt, :nt]), r=[B_src, B_identb], w=[B_psT])
        rows = min(128, ncols)
        eng = evac_engine()
        if eng == "act":
            S.op("act", lambda e: e.copy(out=st[:rows, :nk * 128], in_=psT[:rows, :nk * 128]), r=[B_psT], w=[Bst])
        else:
            S.op("dve", lambda e: e.tensor_copy(out=st[:rows, :nk * 128], in_=psT[:rows, :nk * 128]),
                 r=[B_psT], w=[Bst])
        for (dap, t0, n, Bd) in dsts:
            if ncols >= 128:
                S.dma("sp", lambda e, dap=dap, t0=t0, n=n: e.dma_start(
                    out=dap.rearrange("(ch p) t -> p ch t", p=128),
                    in_=st[:, :nk * 128].rearrange("p (ch t) -> p ch t", t=128)[:, :, t0:t0 + n]),
                    r=[Bst], w=[Bd])
            else:
                S.dma("sp", lambda e, dap=dap, t0=t0, n=n: e.dma_start(out=dap, in_=st[:ncols, t0:t0 + n]),
                      r=[Bst], w=[Bd])

    def cast_bf(src_ap, B_src, nt, ncols):
        hi = nxt("l", NW)
        S.op("act", lambda e: e.copy(out=wkH[hi][:nt, :ncols], in_=src_ap), r=[B_src], w=[B_wkH[hi]])
        return hi

    def sample_prep_gen():
        for b in range(4):
            for (csrc, dst) in ((c_dv, s_va), (c_sv, s_vb)):
                for kt in range(32):
                    vi = nxt("vst", 3)
                    S.dma("pool", lambda e, csrc=csrc, b=b, kt=kt, vi=vi: e.dma_start(
                        out=vstage[vi][:, :], in_=csrc[b, kt * 128:(kt + 1) * 128, :]), w=[B_vstage[vi]])
                    S.dma("sp", lambda e, dst=dst, b=b, kt=kt, vi=vi: e.dma_start(
                        out=dst[b, kt * 128:(kt + 1) * 128, :], in_=vstage[vi][:, :]), r=[B_vstage[vi]], w=[B_skv])
                    yield
            for (csrc, dst) in ((c_dk, s_kaT), (c_sk, s_kbT)):
                for kt in range(32):
                    ki_ = nxt("kst", 2)
                    S.dma("sp", lambda e, csrc=csrc, b=b, kt=kt, ki_=ki_: e.dma_start(
                        out=kstage[ki_][:, :], in_=csrc[b, kt * 128:(kt + 1) * 128, :]), w=[B_kstage[ki_]])
                    for half in range(2):
                        pi = nxt("ps", 2)
                        for k in range(4):
                            ch = half * 4 + k
                            S.op("pe", lambda e, ch=ch, k=k, pi=pi, ki_=ki_: e.transpose(
                                out=psA[pi][:, k * 128:(k + 1) * 128], in_=kstage[ki_][:, ch * 128:(ch + 1) * 128],
                                identity=ident_f[:, :]), r=[B_kstage[ki_], B_identf], w=[B_psA[pi]])
                        S.op("act", lambda e, half=half, pi=pi, ki_=ki_: e.copy(
                            out=kTs[ki_][:, half * 4:(half + 1) * 4, :].rearrange("p a b -> p (a b)"), in_=psA[pi][:, :]),
                            r=[B_psA[pi]], w=[B_kTs[ki_]])
                    S.dma("sp", lambda e, dst=dst, b=b, kt=kt, ki_=ki_: e.dma_start(
                        out=dst[b, :, kt * 128:(kt + 1) * 128].rearrange("(ch p) t -> p ch t", p=128), in_=kTs[ki_][:, :, :]),
                        r=[B_kTs[ki_]], w=[B_skv])
                    yield
            for kt in range(32):
                ki_ = nxt("kist", 2)
                S.dma("sp", lambda e, b=b, kt=kt, ki_=ki_: e.dma_start(out=kistage[ki_][:, :],
                                                                        in_=c_ik[b, kt * 128:(kt + 1) * 128, :]),
                      w=[B_kistage[ki_]])
                S.op("pe", lambda e, ki_=ki_: e.transpose(out=psC[:64, 0:128], in_=kistage[ki_][:, :], identity=ident_f[:, :]),
                     r=[B_kistage[ki_], B_identf], w=[B_psC])
                S.op("act", lambda e, ki_=ki_: e.copy(out=kiTs[ki_][:, :], in_=psC[:64, 0:128]), r=[B_psC], w=[B_kiTs[ki_]])
                S.dma("sp", lambda e, b=b, kt=kt, ki_=ki_: e.dma_start(out=s_kiT[b, :, kt * 128:(kt + 1) * 128],
                                                                        in_=kiTs[ki_][:, :]),
                      r=[B_kiTs[ki_]], w=[B_skv])
                yield

    bg = {"gen": None}

    def bg_step():
        if bg["gen"] is not None:
            try:
                next(bg["gen"])
            except StopIteration:
                bg["gen"] = None

    SEC = {"qa": 0, "ka": 1024, "va": 2048, "ga": 3072, "qb": 4096, "kb": 5120, "vb": 6144, "gb": 7168,
           "qi": 8192, "kiwi": 9216, "mg": 9296}

    def proj_pass(src, rope_src, tiles, blocks, own, xT, B_xT):
        build_xT(src, [(t[0], t[1], t[2]) for t in tiles], xT, B_xT, gcol, B_gcol, True)
        for ti, (tok0, nt, xc0, g0) in enumerate(tiles):
            S.dma("sp", lambda e, ti=ti, tok0=tok0, nt=nt: e.dma_start(out=ropeall[:nt, ti * 48:(ti + 1) * 48],
                                                                      in_=rope_src[tok0:tok0 + nt, :]), w=[B_ropeall])
        for (sec, off, ncols) in blocks:
            col0 = SEC[sec] + off
            wi = load_w(w_in, col0, ncols, NCH, wbf, B_wbf)
            for ti, (tok0, nt, xc0, g0) in enumerate(tiles):
                bg_step()
                pi = nxt("ps", 2)
                pst, Bp = psA[pi], B_psA[pi]
                gemm_tile(pst, Bp, xT, B_xT, xc0, nt, wbf[wi], B_wbf[wi], NCH, ncols)
                rcol = rstd[:nt, ti:ti + 1]
                cs = ropeall[:, ti * 48:(ti + 1) * 48]
                is_samp = own and nt == 64
                if sec in ("qa", "ka", "qb", "kb"):
                    dh = 64 if sec in ("qa", "ka") else 128
                    gain = {"qa": G_QNA, "ka": G_KNA, "qb": G_QNB, "kb": G_KNB}[sec]
                    wi_ = norm_rope(pst, Bp, nt, ncols, dh, gain, rcol, cs, B_ropeall, True)
                    if sec in ("ka", "kb"):
                        if own:
                            od = o_dk if sec == "ka" else o_sk
                            S.dma("sp", lambda e, od=od, wi_=wi_, g0=g0, nt=nt, off=off: e.dma_start(
                                out=od[g0:g0 + nt, off:off + ncols], in_=wkA[wi_][:nt, :ncols]),
                                r=[B_wkA[wi_]], w=[B_out])
                        if (not own) or is_samp:
                            hi = cast_bf(wkA[wi_][:nt, :ncols], B_wkA[wi_], nt, ncols)
                            if not own:
                                dT = kaT if sec == "ka" else kbT
                                dsts = [(dT[off:off + ncols, g0:g0 + nt], 0, nt, B_kv)]
                            else:
                                dT = s_kaT if sec == "ka" else s_kbT
                                dsts = [(dT[b, off:off + ncols, PAST:PAST + 16], 16 * b, 16, B_skv) for b in range(4)]
                            to_T_scratch(wkH[hi], B_wkH[hi], nt, ncols, dsts)
                    else:
                        hi = cast_bf(wkA[wi_][:nt, :ncols], B_wkA[wi_], nt, ncols)
                        dT = qaT if sec == "qa" else qbT
                        to_T_scratch(wkH[hi], B_wkH[hi], nt, ncols, [(dT[off:off + ncols, g0:g0 + nt], 0, nt, B_q)])
                elif sec in ("va", "vb"):
                    if own:
                        wi_ = nxt("wk", NW)
                        S.op("dve", lambda e, wi_=wi_, pst=pst, nt=nt, rcol=rcol: e.tensor_scalar(
                            out=wkA[wi_][:nt, :ncols], in0=pst[:nt, :ncols], scalar1=rcol, scalar2=None, op0=ALU.mult),
                            r=[Bp, B_rstd], w=[B_wkA[wi_]])
                        od = o_dv if sec == "va" else o_sv
                        S.dma("sp", lambda e, od=od, wi_=wi_, g0=g0, nt=nt, off=off: e.dma_start(
                            out=od[g0:g0 + nt, off:off + ncols], in_=wkA[wi_][:nt, :ncols]), r=[B_wkA[wi_]], w=[B_out])
                    if (not own) or is_samp:
                        hi = nxt("l", NW)
                        S.op("dve", lambda e, hi=hi, pst=pst, nt=nt, rcol=rcol: e.tensor_scalar(
                            out=wkH[hi][:nt, :ncols], in0=pst[:nt, :ncols], scalar1=rcol, scalar2=None, op0=ALU.mult),
                            r=[Bp, B_rstd], w=[B_wkH[hi]])
                        if not own:
                            dV = va if sec == "va" else vb
                            S.dma("sp", lambda e, dV=dV, hi=hi, g0=g0, nt=nt, off=off: e.dma_start(
                                out=dV[g0:g0 + nt, off:off + ncols], in_=wkH[hi][:nt, :ncols]), r=[B_wkH[hi]], w=[B_kv])
                        else:
                            dV = s_va if sec == "va" else s_vb
                            for b in range(4):
                                S.dma("sp", lambda e, dV=dV, hi=hi, b=b, off=off: e.dma_start(
                                    out=dV[b, PAST:PAST + 16, off:off + ncols], in_=wkH[hi][16 * b:16 * b + 16, :ncols]),
                                    r=[B_wkH[hi]], w=[B_skv])
                elif sec in ("ga", "gb", "mg"):
                    hi = nxt("l", NW)
                    fn = AF.Silu if sec != "mg" else AF.Sigmoid
                    wi_ = nxt("wk", NW)
                    S.op("dve", lambda e, wi_=wi_, pst=pst, nt=nt, rcol=rcol: e.tensor_scalar(
                        out=wkA[wi_][:nt, :ncols], in0=pst[:nt, :ncols], scalar1=rcol, scalar2=None, op0=ALU.mult),
                        r=[Bp, B_rstd], w=[B_wkA[wi_]])
                    S.op("act", lambda e, hi=hi, wi_=wi_, nt=nt, fn=fn: e.activation(
                        out=wkH[hi][:nt, :ncols], in_=wkA[wi_][:nt, :ncols], func=fn),
                        r=[B_wkA[wi_]], w=[B_wkH[hi]])
                    dG = {"ga": sga, "gb": sgb, "mg": smg}[sec]
                    S.dma("sp", lambda e, dG=dG, hi=hi, g0=g0, nt=nt, off=off: e.dma_start(
                        out=dG[g0:g0 + nt, off:off + ncols], in_=wkH[hi][:nt, :ncols]), r=[B_wkH[hi]], w=[B_g])
                elif sec == "kiwi":
                    wi_ = norm_rope(pst, Bp, nt, 64, 64, G_KNI, rcol, cs, B_ropeall, True)
                    if own:
                        S.dma("sp", lambda e, wi_=wi_, g0=g0, nt=nt: e.dma_start(
                            out=o_ik[g0:g0 + nt, :], in_=wkA[wi_][:nt, 0:64]), r=[B_wkA[wi_]], w=[B_out])
                        S.op("dve", lambda e, ti=ti, pst=pst, nt=nt, rcol=rcol: e.tensor_scalar(
                            out=absw[:nt, ti * 16:(ti + 1) * 16], in0=pst[:nt, 64:80], scalar1=rcol, scalar2=1.0 / 32.0,
                            op0=ALU.mult, op1=ALU.mult), r=[Bp, B_rstd], w=[B_absw])
                        S.op("dve", lambda e, ti=ti, nt=nt: e.scalar_tensor_tensor(
                            out=absw[:nt, ti * 16:(ti + 1) * 16], in0=absw[:nt, ti * 16:(ti + 1) * 16],
                            scalar=-1.0, in1=absw[:nt, ti * 16:(ti + 1) * 16], op0=ALU.mult, op1=ALU.max),
                            r=[B_absw], w=[B_absw])
                        S.op("act", lambda e, ti=ti, pst=pst, nt=nt: e.activation(
                            out=sgnw[:nt, ti * 16:(ti + 1) * 16], in_=pst[:nt, 64:80], func=AF.Sign),
                            r=[Bp], w=[B_sgnw])
                        S.dma("sp", lambda e, ti=ti, g0=g0, nt=nt: e.dma_start(
                            out=sgn_scr[g0:g0 + nt, :], in_=sgnw[:nt, ti * 16:(ti + 1) * 16]), r=[B_sgnw], w=[B_q])
                    if (not own) or is_samp:
                        hi = cast_bf(wkA[wi_][:nt, 0:64], B_wkA[wi_], nt, 64)
                        if not own:
                            dsts = [(kiT[:, g0:g0 + nt], 0, nt, B_kv)]
                        else:
                            dsts = [(s_kiT[b, :, PAST:PAST + 16], 16 * b, 16, B_skv) for b in range(4)]
                        to_T_scratch(wkH[hi], B_wkH[hi], nt, 64, dsts)
                elif sec == "qi":
                    wi_ = norm_rope(pst, Bp, nt, ncols, 64, None, rcol, cs, B_ropeall, False)
                    A3 = wkA[wi_][:nt, :ncols].rearrange("p (g d) -> p g d", d=64)
                    h0 = off // 64
                    S.op("dve", lambda e, A3=A3, ti=ti, nt=nt, h0=h0: e.tensor_tensor(
                        out=A3, in0=A3,
                        in1=absw[:nt, ti * 16 + h0:ti * 16 + h0 + 8].unsqueeze(2).broadcast_to([nt, 8, 64]),
                        op=ALU.mult), r=[B_wkA[wi_], B_absw], w=[B_wkA[wi_]])
                    hi = cast_bf(wkA[wi_][:nt, :ncols], B_wkA[wi_], nt, ncols)
                    to_T_scratch(wkH[hi], B_wkH[hi], nt, ncols, [(qiT[off:off + ncols, g0:g0 + nt], 0, nt, B_q)])

    own_tiles = [(128 * j, 128, 128 * j, 128 * j) for j in range(NBLK)] + [(2048, 64, 2048, 2048)]
    own_blocks = [("kiwi", 0, 80)]
    for sec in ("qa", "ka", "va", "ga", "qb", "kb", "vb", "gb", "qi"):
        own_blocks += [(sec, 0, 512), (sec, 512, 512)]
    own_blocks += [("mg", 512 * k, 512) for k in range(8)]
    if KSEC:
        own_blocks = [b_ for b_ in own_blocks if b_[0] in KSEC.split(',')]
    proj_pass(x_own, rope_own, own_tiles, own_blocks, True, xT, B_xT)

    for ti, (tok0, nt, xc0, g0) in enumerate(own_tiles if KSTOP >= 2 else []):
        xi = nxt("x", NB2)
        S.dma("sp", lambda e, xi=xi, tok0=tok0, nt=nt: e.dma_start(out=xin[xi][:nt, 0:256], in_=p_own[tok0:tok0 + nt, :]),
              w=[B_xin[xi]])
        for k in range(2):
            S.op("pe", lambda e, xi=xi, nt=nt, k=k: e.transpose(out=psC[:, k * 128:k * 128 + nt],
                                                               in_=xin[xi][:nt, k * 128:(k + 1) * 128],
                                                               identity=ident_f[:nt, :nt]),
                 r=[B_xin[xi], B_identf], w=[B_psC])
        for k in range(2):
            S.op("dve", lambda e, nt=nt, k=k, xc0=xc0: e.tensor_copy(out=pTt[:, k, xc0:xc0 + nt],
                                                                     in_=psC[:, k * 128:k * 128 + nt]),
                 r=[B_psC], w=[B_pTt])

    kv_blocks = [("kiwi", 0, 80)]
    for sec in ("ka", "va", "kb", "vb"):
        kv_blocks += [(sec, 0, 512), (sec, 512, 512)]
    if KSTOP >= 2:
        bg["gen"] = sample_prep_gen()
    for g in range(0 if KSTOP < 2 else (1 if DBG else 8)):
        tiles = [(2048 * g + 128 * t, 128, 128 * t, 2048 * g + 128 * t) for t in range(16)]
        proj_pass(x_all, rope_all, tiles, kv_blocks, False, xT, B_xT)
    while bg["gen"] is not None:
        bg_step()


    areset()
    scores = aalloc([128, SEQ]); B_scores = Buf("scores")
    mbias = aalloc([128, SEQ], BF16); B_mbias = Buf("mbias")
    ktc = [aalloc([128, 2048], BF16) for i in range(NKB)]; B_ktc = [Buf(f"ktc{i}") for i in range(NKB)]
    vtc = [aalloc([128, 16, 130], BF16) for i in range(NKB)]; B_vtc = [Buf(f"vtc{i}") for i in range(NKB)]
    kic = [aalloc([128, 512], BF16) for i in range(NKB)]; B_kic = [Buf(f"kic{i}") for i in range(NKB)]
    qa_g = aalloc([128, 2 * 8 * 128], BF16).rearrange("p (c h t) -> p c h t", c=2, h=8); B_qa_g = Buf("qa_g")
    qb_g = aalloc([128, 8, 128], BF16); B_qb_g = Buf("qb_g")
    qi_g = aalloc([128, 16 * 128], BF16).rearrange("p (a two t) -> p a two t", two=2, t=128); B_qi_g = Buf("qi_g")
    sga_g = aalloc([128, 1024], BF16); B_sga_g = Buf("sga_g")
    sgb_g = aalloc([128, 1024], BF16); B_sgb_g = Buf("sgb_g")
    sgn_g = aalloc([128, 16]); B_sgn_g = Buf("sgn_g")
    dg = aalloc([128, 16, 128], BF16); B_dg = Buf("dg")
    u_g = aalloc([128, D], BF16); B_u_g = Buf("u_g")
    ustg = aalloc([128, NCH, 128], BF16); B_ustg = Buf("ustg")
    pT = [aalloc([128, 512], BF16) for i in range(NP)]; B_pT = [Buf(f"pT{i}") for i in range(NP)]
    rl = [aalloc([128, 512], BF16) for i in range(NP)]; B_rl = [Buf(f"rl{i}") for i in range(NP)]
    bis = aalloc([128, 8]); B_bis = Buf("bis")
    ep = aalloc([128, 16]); B_ep = Buf("ep")
    eo = aalloc([128, 256]); B_eo = Buf("eo")
    dsave = aalloc([128, 8 * 2 * 132]).rearrange("p (h c d) -> p h c d", h=8, c=2); B_dsave = Buf("dsave")
    for i in range(NKB):
        S.op("dve", lambda e, i=i: e.memset(vtc[i][:, :, 128:129], 1.0), w=[B_vtc[i]])
    S.op("dve", lambda e: e.memset(qa_g.rearrange("p c h t -> p (c h t)"), 0.0), w=[B_qa_g])
    S.op("dve", lambda e: e.memset(qi_g.rearrange("p a two t -> p (a two t)"), 0.0), w=[B_qi_g])

    def attend(qg):
        nq, tok0, KS, B_KS, nkeys, masked = qg["nq"], qg["tok0"], qg["ks"], qg["bks"], qg["nkeys"], qg["masked"]
        k_aT, v_a, k_bT, v_b, k_iT = KS
        ti_w = qg["ti"]
        wrow0 = qg["wrow0"]
        ntiles_full = nkeys // 128
        rem = nkeys - ntiles_full * 128
        for c_ in range(2):
            S.dma("sp", lambda e, c_=c_: e.dma_start(
                out=qa_g[c_ * 64:(c_ + 1) * 64, c_, :, :nq],
                in_=qaT[:, tok0:tok0 + nq].rearrange("(h p) t -> p h t", p=128)[c_ * 64:(c_ + 1) * 64]),
                r=[B_q], w=[B_qa_g])
        S.dma("sp", lambda e: e.dma_start(out=qb_g[:, :, :nq], in_=qbT[:, tok0:tok0 + nq].rearrange("(h p) t -> p h t", p=128)),
              r=[B_q], w=[B_qb_g])
        for c_ in range(2):
            S.dma("sp", lambda e, c_=c_: e.dma_start(
                out=qi_g[c_ * 64:(c_ + 1) * 64, :, c_, :nq],
                in_=qiT[:, tok0:tok0 + nq].rearrange("(h p) t -> p h t", p=128)[c_ * 64:(c_ + 1) * 64]),
                r=[B_q], w=[B_qi_g])
        S.dma("sp", lambda e: e.dma_start(out=sga_g[:nq, :], in_=sga[tok0:tok0 + nq, :]), r=[B_g], w=[B_sga_g])
        S.dma("sp", lambda e: e.dma_start(out=sgb_g[:nq, :], in_=sgb[tok0:tok0 + nq, :]), r=[B_g], w=[B_sgb_g])
        S.dma("sp", lambda e: e.dma_start(out=sgn_g[:nq, :], in_=sgn_scr[tok0:tok0 + nq, :]), r=[B_q], w=[B_sgn_g])
        for h in range(16):
            S.op("dve", lambda e, h=h: e.tensor_scalar(
                out=dg[:nq, h, :nq], in0=ident_b[:nq, :nq],
                scalar1=sgn_g[:nq, h:h + 1], scalar2=None, op0=ALU.mult),
                r=[B_identb, B_sgn_g], w=[B_dg])
        nchunks = (nkeys + 511) // 512
        for kc in range(nchunks):
            k0 = kc * 512
            cw = min(512, nkeys - k0)
            ci = nxt("kb", NKB)
            for half in range(2):
                S.dma("sp", lambda e, ci=ci, half=half, k0=k0, cw=cw: e.dma_start(
                    out=kic[ci][half * 64:(half + 1) * 64, :cw], in_=k_iT[:, k0:k0 + cw]), r=[B_KS], w=[B_kic[ci]])
            is_tail = masked and kc >= nchunks - 2
            for h in range(16):
                li = nxt("psb", 2)
                hp = (h % 2) * 64
                S.op("pe", lambda e, h=h, li=li, hp=hp, ci=ci, cw=cw: e.matmul(
                    out=psL[li][:nq, :cw], lhsT=qi_g[:, h // 2, h % 2, :nq], rhs=kic[ci][:, :cw],
                    start=True, stop=True), r=[B_qi_g, B_kic[ci]], w=[B_psL[li]])
                ri = nxt("p", NP)
                eng = evac_engine()
                if eng == "act":
                    S.op("act", lambda e, li=li, ri=ri, cw=cw: e.activation(out=rl[ri][:nq, :cw], in_=psL[li][:nq, :cw],
                                                                          func=AF.Relu), r=[B_psL[li]], w=[B_rl[ri]])
                else:
                    S.op("dve", lambda e, li=li, ri=ri, cw=cw: e.tensor_scalar(
                        out=rl[ri][:nq, :cw], in0=psL[li][:nq, :cw], scalar1=0.0, scalar2=None, op0=ALU.max),
                        r=[B_psL[li]], w=[B_rl[ri]])
                S.op("pe", lambda e, h=h, ri=ri, cw=cw, is_tail=is_tail: e.matmul(
                    out=psC[:nq, :cw], lhsT=dg[:nq, h, :nq], rhs=rl[ri][:nq, :cw], start=(h == 0),
                    stop=(h == 15 and not is_tail)), r=[B_dg, B_rl[ri]], w=[B_psC])
            if is_tail:
                mo = (kc - (nchunks - 2)) * 512
                S.op("pe", lambda e, mo=mo, cw=cw: e.matmul(out=psC[:nq, :cw], lhsT=ident_b[:nq, :nq],
                                                            rhs=maskqk[:nq, mo:mo + cw], start=False, stop=True),
                     r=[B_identb, B_maskqk], w=[B_psC])
            S.op("act", lambda e, k0=k0, cw=cw: e.copy(out=scores[:nq, k0:k0 + cw], in_=psC[:nq, :cw]),
                 r=[B_psC], w=[B_scores])
        def bisect():
          S.op("dve", lambda e: e.memset(bis[:nq, 0:1], BIS_LO), w=[B_bis])
          for it in range(NBIS):
              wd = BIS_W / (2.0 ** (it + 1))
              S.op("dve", lambda e, wd=wd: e.tensor_scalar(out=bis[:nq, 1:2], in0=bis[:nq, 0:1], scalar1=wd, scalar2=None,
                                                           op0=ALU.add), r=[B_bis], w=[B_bis])
              S.op("dve", lambda e: e.tensor_scalar(out=mbias[:nq, :nkeys], in0=scores[:nq, :nkeys], scalar1=bis[:nq, 1:2],
                                                    scalar2=0.0, op0=ALU.is_ge, op1=ALU.add, accum_out=bis[:nq, 2:3]),
                   r=[B_scores, B_bis], w=[B_mbias, B_bis])
              S.op("dve", lambda e: e.tensor_scalar(out=bis[:nq, 3:4], in0=bis[:nq, 2:3], scalar1=float(TOPK) - 0.5,
                                                    scalar2=None, op0=ALU.is_ge), r=[B_bis], w=[B_bis])
              S.op("dve", lambda e, wd=wd: e.scalar_tensor_tensor(out=bis[:nq, 0:1], in0=bis[:nq, 3:4], scalar=wd,
                                                                  in1=bis[:nq, 0:1], op0=ALU.mult, op1=ALU.add),
                   r=[B_bis], w=[B_bis])
          S.op("dve", lambda e: e.tensor_scalar(out=mbias[:nq, :nkeys], in0=scores[:nq, :nkeys], scalar1=bis[:nq, 0:1],
                                                scalar2=NEG, op0=ALU.is_lt, op1=ALU.mult), r=[B_scores, B_bis], w=[B_mbias])

        def branch(is_diff):
            kT_d, v_d = (k_aT, v_a) if is_diff else (k_bT, v_b)
            q_g, Bq_g = (qa_g, B_qa_g) if is_diff else (qb_g, B_qb_g)
            ncomp = 2 if is_diff else 1
            per = 2 if is_diff else 4
            scale = (64.0 ** -0.5) if is_diff else (128.0 ** -0.5)
            tl = [(t, 128) for t in range(ntiles_full)] + ([(ntiles_full, rem)] if rem else [])
            ntl = len(tl)
            for h in range(8):
                oi = nxt("ps", 2)
                pso, Bpso = psB[oi], B_psB[oi]
                if is_diff:
                    pso_c = [psB[0], psB[1]]
                    Bpso_c = [B_psB[0], B_psB[1]]
                else:
                    pso_c = [pso]
                    Bpso_c = [Bpso]
                cur = {"chunk": -1, "ci": 0}

                def need_chunk(t):
                    c = t // 16
                    if c == cur["chunk"]:
                        return cur["ci"]
                    ci = nxt("kb", NKB)
                    k0 = c * 2048
                    cwk = min(2048, nkeys - k0)
                    nfull = min(16, ntiles_full - c * 16)
                    S.dma("sp", lambda e, ci=ci, k0=k0, cwk=cwk, h=h: e.dma_start(
                        out=ktc[ci][:, :cwk], in_=kT_d[h * 128:(h + 1) * 128, k0:k0 + cwk]), r=[B_KS], w=[B_ktc[ci]])
                    if nfull > 0:
                        S.dma("sp", lambda e, ci=ci, k0=k0, nfull=nfull, h=h: e.dma_start(
                            out=vtc[ci][:, :nfull, 0:128],
                            in_=v_d[k0:k0 + nfull * 128, h * 128:(h + 1) * 128].rearrange("(t p) d -> p t d", p=128)),
                            r=[B_KS], w=[B_vtc[ci]])
                    if rem and c == ntiles_full // 16:
                        tt = ntiles_full - c * 16
                        S.dma("sp", lambda e, ci=ci, tt=tt, h=h: e.dma_start(
                            out=vtc[ci][:rem, tt, 0:128],
                            in_=v_d[ntiles_full * 128:ntiles_full * 128 + rem, h * 128:(h + 1) * 128]),
                            r=[B_KS], w=[B_vtc[ci]])
                    cur["chunk"] = c
                    cur["ci"] = ci
                    return ci

                i = 0
                first = True
                while i < ntl:
                    batch = [tl[i]]
                    while (len(batch) < per and i + len(batch) < ntl and tl[i + len(batch)][1] == batch[0][1]
                           and tl[i + len(batch)][0] // 16 == batch[0][0] // 16):
                        batch.append(tl[i + len(batch)])
                    i += len(batch)
                    ksz = batch[0][1]
                    ci = need_chunk(batch[0][0])
                    si_ = nxt("psb", 2)
                    pss, Bpss = psA[si_], B_psA[si_]
                    nsl = len(batch) * ncomp
                    for bi, (t, _) in enumerate(batch):
                        tc_ = t % 16
                        mt = (t - (ntiles_full - 8)) if (masked and t >= ntiles_full - 8) else -1
                        for c in range(ncomp):
                            sl = bi * ncomp + c
                            if is_diff:
                                lhs = ktc[ci][:, tc_ * 128:tc_ * 128 + ksz]
                                rhs = q_g[:, c, h, :nq]
                            else:
                                lhs = ktc[ci][:, tc_ * 128:tc_ * 128 + ksz]
                                rhs = q_g[:, h, :nq]
                            has_mask = (mt >= 0) if is_diff else True
                            S.op("pe", lambda e, lhs=lhs, rhs=rhs, sl=sl, has_mask=has_mask, ksz=ksz, pss=pss: e.matmul(
                                out=pss[:ksz, sl * nq:(sl + 1) * nq], lhsT=lhs, rhs=rhs, start=True, stop=not has_mask),
                                r=[B_ktc[ci], Bq_g], w=[Bpss])
                            if is_diff and mt >= 0:
                                S.op("pe", lambda e, sl=sl, mt=mt, ksz=ksz, pss=pss: e.matmul(
                                    out=pss[:ksz, sl * nq:(sl + 1) * nq], lhsT=ident_b[:, :ksz],
                                    rhs=maskT[:, mt * 128:mt * 128 + nq], start=False, stop=True),
                                    r=[B_identb, B_maskT], w=[Bpss])
                            if not is_diff:
                                S.op("pe", lambda e, sl=sl, t=t, ksz=ksz, pss=pss: e.matmul(
                                    out=pss[:ksz, sl * nq:(sl + 1) * nq], lhsT=mbias[:nq, t * 128:t * 128 + ksz],
                                    rhs=ident_b[:nq, :nq], start=False, stop=True), r=[B_mbias, B_identb], w=[Bpss])
                    pi_ = nxt("p", NP)
                    S.op("act", lambda e, pi_=pi_, nsl=nsl, ksz=ksz, pss=pss: e.activation(
                        out=pT[pi_][:ksz, :nsl * nq], in_=pss[:ksz, :nsl * nq], func=AF.Exp, scale=scale),
                        r=[Bpss], w=[B_pT[pi_]])
                    for bi, (t, _) in enumerate(batch):
                        tc_ = t % 16
                        lastt = (t == tl[-1][0])
                        for c in range(ncomp):
                            sl = bi * ncomp + c
                            S.op("pe", lambda e, pi_=pi_, sl=sl, c=c, tc_=tc_, first=first, lastt=lastt, ksz=ksz, po=pso_c[c], ci=ci: e.matmul(
                                out=po[:nq, 0:129], lhsT=pT[pi_][:ksz, sl * nq:(sl + 1) * nq],
                                rhs=vtc[ci][:ksz, tc_, 0:129], start=first, stop=lastt), r=[B_pT[pi_], B_vtc[ci]], w=[Bpso_c[c]])
                        first = False
                if is_diff:
                    for c in range(2):
                        S.op("act", lambda e, c=c, h=h: e.copy(out=dsave[:nq, h, c, 0:129], in_=psB[c][:nq, 0:129]),
                             r=[B_psB[c]], w=[B_dsave])
                else:
                    S.op("dve", lambda e, pso=pso: e.reciprocal(out=ep[:nq, 0:1], in_=pso[:nq, 128:129]), r=[Bpso], w=[B_ep])
                    S.op("dve", lambda e, h=h, pso=pso: e.scalar_tensor_tensor(
                        out=u_g[:nq, 1024 + h * 128:1024 + (h + 1) * 128], in0=pso[:nq, 0:128], scalar=ep[:nq, 0:1],
                        in1=sgb_g[:nq, h * 128:(h + 1) * 128], op0=ALU.mult, op1=ALU.mult),
                        r=[Bpso, B_ep, B_sgb_g], w=[B_u_g])

        def diff_epi():
            for h in range(8):
                S.op("dve", lambda e, h=h: e.reciprocal(out=ep[:nq, 0:1], in_=dsave[:nq, h, 0, 128:129]), r=[B_dsave], w=[B_ep])
                S.op("dve", lambda e, h=h: e.reciprocal(out=ep[:nq, 1:2], in_=dsave[:nq, h, 1, 128:129]), r=[B_dsave], w=[B_ep])
                S.op("dve", lambda e, h=h: e.tensor_tensor(out=ep[:nq, 1:2], in0=ep[:nq, 1:2], in1=NEGLAM[:nq, :],
                                                      op=ALU.mult), r=[B_ep, B_lamt], w=[B_ep])
                S.op("dve", lambda e, h=h: e.tensor_scalar(out=eo[:nq, 0:128], in0=dsave[:nq, h, 0, 0:128], scalar1=ep[:nq, 0:1],
                                                      scalar2=None, op0=ALU.mult), r=[B_dsave, B_ep], w=[B_eo])
                S.op("dve", lambda e, h=h: e.scalar_tensor_tensor(out=eo[:nq, 0:128], in0=dsave[:nq, h, 1, 0:128],
                                                             scalar=ep[:nq, 1:2], in1=eo[:nq, 0:128],
                                                             op0=ALU.mult, op1=ALU.add), r=[B_dsave, B_ep, B_eo], w=[B_eo])
                S.op("act", lambda e, h=h: e.activation(out=eo[:nq, 128:256], in_=eo[:nq, 0:128], func=AF.Square,
                                                   accum_out=ep[:nq, 2:3]), r=[B_eo], w=[B_eo, B_ep])
                S.op("act", lambda e, h=h: e.activation(out=ep[:nq, 3:4], in_=ep[:nq, 2:3], func=AF.Sqrt, scale=1.0 / 128,
                                                   bias=epsc[:nq, :]), r=[B_ep, B_epsc], w=[B_ep])
                S.op("dve", lambda e, h=h: e.reciprocal(out=ep[:nq, 3:4], in_=ep[:nq, 3:4]), r=[B_ep], w=[B_ep])
                S.op("dve", lambda e, h=h: e.tensor_scalar(out=eo[:nq, 0:128], in0=eo[:nq, 0:128], scalar1=ep[:nq, 3:4],
                                                      scalar2=1.0 - LAM_INIT, op0=ALU.mult, op1=ALU.mult),
                     r=[B_eo, B_ep], w=[B_eo])
                S.op("dve", lambda e, h=h: e.tensor_tensor(out=eo[:nq, 0:128], in0=eo[:nq, 0:128], in1=G_SUB[:nq, :],
                                                      op=ALU.mult), r=[B_eo, B_gains], w=[B_eo])
                S.op("dve", lambda e, h=h: e.tensor_tensor(out=u_g[:nq, h * 128:(h + 1) * 128], in0=eo[:nq, 0:128],
                                                           in1=sga_g[:nq, h * 128:(h + 1) * 128], op=ALU.mult),
                     r=[B_eo, B_sga_g], w=[B_u_g])

        branch(True)
        bisect()
        diff_epi()
        if DBG and tok0 == 0:
            S.dma("sp", lambda e: e.dma_start(out=dbg_eo[:, :], in_=eo[:, :]), r=[B_eo], w=[B_out])
            S.dma("sp", lambda e: e.dma_start(out=dbg_ep[:, :], in_=ep[:, :]), r=[B_ep], w=[B_out])
            for i_ in range(2):
                S.op("dve", lambda e, i_=i_: e.tensor_copy(out=scores[:, i_ * 512:(i_ + 1) * 512], in_=psB[i_][:, :]),
                     r=[B_psB[i_]], w=[B_scores])
            S.dma("sp", lambda e: e.dma_start(out=dbg_ps[:, :], in_=scores[:, 0:1024]), r=[B_scores], w=[B_out])
            S.dma("sp", lambda e: e.dma_start(out=dbg_lam[:, :], in_=lamt[:, :]), r=[B_lamt], w=[B_out])
        branch(False)
        for q2 in range(2):
            for k in range(8):
                ch = q2 * 8 + k
                S.op("pe", lambda e, ch=ch, k=k: e.transpose(out=psT[:, k * 128:k * 128 + nq],
                                                             in_=u_g[:nq, ch * 128:(ch + 1) * 128],
                                                             identity=ident_b[:nq, :nq]), r=[B_u_g, B_identb], w=[B_psT])
            S.op("act", lambda e, q2=q2: e.copy(out=ustg[:, q2 * 8:(q2 + 1) * 8, :].rearrange("p a b -> p (a b)"),
                                                in_=psT[:, :]), r=[B_psT], w=[B_ustg])
        S.dma("sp", lambda e: e.dma_start(out=uT[:, tok0:tok0 + nq].rearrange("(ch p) t -> p ch t", p=128),
                                          in_=ustg[:, :, :nq]), r=[B_ustg], w=[B_u])

    qgs = []
    for j in range(NBLK):
        qgs.append(dict(nq=128, tok0=128 * j, ks=(kaT, va, kbT, vb, kiT), bks=B_kv, nkeys=(8 * j + 8) * 128,
                        masked=True, ti=j, wrow0=0))
    for b in range(4):
        qgs.append(dict(nq=16, tok0=2048 + 16 * b, ks=(s_kaT[b], s_va[b], s_kbT[b], s_vb[b], s_kiT[b]), bks=B_skv,
                        nkeys=NKS, masked=False, ti=16, wrow0=16 * b))
    KSAMP = int(os.environ.get('KSAMP', '1'))
    for qg in ([] if KSTOP < 3 else ([qgs[0]] + ([qgs[16]] if KSAMP else []) if DBG else (qgs if KSAMP else qgs[:16]))):
        attend(qg)

    areset()
    uTs = aalloc([128, NCH, NOWN], BF16); B_uTs = Buf("uTs")
    hTs, B_hTs = uTs, B_uTs
    wbfD = [aalloc([128, NCH, 512], BF16) for i in range(2)]; B_wbfD = [Buf(f"wbfD{i}") for i in range(2)]
    wbf2 = [aalloc([128, 8, 512], BF16) for i in range(2)]; B_wbf2 = [Buf(f"wbf2{i}") for i in range(2)]
    mTt = [aalloc([128, NCH, 128], BF16) for i in range(2)]; B_mTt = [Buf(f"mTt{i}") for i in range(2)]
    mab = [aalloc([128, 1024], BF16) for i in range(2)]; B_mab = [Buf(f"mab{i}") for i in range(2)]
    hld = [aalloc([128, 512]) for i in range(2)]; B_hld = [Buf(f"hld{i}") for i in range(2)]
    wkAD = [aalloc([128, 512]) for i in range(NW)]; B_wkAD = [Buf(f"wkAD{i}") for i in range(NW)]
    wkBD = [aalloc([128, 512]) for i in range(NW)]; B_wkBD = [Buf(f"wkBD{i}") for i in range(NW)]
    wkHD = [aalloc([128, 512], BF16) for i in range(NW)]; B_wkHD = [Buf(f"wkHD{i}") for i in range(NW)]
    stgD = [aalloc([128, 4 * 128], BF16) for i in range(NW)]; B_stgD = [Buf(f"stgD{i}") for i in range(NW)]
    B_m = Buf("m_scr")
    if KSTOP >= 4:
        S.dma("sp", lambda e: e.dma_start(out=uTs[:, :, :], in_=uT[:, :].rearrange("(ch p) t -> p ch t", p=128)),
              r=[B_u], w=[B_uTs])
    for n in range(4 if KSTOP >= 4 else 0):
        wa = load_w(w_ba, n * 512, 512, 8, wbf2, B_wbf2)
        wb_ = load_w(w_bb, n * 512, 512, 8, wbf2, B_wbf2)
        for ti, (tok0, nt, xc0, g0) in enumerate(own_tiles):
            mi = nxt("x", 2)
            S.dma("sp", lambda e, mi=mi, tok0=tok0, nt=nt, n=n: e.dma_start(
                out=mab[mi][:nt, :].rearrange("p (a c) -> p a c", a=2),
                in_=smg[tok0:tok0 + nt, :].rearrange("t (a c) -> t a c", a=2)[:, :, n * 512:(n + 1) * 512]),
                r=[B_g], w=[B_mab[mi]])
            pa, Bpa = psA[0], B_psA[0]
            pb, Bpb = psA[1], B_psA[1]
            for ch in range(8):
                S.op("pe", lambda e, ch=ch, xc0=xc0, nt=nt, wa=wa: e.matmul(
                    out=pa[:nt, :], lhsT=uTs[:, ch, xc0:xc0 + nt], rhs=wbf2[wa][:, ch, :], start=(ch == 0),
                    stop=(ch == 7)), r=[B_uTs, B_wbf2[wa]], w=[Bpa])
            for ch in range(8):
                S.op("pe", lambda e, ch=ch, xc0=xc0, nt=nt, wb_=wb_: e.matmul(
                    out=pb[:nt, :], lhsT=uTs[:, 8 + ch, xc0:xc0 + nt], rhs=wbf2[wb_][:, ch, :], start=(ch == 0),
                    stop=(ch == 7)), r=[B_uTs, B_wbf2[wb_]], w=[Bpb])
            wi_ = nxt("wk", NW)
            S.op("dve", lambda e, wi_=wi_, mi=mi, nt=nt: e.tensor_tensor(out=wkAD[wi_][:nt, :], in0=pa[:nt, :],
                                                                          in1=mab[mi][:nt, 0:512], op=ALU.mult),
                 r=[Bpa, B_mab[mi]], w=[B_wkAD[wi_]])
            S.op("dve", lambda e, wi_=wi_, mi=mi, nt=nt: e.tensor_tensor(out=wkBD[wi_][:nt, :], in0=pb[:nt, :],
                                                                          in1=mab[mi][:nt, 512:1024], op=ALU.mult),
                 r=[Bpb, B_mab[mi]], w=[B_wkBD[wi_]])
            hi = nxt("l", NW)
            S.op("dve", lambda e, wi_=wi_, hi=hi, nt=nt: e.tensor_tensor(out=wkHD[hi][:nt, :], in0=wkAD[wi_][:nt, :],
                                                                          in1=wkBD[wi_][:nt, :], op=ALU.add),
                 r=[B_wkAD[wi_], B_wkBD[wi_]], w=[B_wkHD[hi]])
            to_T_scratch(wkHD[hi], B_wkHD[hi], nt, 512, [(mT_scr[n * 512:(n + 1) * 512, tok0:tok0 + nt], 0, nt, B_m)], (stgD, B_stgD))
    S.op("dve", lambda e: e.memset(absw[:, :], 0.0), r=[], w=[B_absw])
    for n in range(4 if KSTOP >= 4 else 0):
        wo = load_w(w_out, n * 512, 512, NCH, wbfD, B_wbfD)
        for ti, (tok0, nt, xc0, g0) in enumerate(own_tiles):
            hi_ = nxt("x", 2)
            S.dma("sp", lambda e, hi_=hi_, tok0=tok0, nt=nt, n=n: e.dma_start(
                out=hld[hi_][:nt, :], in_=x_own[tok0:tok0 + nt, n * 512:(n + 1) * 512]), w=[B_hld[hi_]])
            S.dma("sp", lambda e, hi_=hi_, tok0=tok0, nt=nt: e.dma_start(
                out=mTt[hi_][:, :, :nt], in_=mT_scr[:, tok0:tok0 + nt].rearrange("(ch p) t -> p ch t", p=128)),
                r=[B_m], w=[B_mTt[hi_]])
            pi = nxt("ps", 2)
            gemm_tile(psA[pi], B_psA[pi], mTt[hi_], B_mTt[hi_], 0, nt, wbfD[wo], B_wbfD[wo], NCH, 512)
            wi_ = nxt("wk", NW)
            S.op("dve", lambda e, wi_=wi_, hi_=hi_, pi=pi, nt=nt: e.tensor_tensor(
                out=wkAD[wi_][:nt, :], in0=psA[pi][:nt, :], in1=hld[hi_][:nt, :], op=ALU.add),
                r=[B_psA[pi], B_hld[hi_]], w=[B_wkAD[wi_]])
            S.dma("sp", lambda e, wi_=wi_, tok0=tok0, nt=nt, n=n: e.dma_start(
                out=hbuf[tok0:tok0 + nt, n * 512:(n + 1) * 512], in_=wkAD[wi_][:nt, :]), r=[B_wkAD[wi_]], w=[B_h])
            S.op("act", lambda e, wi_=wi_, nt=nt, ti=ti, n=n: e.activation(
                out=wkBD[wi_][:nt, :], in_=wkAD[wi_][:nt, :], func=AF.Square,
                accum_out=absw[:nt, ti * 16 + n:ti * 16 + n + 1]), r=[B_wkAD[wi_]], w=[B_wkBD[wi_], B_absw])
            for k in range(4):
                S.op("pe", lambda e, wi_=wi_, k=k, nt=nt: e.transpose(out=psC[:, k * 128:k * 128 + nt],
                                                                     in_=wkAD[wi_][:nt, k * 128:(k + 1) * 128],
                                                                     identity=ident_f[:nt, :nt]),
                     r=[B_wkAD[wi_], B_identf], w=[B_psC])
            for k in range(4):
                S.op("dve", lambda e, k=k, n=n, xc0=xc0, nt=nt: e.tensor_scalar(
                    out=hTs[:, 4 * n + k, xc0:xc0 + nt], in0=psC[:, k * 128:k * 128 + nt],
                    scalar1=pgcol[:, 4 * n + k:4 * n + k + 1], scalar2=None, op0=ALU.mult),
                    r=[B_psC, B_pgcol], w=[B_hTs])
    for ti, (tok0, nt, xc0, g0) in enumerate(own_tiles if KSTOP >= 4 else []):
        S.op("dve", lambda e, ti=ti, nt=nt: e.tensor_reduce(out=ssqt[:nt, ti:ti + 1], in_=absw[:nt, ti * 16:ti * 16 + 4],
                                                            axis=AX.X, op=ALU.add), r=[B_absw], w=[B_ssqt])
        S.op("act", lambda e, ti=ti, nt=nt: e.activation(out=rstd[:nt, ti:ti + 1], in_=ssqt[:nt, ti:ti + 1], func=AF.Sqrt,
                                                         scale=1.0 / D, bias=epsc[:nt, :]), r=[B_ssqt, B_epsc], w=[B_rstd])
        S.op("dve", lambda e, ti=ti, nt=nt: e.reciprocal(out=rstd[:nt, ti:ti + 1], in_=rstd[:nt, ti:ti + 1]),
             r=[B_rstd], w=[B_rstd])
    for n in range(4 if KSTOP >= 4 else 0):
        wg = load_w(w_pg, n * 512, 512, NCH, wbfD, B_wbfD)
        wp = load_w(w_ple, n * 512, 512, 2, wbf2, B_wbf2)
        for ti, (tok0, nt, xc0, g0) in enumerate(own_tiles):
            hi_ = nxt("x", 2)
            S.dma("sp", lambda e, hi_=hi_, tok0=tok0, nt=nt, n=n: e.dma_start(
                out=hld[hi_][:nt, :], in_=hbuf[tok0:tok0 + nt, n * 512:(n + 1) * 512]), r=[B_h], w=[B_hld[hi_]])
            pi = nxt("ps", 2)
            gemm_tile(psA[pi], B_psA[pi], hTs, B_hTs, xc0, nt, wbfD[wg], B_wbfD[wg], NCH, 512)
            gemm_tile(psB[pi], B_psB[pi], pTt, B_pTt, xc0, nt, wbf2[wp], B_wbf2[wp], 2, 512)
            wi_ = nxt("wk", NW)
            S.op("dve", lambda e, wi_=wi_, pi=pi, nt=nt, ti=ti: e.tensor_scalar(
                out=wkAD[wi_][:nt, :], in0=psA[pi][:nt, :], scalar1=rstd[:nt, ti:ti + 1], scalar2=None, op0=ALU.mult),
                r=[B_psA[pi], B_rstd], w=[B_wkAD[wi_]])
            S.op("act", lambda e, wi_=wi_, nt=nt: e.activation(
                out=wkAD[wi_][:nt, :], in_=wkAD[wi_][:nt, :], func=AF.Sigmoid),
                r=[B_wkAD[wi_]], w=[B_wkAD[wi_]])
            S.op("dve", lambda e, wi_=wi_, pi=pi, nt=nt: e.tensor_tensor(out=wkBD[wi_][:nt, :], in0=wkAD[wi_][:nt, :],
                                                                          in1=psB[pi][:nt, :], op=ALU.mult),
                 r=[B_wkAD[wi_], B_psB[pi]], w=[B_wkBD[wi_]])
            S.op("dve", lambda e, wi_=wi_, hi_=hi_, nt=nt: e.tensor_tensor(out=wkBD[wi_][:nt, :], in0=wkBD[wi_][:nt, :],
                                                                            in1=hld[hi_][:nt, :], op=ALU.add),
                 r=[B_wkBD[wi_], B_hld[hi_]], w=[B_wkBD[wi_]])
            S.dma("sp", lambda e, wi_=wi_, tok0=tok0, nt=nt, n=n: e.dma_start(
                out=y_own[tok0:tok0 + nt, n * 512:(n + 1) * 512], in_=wkBD[wi_][:nt, :]), r=[B_wkBD[wi_]], w=[B_out])

    with nc.Block() as block:
        @block.tensor
        def _(e):
            S.emit("pe", e)

        @block.scalar
        def _(e):
            S.emit("act", e)

        @block.vector
        def _(e):
            S.emit("dve", e)

        @block.gpsimd
        def _(e):
            S.emit("pool", e)

        @block.sync
        def _(e):
            S.emit("sp", e)
            S.final_waits(e)
    es.close()
    return nc, S


def _rope_table(pos):
    out = np.zeros((len(pos), 48), np.float32)
    p = pos.astype(np.float32)[:, None]
    for (r, c0) in ((16, 0), (32, 16)):
        inv = (np.float32(500000.0) ** (-np.arange(0, r, 2, dtype=np.float32) / np.float32(r))).astype(np.float32)
        ang = (p * inv[None, :]).astype(np.float32)
        out[:, c0:c0 + r // 2] = np.cos(ang)
        out[:, c0 + r // 2:c0 + r] = np.sin(ang)
    return out


_CACHE = {}


def kernel(x_prompt, x_sample, p_prompt, p_sample, cache_diff_k, cache_diff_v, cache_dsa_k, cache_dsa_v,
           cache_idx_k, ln_g, w_in, q_norm_a, k_norm_a, lam_q1, lam_k1, lam_q2, lam_k2, subln_a,
           q_norm_b, k_norm_b, k_norm_idx, w_branch_a, w_branch_b, w_out, ple_norm, w_ple_gate, w_ple):
    f = lambda a: np.ascontiguousarray(np.asarray(a, dtype=np.float32))
    xp = f(x_prompt)[0]
    xs = f(x_sample).reshape(512, D)
    pp = f(p_prompt)[0, 0]
    psm = f(p_sample)[0].reshape(512, 256)
    cdk = f(cache_diff_k)[0].reshape(32, PAST, 1024)
    cdv = f(cache_diff_v)[0].reshape(32, PAST, 1024)
    csk = f(cache_dsa_k)[0].reshape(32, PAST, 1024)
    csv = f(cache_dsa_v)[0].reshape(32, PAST, 1024)
    cik = f(cache_idx_k)[0]
    if "nc" not in _CACHE:
        _CACHE["nc"] = build_program()
    nc, S = _CACHE["nc"]
    rope_all = _rope_table(np.arange(SEQ))
    rope_s = _rope_table(PAST + np.arange(16))
    ident = np.eye(128, dtype=np.float32)
    shared = dict(x_all=xp[:XROWS], rope_all=rope_all[:XROWS], w_in=f(w_in)[0], ln_g=f(ln_g)[0], q_norm_a=f(q_norm_a)[0],
                  k_norm_a=f(k_norm_a)[0], lam_q1=f(lam_q1)[0], lam_k1=f(lam_k1)[0], lam_q2=f(lam_q2)[0],
                  lam_k2=f(lam_k2)[0], subln_a=f(subln_a)[0], q_norm_b=f(q_norm_b)[0], k_norm_b=f(k_norm_b)[0],
                  k_norm_idx=f(k_norm_idx)[0], w_branch_a=f(w_branch_a)[0], w_branch_b=f(w_branch_b)[0],
                  w_out=f(w_out)[0], ple_norm=f(ple_norm)[0], w_ple_gate=f(w_ple_gate)[0], w_ple=f(w_ple)[0],
                  ident=ident)
    in_maps = []
    rows_all = []
    kk = np.arange(128)[:, None]
    qq = np.arange(128)[None, :]
    diag_vis = (kk < 64) | (qq >= 64)
    for c in range(NCORES):
        rows = np.concatenate([np.arange((8 * j + c) * 128, (8 * j + c + 1) * 128) for j in range(NBLK)])
        rows_all.append(rows)
        x_own = np.concatenate([xp[rows], xs[64 * c:64 * (c + 1)]], 0)
        p_own = np.concatenate([pp[rows], psm[64 * c:64 * (c + 1)]], 0)
        rope_own = np.concatenate([rope_all[rows], np.tile(rope_s, (4, 1))], 0)
        mT = np.zeros((128, 8, 128), np.float32)
        for m in range(8):
            if m > c:
                mT[:, m, :] = NEG
            elif m == c:
                mT[:, m, :] = np.where(diag_vis, 0.0, NEG)
        mqk = np.ascontiguousarray(mT.transpose(2, 1, 0))
        d = dict(shared)
        d.update(x_own=np.ascontiguousarray(x_own), p_own=np.ascontiguousarray(p_own),
                 rope_own=np.ascontiguousarray(rope_own),
                 c_dk=np.ascontiguousarray(cdk[4 * c:4 * c + 4, :CROWS]), c_dv=np.ascontiguousarray(cdv[4 * c:4 * c + 4, :CROWS]),
                 c_sk=np.ascontiguousarray(csk[4 * c:4 * c + 4, :CROWS]), c_sv=np.ascontiguousarray(csv[4 * c:4 * c + 4, :CROWS]),
                 c_ik=np.ascontiguousarray(cik[4 * c:4 * c + 4, :CROWS]),
                 maskT=mT.reshape(128, 1024), maskqk=mqk.reshape(128, 1024))
        in_maps.append(d)
    res = run_bass_kernel_spmd(nc, in_maps[:KCORES], core_ids=list(range(KCORES)))
    R = res.results
    _CACHE['dbg'] = R[0]
    y_p = np.zeros((1, SEQ, D), np.float32)
    y_s = np.zeros((32, 16, D), np.float32)
    outs_p = {k: np.zeros((SEQ, n), np.float32) for k, n in (("o_dk", 1024), ("o_dv", 1024), ("o_sk", 1024),
                                                                ("o_sv", 1024), ("o_ik", 64))}
    outs_s = {k: np.zeros((512, n), np.float32) for k, n in (("o_dk", 1024), ("o_dv", 1024), ("o_sk", 1024),
                                                               ("o_sv", 1024), ("o_ik", 64))}
    for c in range(KCORES):
        r = R[c]
        yo = np.asarray(r["y_own"])
        y_p[0, rows_all[c]] = yo[:2048]
        y_s[4 * c:4 * c + 4] = yo[2048:].reshape(4, 16, D)
        for k in outs_p:
            a = np.asarray(r[k])
            outs_p[k][rows_all[c]] = a[:2048]
            outs_s[k][64 * c:64 * (c + 1)] = a[2048:]
    return (y_p, y_s,
            outs_p["o_dk"].reshape(1, 1, SEQ, 8, 2, 64), outs_p["o_dv"].reshape(1, 1, SEQ, 8, 128),
            outs_p["o_sk"].reshape(1, 1, SEQ, 8, 128), outs_p["o_sv"].reshape(1, 1, SEQ, 8, 128),
            outs_p["o_ik"].reshape(1, 1, SEQ, 64),
            outs_s["o_dk"].reshape(1, 32, 16, 8, 2, 64), outs_s["o_dv"].reshape(1, 32, 16, 8, 128),
            outs_s["o_sk"].reshape(1, 32, 16, 8, 128), outs_s["o_sv"].reshape(1, 32, 16, 8, 128),
            outs_s["o_ik"].reshape(1, 32, 16, 64))
```

```python
import contextlib
import numpy as np
import concourse.bass as bass
import concourse.mybir as mybir
from concourse.bass_utils import run_bass_kernel_spmd

F32 = mybir.dt.float32
BF16 = mybir.dt.bfloat16
ALU = mybir.AluOpType
AF = mybir.ActivationFunctionType
AX = mybir.AxisListType

NCORES = 8
D = 2048
NCH = 16
SEQ = 16384
NBLK = 16
NOWN = NBLK * 128 + 64
PAST = 4096
NKS = PAST + 16
N_IN = 13392
EPS = 1e-6
import os
DBG = bool(int(os.environ.get('KDBG', '0')))
KSTOP = int(os.environ.get('KSTOP', '9'))
SMALL = DBG and KSTOP < 2
KCORES = int(os.environ.get('KCORES', '8'))
KMAXOPS = int(os.environ.get('KMAXOPS', '100000000'))
KSEC = os.environ.get('KSEC', '')
KTRACE = [int(v) for v in os.environ.get('KTRACE', '').split(',')] if os.environ.get('KTRACE') else None
XROWS = 128 if SMALL else 16384
CROWS = 128 if SMALL else 4096
NEG = -30000.0
NBIS = 18
BIS_LO = -16.0
BIS_W = 32.0
TOPK = 256
LAM_INIT = 0.2


class Buf:
    __slots__ = ("name", "w", "r")

    def __init__(self, name):
        self.name = name
        self.w = None
        self.r = []


class Sched:
    def __init__(self, sems):
        self.free = list(sems)
        self.q = {e: [] for e in ("pe", "act", "dve", "pool", "sp")}
        self.cnt = {}
        self.sem = {}
        for e in ("pe", "act", "dve"):
            self.sem[e] = self.free.pop()
            self.cnt[e] = 0
        self.dsem = {"sp": [self.free.pop() for _ in range(20)],
                     "pool": [self.free.pop() for _ in range(20)]}
        self.dval = {}
        self.drot = {"sp": 0, "pool": 0}
        self.known = {e: {} for e in self.q}
        self.nops = 0

    def _waits(self, eng, r, w):
        need = {}
        for b in r:
            if b.w is not None:
                s, v = b.w
                need[s] = max(need.get(s, 0), v)
        for b in w:
            if b.w is not None:
                s, v = b.w
                need[s] = max(need.get(s, 0), v)
            for (s, v) in b.r:
                need[s] = max(need.get(s, 0), v)
        out = []
        kn = self.known[eng]
        for s, v in need.items():
            if eng == "pe" and s is self.sem.get("pe"):
                continue
            if kn.get(id(s), 0) >= v:
                continue
            kn[id(s)] = v
            out.append((s, v))
        return out

    def _commit(self, ev, r, w):
        for b in r:
            b.r.append(ev)
            if len(b.r) > 64:
                m = {}
                for (s, v) in b.r:
                    k = id(s)
                    if k not in m or m[k][1] < v:
                        m[k] = (s, v)
                b.r = list(m.values())
        for b in w:
            b.w = ev
            b.r = []

    def op(self, eng, fn, r=(), w=()):
        if self.nops >= KMAXOPS:
            return
        waits = self._waits(eng, r, w)
        if self.cnt[eng] >= 30000:
            self.sem[eng] = self.free.pop()
            self.cnt[eng] = 0
        self.cnt[eng] += 1
        s = self.sem[eng]
        ev = (s, self.cnt[eng])
        self.q[eng].append((waits, fn, s, 1))
        self._commit(ev, r, w)
        self.nops += 1
        if KTRACE and KTRACE[0] <= self.nops <= KTRACE[1]:
            print("OP", self.nops, eng, fn.__code__.co_firstlineno, flush=True)

    def dma(self, qe, fn, r=(), w=()):
        if self.nops >= KMAXOPS:
            return
        waits = self._waits(qe, r, w)
        i = self.drot[qe]
        self.drot[qe] = (i + 1) % len(self.dsem[qe])
        s = self.dsem[qe][i]
        prev = self.dval.get(id(s), 0)
        kn = self.known[qe]
        if prev > 0 and kn.get(id(s), 0) < prev:
            kn[id(s)] = prev
            waits.append((s, prev))
        self.dval[id(s)] = prev + 16
        ev = (s, prev + 16)
        self.q[qe].append((waits, fn, s, 16))
        self._commit(ev, r, w)
        self.nops += 1
        if KTRACE and KTRACE[0] <= self.nops <= KTRACE[1]:
            print("DMA", self.nops, qe, fn.__code__.co_firstlineno, flush=True)

    def barrier(self):
        evs = []
        for en in ("pe", "act", "dve"):
            if self.cnt[en] > 0:
                evs.append((self.sem[en], self.cnt[en]))
        for qe in ("sp", "pool"):
            for s in self.dsem[qe]:
                v = self.dval.get(id(s), 0)
                if v > 0:
                    evs.append((s, v))
        for eng in self.q:
            kn = self.known[eng]
            waits = []
            for (s, v) in evs:
                if kn.get(id(s), 0) >= v:
                    continue
                kn[id(s)] = v
                waits.append((s, v))
            self.q[eng].append((waits, None, None, 0))

    def emit(self, qe, e):
        for (waits, fn, s, inc) in self.q[qe]:
            for (ws, wv) in waits:
                e.wait_ge(ws, wv)
            if fn is not None:
                fn(e).then_inc(s, inc)

    def final_waits(self, e):
        for qe in ("sp", "pool"):
            for s in self.dsem[qe]:
                v = self.dval.get(id(s), 0)
                if v > 0:
                    e.wait_ge(s, v)
        for en in ("pe", "act", "dve"):
            if self.cnt[en] > 0:
                e.wait_ge(self.sem[en], self.cnt[en])


def build_program():
    nc = bass.Bass("TRN2", target_bir_lowering=False)
    es = contextlib.ExitStack()

    def din(name, shape, dt=F32):
        return nc.dram_tensor(name, list(shape), dt, kind="ExternalInput")

    def dout(name, shape, dt=F32):
        return nc.dram_tensor(name, list(shape), dt, kind="ExternalOutput")

    def dscr(name, shape, dt=BF16):
        if DBG and name in ("uT", "hbuf", "mT_scr", "qaT", "qbT", "qiT", "sga", "sgn_scr", "kiT", "kaT", "va"):
            return nc.dram_tensor(name, list(shape), dt, kind="ExternalOutput")
        return nc.dram_tensor(name, list(shape), dt, kind="Internal")

    x_all = din("x_all", [XROWS, D]).ap()
    x_own = din("x_own", [NOWN, D]).ap()
    p_own = din("p_own", [NOWN, 256]).ap()
    rope_all = din("rope_all", [XROWS, 48]).ap()
    rope_own = din("rope_own", [NOWN, 48]).ap()
    c_dk = din("c_dk", [4, CROWS, 1024]).ap()
    c_dv = din("c_dv", [4, CROWS, 1024]).ap()
    c_sk = din("c_sk", [4, CROWS, 1024]).ap()
    c_sv = din("c_sv", [4, CROWS, 1024]).ap()
    c_ik = din("c_ik", [4, CROWS, 64]).ap()
    w_in = din("w_in", [D, N_IN]).ap()
    ln_g_h = din("ln_g", [D])
    vec_h = {}
    for nm, n in (("q_norm_a", 64), ("k_norm_a", 64), ("lam_q1", 64), ("lam_k1", 64), ("lam_q2", 64),
                  ("lam_k2", 64), ("subln_a", 128), ("q_norm_b", 128), ("k_norm_b", 128), ("k_norm_idx", 64)):
        vec_h[nm] = din(nm, [n])
    w_ba = din("w_branch_a", [1024, D]).ap()
    w_bb = din("w_branch_b", [1024, D]).ap()
    w_out = din("w_out", [D, D]).ap()
    ple_g_h = din("ple_norm", [D])
    w_pg = din("w_ple_gate", [D, D]).ap()
    w_ple = din("w_ple", [256, D]).ap()
    ident_d = din("ident", [128, 128]).ap()
    maskT_d = din("maskT", [128, 8 * 128]).ap()
    maskqk_d = din("maskqk", [128, 8 * 128]).ap()

    if DBG:
        dbg_eo = dout("dbg_eo", [128, 256]).ap()
        dbg_ep = dout("dbg_ep", [128, 16]).ap()
        dbg_ps = dout("dbg_ps", [128, 1024]).ap()
        dbg_lam = dout("dbg_lam", [128, 8]).ap()
    y_own = dout("y_own", [NOWN, D]).ap()
    o_dk = dout("o_dk", [NOWN, 1024]).ap()
    o_dv = dout("o_dv", [NOWN, 1024]).ap()
    o_sk = dout("o_sk", [NOWN, 1024]).ap()
    o_sv = dout("o_sv", [NOWN, 1024]).ap()
    o_ik = dout("o_ik", [NOWN, 64]).ap()

    kaT = dscr("kaT", [1024, SEQ]).ap()
    va = dscr("va", [SEQ, 1024]).ap()
    kbT = dscr("kbT", [1024, SEQ]).ap()
    vb = dscr("vb", [SEQ, 1024]).ap()
    kiT = dscr("kiT", [64, SEQ]).ap()
    s_kaT = dscr("s_kaT", [4, 1024, NKS]).ap()
    s_va = dscr("s_va", [4, NKS, 1024]).ap()
    s_kbT = dscr("s_kbT", [4, 1024, NKS]).ap()
    s_vb = dscr("s_vb", [4, NKS, 1024]).ap()
    s_kiT = dscr("s_kiT", [4, 64, NKS]).ap()
    qaT = dscr("qaT", [1024, NOWN]).ap()
    qbT = dscr("qbT", [1024, NOWN]).ap()
    qiT = dscr("qiT", [1024, NOWN]).ap()
    sga = dscr("sga", [NOWN, 1024]).ap()
    sgb = dscr("sgb", [NOWN, 1024]).ap()
    smg = dscr("smg", [NOWN, 4096]).ap()
    uT = dscr("uT", [D, NOWN]).ap()
    hbuf = dscr("hbuf", [NOWN, D], F32).ap()
    sgn_scr = dscr("sgn_scr", [NOWN, 16], F32).ap()
    mT_scr = dscr("mT_scr", [D, NOWN]).ap()
    B_kv = Buf("kv_scr")
    B_skv = Buf("skv_scr")
    B_q = Buf("q_scr")
    B_g = Buf("g_scr")
    B_u = Buf("u_scr")
    B_h = Buf("h_scr")
    B_out = Buf("outs")

    def sb(name, shape, dt=F32):
        return es.enter_context(nc.sbuf_tensor(name, list(shape), dt))

    def ps(name, shape, dt=F32):
        return es.enter_context(nc.psum_tensor(name, list(shape), dt))

    sems = [es.enter_context(nc.semaphore(f"s{i}")) for i in range(90)]
    S = Sched(sems)

    ident_f = sb("ident_fsb", [128, 128]); B_identf = Buf("identf")
    ident_b = sb("ident_b", [128, 128], BF16); B_identb = Buf("identb")
    maskT = sb("maskT_sb", [128, 1024], BF16); B_maskT = Buf("maskT")
    maskqk = sb("maskqk_sb", [128, 1024], BF16); B_maskqk = Buf("maskqk")
    gcol = sb("gcol", [128, 16]); B_gcol = Buf("gcol")
    pgcol = sb("pgcol", [128, 16]); B_pgcol = Buf("pgcol")
    gains = sb("gains", [128, 6 * 128]); B_gains = Buf("gains")
    lamv = sb("lamv", [128, 4 * 64]); B_lamv = Buf("lamv")
    lamt = sb("lamt", [128, 8]); B_lamt = Buf("lamt")
    lamj = sb("lamj", [128, 64]); B_lamj = Buf("lamj")
    epsc = sb("epsc", [128, 1]); B_epsc = Buf("epsc")
    rstd = sb("rstd", [128, 20]); B_rstd = Buf("rstd")
    ssqt = sb("ssqt", [128, 20]); B_ssqt = Buf("ssqt")
    absw = sb("absw", [128, 17 * 16]); B_absw = Buf("absw")
    sgnw = sb("sgnw", [128, 17 * 16]); B_sgnw = Buf("sgnw")
    pTt = sb("pTt", [128, 2, NOWN], BF16); B_pTt = Buf("pTt")
    ARENA = 180224
    arena = sb("arena", [128, ARENA // 2], BF16)
    ar = {"off": 0}

    def aalloc(shape, dt=F32):
        n = 1
        for s_ in shape[1:]:
            n *= s_
        nb = n * (4 if dt == F32 else 2)
        nb = (nb + 63) // 64 * 64
        o = ar["off"]
        ar["off"] = o + nb
        assert ar["off"] <= ARENA, (ar["off"], ARENA)
        v = arena[:, o // 2:(o + n * (4 if dt == F32 else 2)) // 2]
        if dt == F32:
            v = v.bitcast(F32)
        if len(shape) == 3:
            v = v.rearrange("p (a b) -> p a b", b=shape[2])
        if shape[0] < 128:
            v = v[:shape[0]]
        return v

    def areset():
        S.barrier()
        ar["off"] = 0

    NB2 = 2
    NW = 3
    NKB = 3
    NP = 3

    psA = [ps("psA0", [128, 512]), ps("psA1", [128, 512])]; B_psA = [Buf("psA0"), Buf("psA1")]
    psB = [ps("psB0", [128, 512]), ps("psB1", [128, 512])]; B_psB = [Buf("psB0"), Buf("psB1")]
    psL = [ps("psL0", [128, 512]), ps("psL1", [128, 512])]; B_psL = [Buf("psL0"), Buf("psL1")]
    psC = ps("psC", [128, 512]); B_psC = Buf("psC")
    psT = ps("psT", [128, 1024], BF16); B_psT = Buf("psT")

    xT = aalloc([128, NCH, NOWN], BF16); B_xT = Buf("xT")
    xin = [aalloc([128, D]) for i in range(NB2)]; B_xin = [Buf(f"xin{i}") for i in range(NB2)]
    xin_junk = aalloc([128, D], BF16); B_junk = Buf("junk")
    wbf = [aalloc([128, NCH, 512], BF16) for i in range(2)]; B_wbf = [Buf(f"wbf{i}") for i in range(2)]
    ropeall = aalloc([128, 17 * 48]); B_ropeall = Buf("ropeall")
    wkA = [aalloc([128, 512]) for i in range(NW)]; B_wkA = [Buf(f"wkA{i}") for i in range(NW)]
    wkB = [aalloc([128, 512]) for i in range(NW)]; B_wkB = [Buf(f"wkB{i}") for i in range(NW)]
    wkS = [aalloc([128, 16]) for i in range(NW)]; B_wkS = [Buf(f"wkS{i}") for i in range(NW)]
    wkR = [aalloc([128, 4 * 64]) for i in range(NW)]; B_wkR = [Buf(f"wkR{i}") for i in range(NW)]
    wkH = [aalloc([128, 512], BF16) for i in range(NW)]; B_wkH = [Buf(f"wkH{i}") for i in range(NW)]
    stg = [aalloc([128, 4 * 128], BF16) for i in range(NW)]; B_stg = [Buf(f"stg{i}") for i in range(NW)]
    kstage = [aalloc([128, 1024]) for i in range(2)]; B_kstage = [Buf(f"kstage{i}") for i in range(2)]
    kistage = [aalloc([128, 64]) for i in range(2)]; B_kistage = [Buf(f"kistage{i}") for i in range(2)]
    kTs = [aalloc([128, 8, 128], BF16) for i in range(2)]; B_kTs = [Buf(f"kTs{i}") for i in range(2)]
    kiTs = [aalloc([64, 128], BF16) for i in range(2)]; B_kiTs = [Buf(f"kiTs{i}") for i in range(2)]
    vstage = [aalloc([128, 1024], BF16) for i in range(3)]; B_vstage = [Buf(f"vstage{i}") for i in range(3)]

    def bc_rows(handle, n):
        return bass.AP(handle, 0, [[0, 128], [1, n]])

    S.dma("sp", lambda e: e.dma_start(out=ident_f[:, :], in_=ident_d[:, :]), w=[B_identf])
    S.dma("pool", lambda e: e.dma_start(out=ident_b[:, :], in_=ident_d[:, :]), w=[B_identb])
    S.dma("pool", lambda e: e.dma_start(out=maskT[:, :], in_=maskT_d[:, :]), w=[B_maskT])
    S.dma("pool", lambda e: e.dma_start(out=maskqk[:, :], in_=maskqk_d[:, :]), w=[B_maskqk])
    S.dma("sp", lambda e: e.dma_start(out=gcol[:, :], in_=bass.AP(ln_g_h, 0, [[1, 128], [128, 16]]),
                                      allow_slow_non_contiguous=True), w=[B_gcol])
    S.dma("sp", lambda e: e.dma_start(out=pgcol[:, :], in_=bass.AP(ple_g_h, 0, [[1, 128], [128, 16]]),
                                      allow_slow_non_contiguous=True), w=[B_pgcol])
    for i, (nm, n) in enumerate((("q_norm_a", 64), ("k_norm_a", 64), ("q_norm_b", 128), ("k_norm_b", 128),
                                 ("k_norm_idx", 64), ("subln_a", 128))):
        S.dma("sp", lambda e, i=i, nm=nm, n=n: e.dma_start(out=gains[:, i * 128:i * 128 + n],
                                                           in_=bc_rows(vec_h[nm], n)), w=[B_gains])
    for i, nm in enumerate(("lam_q1", "lam_k1", "lam_q2", "lam_k2")):
        S.dma("sp", lambda e, i=i, nm=nm: e.dma_start(out=lamv[:, i * 64:(i + 1) * 64], in_=bc_rows(vec_h[nm], 64)),
              w=[B_lamv])
    G_QNA, G_KNA, G_QNB, G_KNB, G_KNI, G_SUB = [gains[:, i * 128:(i + 1) * 128] for i in range(6)]
    S.op("dve", lambda e: e.memset(epsc[:, :], EPS), w=[B_epsc])
    S.op("dve", lambda e: e.tensor_tensor(out=lamj[:, :], in0=lamv[:, 0:64], in1=lamv[:, 64:128], op=ALU.mult),
         r=[B_lamv], w=[B_lamj])
    S.op("dve", lambda e: e.tensor_reduce(out=lamt[:, 0:1], in_=lamj[:, :], axis=AX.X, op=ALU.add),
         r=[B_lamj], w=[B_lamt])
    S.op("dve", lambda e: e.tensor_tensor(out=lamj[:, :], in0=lamv[:, 128:192], in1=lamv[:, 192:256], op=ALU.mult),
         r=[B_lamv, B_lamt], w=[B_lamj])
    S.op("dve", lambda e: e.tensor_reduce(out=lamt[:, 1:2], in_=lamj[:, :], axis=AX.X, op=ALU.add),
         r=[B_lamj], w=[B_lamt])
    S.op("act", lambda e: e.activation(out=lamt[:, 2:4], in_=lamt[:, 0:2], func=AF.Exp), r=[B_lamt], w=[B_lamt])
    S.op("dve", lambda e: e.tensor_tensor(out=lamt[:, 4:5], in0=lamt[:, 3:4], in1=lamt[:, 2:3], op=ALU.subtract),
         r=[B_lamt], w=[B_lamt])
    S.op("dve", lambda e: e.tensor_scalar(out=lamt[:, 4:5], in0=lamt[:, 4:5], scalar1=-LAM_INIT, scalar2=None,
                                          op0=ALU.add), r=[B_lamt], w=[B_lamt])
    NEGLAM = lamt[:, 4:5]

    rot = {"w": 0, "x": 0, "wk": 0, "ps": 0, "psb": 0, "kb": 0, "p": 0, "l": 0, "r": 0, "ev": 0, "vst": 0, "kst": 0, "kist": 0}

    def nxt(k, n):
        v = rot[k]
        rot[k] = (v + 1) % n
        return v

    def evac_engine():
        return "act" if nxt("ev", 2) == 0 else "dve"

    def build_xT(src, tiles, dst, B_dst, colg, B_colg, with_stats):
        for ti, (tok0, nt, c0) in enumerate(tiles):
            xi = nxt("x", NB2)
            S.dma("sp", lambda e, xi=xi, tok0=tok0, nt=nt: e.dma_start(out=xin[xi][:nt, :], in_=src[tok0:tok0 + nt, :]),
                  w=[B_xin[xi]])
            if with_stats:
                S.op("act", lambda e, xi=xi, nt=nt, ti=ti: e.activation(
                    out=xin_junk[:nt, :], in_=xin[xi][:nt, :], func=AF.Square, accum_out=ssqt[:nt, ti:ti + 1]),
                    r=[B_xin[xi]], w=[B_junk, B_ssqt])
                S.op("act", lambda e, nt=nt, ti=ti: e.activation(
                    out=rstd[:nt, ti:ti + 1], in_=ssqt[:nt, ti:ti + 1], func=AF.Sqrt, scale=1.0 / D, bias=epsc[:nt, :]),
                    r=[B_ssqt, B_epsc], w=[B_rstd])
                S.op("dve", lambda e, nt=nt, ti=ti: e.reciprocal(out=rstd[:nt, ti:ti + 1], in_=rstd[:nt, ti:ti + 1]),
                     r=[B_rstd], w=[B_rstd])
            for q4 in range(4):
                S_ps = B_psC
                for k in range(4):
                    ch = q4 * 4 + k
                    S.op("pe", lambda e, xi=xi, nt=nt, ch=ch, k=k: e.transpose(
                        out=psC[:, k * 128:k * 128 + nt], in_=xin[xi][:nt, ch * 128:(ch + 1) * 128],
                        identity=ident_f[:nt, :nt]), r=[B_xin[xi], B_identf], w=[S_ps])
                for k in range(4):
                    ch = q4 * 4 + k
                    S.op("dve", lambda e, nt=nt, ch=ch, k=k, c0=c0: e.tensor_scalar(
                        out=dst[:, ch, c0:c0 + nt], in0=psC[:, k * 128:k * 128 + nt], scalar1=colg[:, ch:ch + 1],
                        scalar2=None, op0=ALU.mult), r=[S_ps, B_colg], w=[B_dst])


    def load_w(wsrc, col0, ncols, nchk, tiles_, B_tiles):
        wi = nxt("w", 2)
        S.dma("pool", lambda e, wi=wi: e.dma_start(
            out=tiles_[wi][:, :nchk, :ncols],
            in_=wsrc[:, col0:col0 + ncols].rearrange("(ch p) c -> p ch c", p=128)), w=[B_tiles[wi]])
        return wi

    def gemm_tile(pst, B_pst, aT, B_aT, tcol0, nt, wt, B_wt, nchk, ncols):
        for ch in range(nchk):
            S.op("pe", lambda e, ch=ch: e.matmul(out=pst[:nt, :ncols], lhsT=aT[:, ch, tcol0:tcol0 + nt],
                                                 rhs=wt[:, ch, :ncols], start=(ch == 0), stop=(ch == nchk - 1)),
                 r=[B_aT, B_wt], w=[B_pst])

    def norm_rope(pst, B_pst, nt, ncols, dh, gain_ap, rcol, cs, B_cs, do_norm):
        G = ncols // dh
        r2 = dh // 8
        c0 = 0 if dh == 64 else 16
        wi_ = nxt("wk", NW)
        A, Bt, Ss, Rr = wkA[wi_], wkB[wi_], wkS[wi_], wkR[wi_]
        BA, BB, BS, BR = B_wkA[wi_], B_wkB[wi_], B_wkS[wi_], B_wkR[wi_]
        S.op("dve", lambda e: e.tensor_scalar(out=A[:nt, :ncols], in0=pst[:nt, :ncols], scalar1=rcol, scalar2=None,
                                              op0=ALU.mult), r=[B_pst, B_rstd], w=[BA])
        A3 = A[:nt, :ncols].rearrange("p (g d) -> p g d", d=dh)
        if do_norm:
            S.op("act", lambda e: e.activation(out=Bt[:nt, :ncols], in_=A[:nt, :ncols], func=AF.Square), r=[BA], w=[BB])
            S.op("dve", lambda e: e.tensor_reduce(out=Ss[:nt, 0:G], in_=Bt[:nt, :ncols].rearrange("p (g d) -> p g d", d=dh),
                                                  axis=AX.X, op=ALU.add), r=[BB], w=[BS])
            S.op("act", lambda e: e.activation(out=Ss[:nt, 8:8 + G], in_=Ss[:nt, 0:G], func=AF.Sqrt, scale=1.0 / dh,
                                               bias=epsc[:nt, :]), r=[BS, B_epsc], w=[BS])
            S.op("dve", lambda e: e.reciprocal(out=Ss[:nt, 8:8 + G], in_=Ss[:nt, 8:8 + G]), r=[BS], w=[BS])
            S.op("dve", lambda e: e.tensor_tensor(out=A3, in0=A3,
                                                  in1=Ss[:nt, 8:8 + G].unsqueeze(2).broadcast_to([nt, G, dh]),
                                                  op=ALU.mult), r=[BA, BS], w=[BA])
            S.op("dve", lambda e: e.tensor_tensor(out=A3, in0=A3,
                                                  in1=gain_ap[:nt, 0:dh].unsqueeze(1).broadcast_to([nt, G, dh]),
                                                  op=ALU.mult), r=[BA, B_gains], w=[BA])
        x1 = A3[:, :, 0:r2]
        x2 = A3[:, :, r2:2 * r2]
        cosb = cs[:nt, c0:c0 + r2].unsqueeze(1).broadcast_to([nt, G, r2])
        sinb = cs[:nt, c0 + r2:c0 + 2 * r2].unsqueeze(1).broadcast_to([nt, G, r2])
        n = G * r2
        T = [Rr[:nt, k * 64:k * 64 + n].rearrange("p (g d) -> p g d", d=r2) for k in range(4)]
        S.op("dve", lambda e: e.tensor_tensor(out=T[0], in0=x1, in1=cosb, op=ALU.mult), r=[BA, B_cs], w=[BR])
        S.op("dve", lambda e: e.tensor_tensor(out=T[1], in0=x2, in1=sinb, op=ALU.mult), r=[BA, B_cs], w=[BR])
        S.op("dve", lambda e: e.tensor_tensor(out=T[2], in0=x2, in1=cosb, op=ALU.mult), r=[BA, B_cs], w=[BR])
        S.op("dve", lambda e: e.tensor_tensor(out=T[3], in0=x1, in1=sinb, op=ALU.mult), r=[BA, B_cs], w=[BR])
        S.op("dve", lambda e: e.tensor_tensor(out=x1, in0=T[0], in1=T[1], op=ALU.subtract), r=[BR], w=[BA])
        S.op("dve", lambda e: e.tensor_tensor(out=x2, in0=T[2], in1=T[3], op=ALU.add), r=[BR], w=[BA])
        return wi_

    def to_T_scratch(src_bf, B_src, nt, ncols, dsts, stg_tiles=None):
        nk = (ncols + 127) // 128
        si = nxt("r", NW)
        if stg_tiles is None:
            stg_tiles = (stg, B_stg)
        st = stg_tiles[0][si]
        Bst = stg_tiles[1][si]
        for k in range(nk):
            cw = min(128, ncols - k * 128)
            S.op("pe", lambda e, k=k, cw=cw: e.transpose(out=psT[:cw, k * 128:k * 128 + nt],
                                                         in_=src_bf[:nt, k * 128:k * 128 + cw],
                                                         identity=ident_b[:nt, :nt]), r=[B_src, B_identb], w=[B_psT])
        rows = min(128, ncols)
        eng = evac_engine()
        if eng == "act":
            S.op("act", lambda e: e.copy(out=st[:rows, :nk * 128], in_=psT[:rows, :nk * 128]), r=[B_psT], w=[Bst])
        else:
            S.op("dve", lambda e: e.tensor_copy(out=st[:rows, :nk * 128], in_=psT[:rows, :nk * 128]),
                 r=[B_psT], w=[Bst])
        for (dap, t0, n, Bd) in dsts:
            if ncols >= 128:
                S.dma("sp", lambda e, dap=dap, t0=t0, n=n: e.dma_start(
                    out=dap.rearrange("(ch p) t -> p ch t", p=128),
                    in_=st[:, :nk * 128].rearrange("p (ch t) -> p ch t", t=128)[:, :, t0:t0 + n]),
                    r=[Bst], w=[Bd])
            else:
                S.dma("sp", lambda e, dap=dap, t0=t0, n=n: e.dma_start(out=dap, in_=st[:ncols, t0:t0 + n]),
                      r=[Bst], w=[Bd])

    def cast_bf(src_ap, B_src, nt, ncols):
        hi = nxt("l", NW)
        S.op("act", lambda e: e.copy(out=wkH[hi][:nt, :ncols], in_=src_ap), r=[B_src], w=[B_wkH[hi]])
        return hi

    def sample_prep_gen():
        for b in range(4):
            for (csrc, dst) in ((c_dv, s_va), (c_sv, s_vb)):
                for kt in range(32):
                    vi = nxt("vst", 3)
                    S.dma("pool", lambda e, csrc=csrc, b=b, kt=kt, vi=vi: e.dma_start(
                        out=vstage[vi][:, :], in_=csrc[b, kt * 128:(kt + 1) * 128, :]), w=[B_vstage[vi]])
                    S.dma("sp", lambda e, dst=dst, b=b, kt=kt, vi=vi: e.dma_start(
                        out=dst[b, kt * 128:(kt + 1) * 128, :], in_=vstage[vi][:, :]), r=[B_vstage[vi]], w=[B_skv])
                    yield
            for (csrc, dst) in ((c_dk, s_kaT), (c_sk, s_kbT)):
                for kt in range(32):
                    ki_ = nxt("kst", 2)
                    S.dma("sp", lambda e, csrc=csrc, b=b, kt=kt, ki_=ki_: e.dma_start(
                        out=kstage[ki_][:, :], in_=csrc[b, kt * 128:(kt + 1) * 128, :]), w=[B_kstage[ki_]])
                    for half in range(2):
                        pi = nxt("ps", 2)
                        for k in range(4):
                            ch = half * 4 + k
                            S.op("pe", lambda e, ch=ch, k=k, pi=pi, ki_=ki_: e.transpose(
                                out=psA[pi][:, k * 128:(k + 1) * 128], in_=kstage[ki_][:, ch * 128:(ch + 1) * 128],
                                identity=ident_f[:, :]), r=[B_kstage[ki_], B_identf], w=[B_psA[pi]])
                        S.op("act", lambda e, half=half, pi=pi, ki_=ki_: e.copy(
                            out=kTs[ki_][:, half * 4:(half + 1) * 4, :].rearrange("p a b -> p (a b)"), in_=psA[pi][:, :]),
                            r=[B_psA[pi]], w=[B_kTs[ki_]])
                    S.dma("sp", lambda e, dst=dst, b=b, kt=kt, ki_=ki_: e.dma_start(
                        out=dst[b, :, kt * 128:(kt + 1) * 128].rearrange("(ch p) t -> p ch t", p=128), in_=kTs[ki_][:, :, :]),
                        r=[B_kTs[ki_]], w=[B_skv])
                    yield
            for kt in range(32):
                ki_ = nxt("kist", 2)
                S.dma("sp", lambda e, b=b, kt=kt, ki_=ki_: e.dma_start(out=kistage[ki_][:, :],
                                                                        in_=c_ik[b, kt * 128:(kt + 1) * 128, :]),
                      w=[B_kistage[ki_]])
                S.op("pe", lambda e, ki_=ki_: e.transpose(out=psC[:64, 0:128], in_=kistage[ki_][:, :], identity=ident_f[:, :]),
                     r=[B_kistage[ki_], B_identf], w=[B_psC])
                S.op("act", lambda e, ki_=ki_: e.copy(out=kiTs[ki_][:, :], in_=psC[:64, 0:128]), r=[B_psC], w=[B_kiTs[ki_]])
                S.dma("sp", lambda e, b=b, kt=kt, ki_=ki_: e.dma_start(out=s_kiT[b, :, kt * 128:(kt + 1) * 128],
                                                                        in_=kiTs[ki_][:, :]),
                      r=[B_kiTs[ki_]], w=[B_skv])
                yield

    bg = {"gen": None}

    def bg_step():
        if bg["gen"] is not None:
            try:
                next(bg["gen"])
            except StopIteration:
                bg["gen"] = None

    SEC = {"qa": 0, "ka": 1024, "va": 2048, "ga": 3072, "qb": 4096, "kb": 5120, "vb": 6144, "gb": 7168,
           "qi": 8192, "kiwi": 9216, "mg": 9296}

    def proj_pass(src, rope_src, tiles, blocks, own, xT, B_xT):
        build_xT(src, [(t[0], t[1], t[2]) for t in tiles], xT, B_xT, gcol, B_gcol, True)
        for ti, (tok0, nt, xc0, g0) in enumerate(tiles):
            S.dma("sp", lambda e, ti=ti, tok0=tok0, nt=nt: e.dma_start(out=ropeall[:nt, ti * 48:(ti + 1) * 48],
                                                                      in_=rope_src[tok0:tok0 + nt, :]), w=[B_ropeall])
        for (sec, off, ncols) in blocks:
            col0 = SEC[sec] + off
            wi = load_w(w_in, col0, ncols, NCH, wbf, B_wbf)
            for ti, (tok0, nt, xc0, g0) in enumerate(tiles):
                bg_step()
                pi = nxt("ps", 2)
                pst, Bp = psA[pi], B_psA[pi]
                gemm_tile(pst, Bp, xT, B_xT, xc0, nt, wbf[wi], B_wbf[wi], NCH, ncols)
                rcol = rstd[:nt, ti:ti + 1]
                cs = ropeall[:, ti * 48:(ti + 1) * 48]
                is_samp = own and nt == 64
                if sec in ("qa", "ka", "qb", "kb"):
                    dh = 64 if sec in ("qa", "ka") else 128
                    gain = {"qa": G_QNA, "ka": G_KNA, "qb": G_QNB, "kb": G_KNB}[sec]
                    wi_ = norm_rope(pst, Bp, nt, ncols, dh, gain, rcol, cs, B_ropeall, True)
                    if sec in ("ka", "kb"):
                        if own:
                            od = o_dk if sec == "ka" else o_sk
                            S.dma("sp", lambda e, od=od, wi_=wi_, g0=g0, nt=nt, off=off: e.dma_start(
                                out=od[g0:g0 + nt, off:off + ncols], in_=wkA[wi_][:nt, :ncols]),
                                r=[B_wkA[wi_]], w=[B_out])
                        if (not own) or is_samp:
                            hi = cast_bf(wkA[wi_][:nt, :ncols], B_wkA[wi_], nt, ncols)
                            if not own:
                                dT = kaT if sec == "ka" else kbT
                                dsts = [(dT[off:off + ncols, g0:g0 + nt], 0, nt, B_kv)]
                            else:
                                dT = s_kaT if sec == "ka" else s_kbT
                                dsts = [(dT[b, off:off + ncols, PAST:PAST + 16], 16 * b, 16, B_skv) for b in range(4)]
                            to_T_scratch(wkH[hi], B_wkH[hi], nt, ncols, dsts)
                    else:
                        hi = cast_bf(wkA[wi_][:nt, :ncols], B_wkA[wi_], nt, ncols)
                        dT = qaT if sec == "qa" else qbT
                        to_T_scratch(wkH[hi], B_wkH[hi], nt, ncols, [(dT[off:off + ncols, g0:g0 + nt], 0, nt, B_q)])
                elif sec in ("va", "vb"):
                    if own:
                        wi_ = nxt("wk", NW)
                        S.op("dve", lambda e, wi_=wi_, pst=pst, nt=nt, rcol=rcol: e.tensor_scalar(
                            out=wkA[wi_][:nt, :ncols], in0=pst[:nt, :ncols], scalar1=rcol, scalar2=None, op0=ALU.mult),
                            r=[Bp, B_rstd], w=[B_wkA[wi_]])
                        od = o_dv if sec == "va" else o_sv
                        S.dma("sp", lambda e, od=od, wi_=wi_, g0=g0, nt=nt, off=off: e.dma_start(
                            out=od[g0:g0 + nt, off:off + ncols], in_=wkA[wi_][:nt, :ncols]), r=[B_wkA[wi_]], w=[B_out])
                    if (not own) or is_samp:
                        hi = nxt("l", NW)
                        S.op("dve", lambda e, hi=hi, pst=pst, nt=nt, rcol=rcol: e.tensor_scalar(
                            out=wkH[hi][:nt, :ncols], in0=pst[:nt, :ncols], scalar1=rcol, scalar2=None, op0=ALU.mult),
                            r=[Bp, B_rstd], w=[B_wkH[hi]])
                        if not own:
                            dV = va if sec == "va" else vb
                            S.dma("sp", lambda e, dV=dV, hi=hi, g0=g0, nt=nt, off=off: e.dma_start(
                                out=dV[g0:g0 + nt, off:off + ncols], in_=wkH[hi][:nt, :ncols]), r=[B_wkH[hi]], w=[B_kv])
                        else:
                            dV = s_va if sec == "va" else s_vb
                            for b in range(4):
                                S.dma("sp", lambda e, dV=dV, hi=hi, b=b, off=off: e.dma_start(
                                    out=dV[b, PAST:PAST + 16, off:off + ncols], in_=wkH[hi][16 * b:16 * b + 16, :ncols]),
                                    r=[B_wkH[hi]], w=[B_skv])
                elif sec in ("ga", "gb", "mg"):
                    hi = nxt("l", NW)
                    fn = AF.Silu if sec != "mg" else AF.Sigmoid
                    wi_ = nxt("wk", NW)
                    S.op("dve", lambda e, wi_=wi_, pst=pst, nt=nt, rcol=rcol: e.tensor_scalar(
                        out=wkA[wi_][:nt, :ncols], in0=pst[:nt, :ncols], scalar1=rcol, scalar2=None, op0=ALU.mult),
                        r=[Bp, B_rstd], w=[B_wkA[wi_]])
                    S.op("act", lambda e, hi=hi, wi_=wi_, nt=nt, fn=fn: e.activation(
                        out=wkH[hi][:nt, :ncols], in_=wkA[wi_][:nt, :ncols], func=fn),
                        r=[B_wkA[wi_]], w=[B_wkH[hi]])
                    dG = {"ga": sga, "gb": sgb, "mg": smg}[sec]
                    S.dma("sp", lambda e, dG=dG, hi=hi, g0=g0, nt=nt, off=off: e.dma_start(
                        out=dG[g0:g0 + nt, off:off + ncols], in_=wkH[hi][:nt, :ncols]), r=[B_wkH[hi]], w=[B_g])
                elif sec == "kiwi":
                    wi_ = norm_rope(pst, Bp, nt, 64, 64, G_KNI, rcol, cs, B_ropeall, True)
                    if own:
                        S.dma("sp", lambda e, wi_=wi_, g0=g0, nt=nt: e.dma_start(
                            out=o_ik[g0:g0 + nt, :], in_=wkA[wi_][:nt, 0:64]), r=[B_wkA[wi_]], w=[B_out])
                        S.op("dve", lambda e, ti=ti, pst=pst, nt=nt, rcol=rcol: e.tensor_scalar(
                            out=absw[:nt, ti * 16:(ti + 1) * 16], in0=pst[:nt, 64:80], scalar1=rcol, scalar2=1.0 / 32.0,
                            op0=ALU.mult, op1=ALU.mult), r=[Bp, B_rstd], w=[B_absw])
                        S.op("dve", lambda e, ti=ti, nt=nt: e.scalar_tensor_tensor(
                            out=absw[:nt, ti * 16:(ti + 1) * 16], in0=absw[:nt, ti * 16:(ti + 1) * 16],
                            scalar=-1.0, in1=absw[:nt, ti * 16:(ti + 1) * 16], op0=ALU.mult, op1=ALU.max),
                            r=[B_absw], w=[B_absw])
                        S.op("act", lambda e, ti=ti, pst=pst, nt=nt: e.activation(
                            out=sgnw[:nt, ti * 16:(ti + 1) * 16], in_=pst[:nt, 64:80], func=AF.Sign),
                            r=[Bp], w=[B_sgnw])
                        S.dma("sp", lambda e, ti=ti, g0=g0, nt=nt: e.dma_start(
                            out=sgn_scr[g0:g0 + nt, :], in_=sgnw[:nt, ti * 16:(ti + 1) * 16]), r=[B_sgnw], w=[B_q])
                    if (not own) or is_samp:
                        hi = cast_bf(wkA[wi_][:nt, 0:64], B_wkA[wi_], nt, 64)
                        if not own:
                            dsts = [(kiT[:, g0:g0 + nt], 0, nt, B_kv)]
                        else:
                            dsts = [(s_kiT[b, :, PAST:PAST + 16], 16 * b, 16, B_skv) for b in range(4)]
                        to_T_scratch(wkH[hi], B_wkH[hi], nt, 64, dsts)
                elif sec == "qi":
                    wi_ = norm_rope(pst, Bp, nt, ncols, 64, None, rcol, cs, B_ropeall, False)
                    A3 = wkA[wi_][:nt, :ncols].rearrange("p (g d) -> p g d", d=64)
                    h0 = off // 64
                    S.op("dve", lambda e, A3=A3, ti=ti, nt=nt, h0=h0: e.tensor_tensor(
                        out=A3, in0=A3,
                        in1=absw[:nt, ti * 16 + h0:ti * 16 + h0 + 8].unsqueeze(2).broadcast_to([nt, 8, 64]),
                        op=ALU.mult), r=[B_wkA[wi_], B_absw], w=[B_wkA[wi_]])
                    hi = cast_bf(wkA[wi_][:nt, :ncols], B_wkA[wi_], nt, ncols)
                    to_T_scratch(wkH[hi], B_wkH[hi], nt, ncols, [(qiT[off:off + ncols, g0:g0 + nt], 0, nt, B_q)])

    own_tiles = [(128 * j, 128, 128 * j, 128 * j) for j in range(NBLK)] + [(2048, 64, 2048, 2048)]
    own_blocks = [("kiwi", 0, 80)]
    for sec in ("qa", "ka", "va", "ga", "qb", "kb", "vb", "gb", "qi"):
        own_blocks += [(sec, 0, 512), (sec, 512, 512)]
    own_blocks += [("mg", 512 * k, 512) for k in range(8)]
    if KSEC:
        own_blocks = [b_ for b_ in own_blocks if b_[0] in KSEC.split(',')]
    proj_pass(x_own, rope_own, own_tiles, own_blocks, True, xT, B_xT)

    for ti, (tok0, nt, xc0, g0) in enumerate(own_tiles if KSTOP >= 2 else []):
        xi = nxt("x", NB2)
        S.dma("sp", lambda e, xi=xi, tok0=tok0, nt=nt: e.dma_start(out=xin[xi][:nt, 0:256], in_=p_own[tok0:tok0 + nt, :]),
              w=[B_xin[xi]])
        for k in range(2):
            S.op("pe", lambda e, xi=xi, nt=nt, k=k: e.transpose(out=psC[:, k * 128:k * 128 + nt],
                                                               in_=xin[xi][:nt, k * 128:(k + 1) * 128],
                                                               identity=ident_f[:nt, :nt]),
                 r=[B_xin[xi], B_identf], w=[B_psC])
        for k in range(2):
            S.op("dve", lambda e, nt=nt, k=k, xc0=xc0: e.tensor_copy(out=pTt[:, k, xc0:xc0 + nt],
                                                                     in_=psC[:, k * 128:k * 128 + nt]),
                 r=[B_psC], w=[B_pTt])

    kv_blocks = [("kiwi", 0, 80)]
    for sec in ("ka", "va", "kb", "vb"):
        kv_blocks += [(sec, 0, 512), (sec, 512, 512)]
    if KSTOP >= 2:
        bg["gen"] = sample_prep_gen()
    for g in range(0 if KSTOP < 2 else (1 if DBG else 8)):
        tiles = [(2048 * g + 128 * t, 128, 128 * t, 2048 * g + 128 * t) for t in range(16)]
        proj_pass(x_all, rope_all, tiles, kv_blocks, False, xT, B_xT)
    while bg["gen"] is not None:
        bg_step()


    areset()
    scores = aalloc([128, SEQ]); B_scores = Buf("scores")
    mbias = aalloc([128, SEQ], BF16); B_mbias = Buf("mbias")
    ktc = [aalloc([128, 2048], BF16) for i in range(NKB)]; B_ktc = [Buf(f"ktc{i}") for i in range(NKB)]
    vtc = [aalloc([128, 16, 130], BF16) for i in range(NKB)]; B_vtc = [Buf(f"vtc{i}") for i in range(NKB)]
    kic = [aalloc([128, 512], BF16) for i in range(NKB)]; B_kic = [Buf(f"kic{i}") for i in range(NKB)]
    qa_g = aalloc([128, 2 * 8 * 128], BF16).rearrange("p (c h t) -> p c h t", c=2, h=8); B_qa_g = Buf("qa_g")
    qb_g = aalloc([128, 8, 128], BF16); B_qb_g = Buf("qb_g")
    qi_g = aalloc([128, 16 * 128], BF16).rearrange("p (a two t) -> p a two t", two=2, t=128); B_qi_g = Buf("qi_g")
    sga_g = aalloc([128, 1024], BF16); B_sga_g = Buf("sga_g")
    sgb_g = aalloc([128, 1024], BF16); B_sgb_g = Buf("sgb_g")
    sgn_g = aalloc([128, 16]); B_sgn_g = Buf("sgn_g")
    dg = aalloc([128, 16, 128], BF16); B_dg = Buf("dg")
    u_g = aalloc([128, D], BF16); B_u_g = Buf("u_g")
    ustg = aalloc([128, NCH, 128], BF16); B_ustg = Buf("ustg")
    pT = [aalloc([128, 512], BF16) for i in range(NP)]; B_pT = [Buf(f"pT{i}") for i in range(NP)]
    rl = [aalloc([128, 512], BF16) for i in range(NP)]; B_rl = [Buf(f"rl{i}") for i in range(NP)]
    bis = aalloc([128, 8]); B_bis = Buf("bis")
    ep = aalloc([128, 16]); B_ep = Buf("ep")
    eo = aalloc([128, 256]); B_eo = Buf("eo")
    dsave = aalloc([128, 8 * 2 * 132]).rearrange("p (h c d) -> p h c d", h=8, c=2); B_dsave = Buf("dsave")
    for i in range(NKB):
        S.op("dve", lambda e, i=i: e.memset(vtc[i][:, :, 128:129], 1.0), w=[B_vtc[i]])
    S.op("dve", lambda e: e.memset(qa_g.rearrange("p c h t -> p (c h t)"), 0.0), w=[B_qa_g])
    S.op("dve", lambda e: e.memset(qi_g.rearrange("p a two t -> p (a two t)"), 0.0), w=[B_qi_g])

    def attend(qg):
        nq, tok0, KS, B_KS, nkeys, masked = qg["nq"], qg["tok0"], qg["ks"], qg["bks"], qg["nkeys"], qg["masked"]
        k_aT, v_a, k_bT, v_b, k_iT = KS
        ti_w = qg["ti"]
        wrow0 = qg["wrow0"]
        ntiles_full = nkeys // 128
        rem = nkeys - ntiles_full * 128
        for c_ in range(2):
            S.dma("sp", lambda e, c_=c_: e.dma_start(
                out=qa_g[c_ * 64:(c_ + 1) * 64, c_, :, :nq],
                in_=qaT[:, tok0:tok0 + nq].rearrange("(h p) t -> p h t", p=128)[c_ * 64:(c_ + 1) * 64]),
                r=[B_q], w=[B_qa_g])
        S.dma("sp", lambda e: e.dma_start(out=qb_g[:, :, :nq], in_=qbT[:, tok0:tok0 + nq].rearrange("(h p) t -> p h t", p=128)),
              r=[B_q], w=[B_qb_g])
        for c_ in range(2):
            S.dma("sp", lambda e, c_=c_: e.dma_start(
                out=qi_g[c_ * 64:(c_ + 1) * 64, :, c_, :nq],
                in_=qiT[:, tok0:tok0 + nq].rearrange("(h p) t -> p h t", p=128)[c_ * 64:(c_ + 1) * 64]),
                r=[B_q], w=[B_qi_g])
        S.dma("sp", lambda e: e.dma_start(out=sga_g[:nq, :], in_=sga[tok0:tok0 + nq, :]), r=[B_g], w=[B_sga_g])
        S.dma("sp", lambda e: e.dma_start(out=sgb_g[:nq, :], in_=sgb[tok0:tok0 + nq, :]), r=[B_g], w=[B_sgb_g])
        S.dma("sp", lambda e: e.dma_start(out=sgn_g[:nq, :], in_=sgn_scr[tok0:tok0 + nq, :]), r=[B_q], w=[B_sgn_g])
        for h in range(16):
            S.op("dve", lambda e, h=h: e.tensor_scalar(
                out=dg[:nq, h, :nq], in0=ident_b[:nq, :nq],
                scalar1=sgn_g[:nq, h:h + 1], scalar2=None, op0=ALU.mult),
                r=[B_identb, B_sgn_g], w=[B_dg])
        nchunks = (nkeys + 511) // 512
        for kc in range(nchunks):
            k0 = kc * 512
            cw = min(512, nkeys - k0)
            ci = nxt("kb", NKB)
            for half in range(2):
                S.dma("sp", lambda e, ci=ci, half=half, k0=k0, cw=cw: e.dma_start(
                    out=kic[ci][half * 64:(half + 1) * 64, :cw], in_=k_iT[:, k0:k0 + cw]), r=[B_KS], w=[B_kic[ci]])
            is_tail = masked and kc >= nchunks - 2
            for h in range(16):
                li = nxt("psb", 2)
                hp = (h % 2) * 64
                S.op("pe", lambda e, h=h, li=li, hp=hp, ci=ci, cw=cw: e.matmul(
                    out=psL[li][:nq, :cw], lhsT=qi_g[:, h // 2, h % 2, :nq], rhs=kic[ci][:, :cw],
                    start=True, stop=True), r=[B_qi_g, B_kic[ci]], w=[B_psL[li]])
                ri = nxt("p", NP)
                eng = evac_engine()
                if eng == "act":
                    S.op("act", lambda e, li=li, ri=ri, cw=cw: e.activation(out=rl[ri][:nq, :cw], in_=psL[li][:nq, :cw],
                                                                          func=AF.Relu), r=[B_psL[li]], w=[B_rl[ri]])
                else:
                    S.op("dve", lambda e, li=li, ri=ri, cw=cw: e.tensor_scalar(
                        out=rl[ri][:nq, :cw], in0=psL[li][:nq, :cw], scalar1=0.0, scalar2=None, op0=ALU.max),
                        r=[B_psL[li]], w=[B_rl[ri]])
                S.op("pe", lambda e, h=h, ri=ri, cw=cw, is_tail=is_tail: e.matmul(
                    out=psC[:nq, :cw], lhsT=dg[:nq, h, :nq], rhs=rl[ri][:nq, :cw], start=(h == 0),
                    stop=(h == 15 and not is_tail)), r=[B_dg, B_rl[ri]], w=[B_psC])
            if is_tail:
                mo = (kc - (nchunks - 2)) * 512
                S.op("pe", lambda e, mo=mo, cw=cw: e.matmul(out=psC[:nq, :cw], lhsT=ident_b[:nq, :nq],
                                                            rhs=maskqk[:nq, mo:mo + cw], start=False, stop=True),
                     r=[B_identb, B_maskqk], w=[B_psC])
            S.op("act", lambda e, k0=k0, cw=cw: e.copy(out=scores[:nq, k0:k0 + cw], in_=psC[:nq, :cw]),
                 r=[B_psC], w=[B_scores])
        def bisect():
          S.op("dve", lambda e: e.memset(bis[:nq, 0:1], BIS_LO), w=[B_bis])
          for it in range(NBIS):
              wd = BIS_W / (2.0 ** (it + 1))
              S.op("dve", lambda e, wd=wd: e.tensor_scalar(out=bis[:nq, 1:2], in0=bis[:nq, 0:1], scalar1=wd, scalar2=None,
                                                           op0=ALU.add), r=[B_bis], w=[B_bis])
              S.op("dve", lambda e: e.tensor_scalar(out=mbias[:nq, :nkeys], in0=scores[:nq, :nkeys], scalar1=bis[:nq, 1:2],
                                                    scalar2=0.0, op0=ALU.is_ge, op1=ALU.add, accum_out=bis[:nq, 2:3]),
                   r=[B_scores, B_bis], w=[B_mbias, B_bis])
              S.op("dve", lambda e: e.tensor_scalar(out=bis[:nq, 3:4], in0=bis[:nq, 2:3], scalar1=float(TOPK) - 0.5,
                                                    scalar2=None, op0=ALU.is_ge), r=[B_bis], w=[B_bis])
              S.op("dve", lambda e, wd=wd: e.scalar_tensor_tensor(out=bis[:nq, 0:1], in0=bis[:nq, 3:4], scalar=wd,
                                                                  in1=bis[:nq, 0:1], op0=ALU.mult, op1=ALU.add),
                   r=[B_bis], w=[B_bis])
          S.op("dve", lambda e: e.tensor_scalar(out=mbias[:nq, :nkeys], in0=scores[:nq, :nkeys], scalar1=bis[:nq, 0:1],
                                                scalar2=NEG, op0=ALU.is_lt, op1=ALU.mult), r=[B_scores, B_bis], w=[B_mbias])

        def branch(is_diff):
            kT_d, v_d = (k_aT, v_a) if is_diff else (k_bT, v_b)
            q_g, Bq_g = (qa_g, B_qa_g) if is_diff else (qb_g, B_qb_g)
            ncomp = 2 if is_diff else 1
            per = 2 if is_diff else 4
            scale = (64.0 ** -0.5) if is_diff else (128.0 ** -0.5)
            tl = [(t, 128) for t in range(ntiles_full)] + ([(ntiles_full, rem)] if rem else [])
            ntl = len(tl)
            for h in range(8):
                oi = nxt("ps", 2)
                pso, Bpso = psB[oi], B_psB[oi]
                if is_diff:
                    pso_c = [psB[0], psB[1]]
                    Bpso_c = [B_psB[0], B_psB[1]]
                else:
                    pso_c = [pso]
                    Bpso_c = [Bpso]
                cur = {"chunk": -1, "ci": 0}

                def need_chunk(t):
                    c = t // 16
                    if c == cur["chunk"]:
                        return cur["ci"]
                    ci = nxt("kb", NKB)
                    k0 = c * 2048
                    cwk = min(2048, nkeys - k0)
                    nfull = min(16, ntiles_full - c * 16)
                    S.dma("sp", lambda e, ci=ci, k0=k0, cwk=cwk, h=h: e.dma_start(
                        out=ktc[ci][:, :cwk], in_=kT_d[h * 128:(h + 1) * 128, k0:k0 + cwk]), r=[B_KS], w=[B_ktc[ci]])
                    if nfull > 0:
                        S.dma("sp", lambda e, ci=ci, k0=k0, nfull=nfull, h=h: e.dma_start(
                            out=vtc[ci][:, :nfull, 0:128],
                            in_=v_d[k0:k0 + nfull * 128, h * 128:(h + 1) * 128].rearrange("(t p) d -> p t d", p=128)),
                            r=[B_KS], w=[B_vtc[ci]])
                    if rem and c == ntiles_full // 16:
                        tt = ntiles_full - c * 16
                        S.dma("sp", lambda e, ci=ci, tt=tt, h=h: e.dma_start(
                            out=vtc[ci][:rem, tt, 0:128],
                            in_=v_d[ntiles_full * 128:ntiles_full * 128 + rem, h * 128:(h + 1) * 128]),
                            r=[B_KS], w=[B_vtc[ci]])
                    cur["chunk"] = c
                    cur["ci"] = ci
                    return ci

                i = 0
                first = True
                while i < ntl:
                    batch = [tl[i]]
                    while (len(batch) < per and i + len(batch) < ntl and tl[i + len(batch)][1] == batch[0][1]
                           and tl[i + len(batch)][0] // 16 == batch[0][0] // 16):
                        batch.append(tl[i + len(batch)])
                    i += len(batch)
                    ksz = batch[0][1]
                    ci = need_chunk(batch[0][0])
                    si_ = nxt("psb", 2)
                    pss, Bpss = psA[si_], B_psA[si_]
                    nsl = len(batch) * ncomp
                    for bi, (t, _) in enumerate(batch):
                        tc_ = t % 16
                        mt = (t - (ntiles_full - 8)) if (masked and t >= ntiles_full - 8) else -1
                        for c in range(ncomp):
                            sl = bi * ncomp + c
                            if is_diff:
                                lhs = ktc[ci][:, tc_ * 128:tc_ * 128 + ksz]
                                rhs = q_g[:, c, h, :nq]
                            else:
                                lhs = ktc[ci][:, tc_ * 128:tc_ * 128 + ksz]
                                rhs = q_g[:, h, :nq]
                            has_mask = (mt >= 0) if is_diff else True
                            S.op("pe", lambda e, lhs=lhs, rhs=rhs, sl=sl, has_mask=has_mask, ksz=ksz, pss=pss: e.matmul(
                                out=pss[:ksz, sl * nq:(sl + 1) * nq], lhsT=lhs, rhs=rhs, start=True, stop=not has_mask),
                                r=[B_ktc[ci], Bq_g], w=[Bpss])
                            if is_diff and mt >= 0:
                                S.op("pe", lambda e, sl=sl, mt=mt, ksz=ksz, pss=pss: e.matmul(
                                    out=pss[:ksz, sl * nq:(sl + 1) * nq], lhsT=ident_b[:, :ksz],
                                    rhs=maskT[:, mt * 128:mt * 128 + nq], start=False, stop=True),
                                    r=[B_identb, B_maskT], w=[Bpss])
                            if not is_diff:
                                S.op("pe", lambda e, sl=sl, t=t, ksz=ksz, pss=pss: e.matmul(
                                    out=pss[:ksz, sl * nq:(sl + 1) * nq], lhsT=mbias[:nq, t * 128:t * 128 + ksz],
                                    rhs=ident_b[:nq, :nq], start=False, stop=True), r=[B_mbias, B_identb], w=[Bpss])
                    pi_ = nxt("p", NP)
                    S.op("act", lambda e, pi_=pi_, nsl=nsl, ksz=ksz, pss=pss: e.activation(
                        out=pT[pi_][:ksz, :nsl * nq], in_=pss[:ksz, :nsl * nq], func=AF.Exp, scale=scale),
                        r=[Bpss], w=[B_pT[pi_]])
                    for bi, (t, _) in enumerate(batch):
                        tc_ = t % 16
                        lastt = (t == tl[-1][0])
                        for c in range(ncomp):
                            sl = bi * ncomp + c
                            S.op("pe", lambda e, pi_=pi_, sl=sl, c=c, tc_=tc_, first=first, lastt=lastt, ksz=ksz, po=pso_c[c], ci=ci: e.matmul(
                                out=po[:nq, 0:129], lhsT=pT[pi_][:ksz, sl * nq:(sl + 1) * nq],
                                rhs=vtc[ci][:ksz, tc_, 0:129], start=first, stop=lastt), r=[B_pT[pi_], B_vtc[ci]], w=[Bpso_c[c]])
                        first = False
                if is_diff:
                    for c in range(2):
                        S.op("act", lambda e, c=c, h=h: e.copy(out=dsave[:nq, h, c, 0:129], in_=psB[c][:nq, 0:129]),
                             r=[B_psB[c]], w=[B_dsave])
                else:
                    S.op("dve", lambda e, pso=pso: e.reciprocal(out=ep[:nq, 0:1], in_=pso[:nq, 128:129]), r=[Bpso], w=[B_ep])
                    S.op("dve", lambda e, h=h, pso=pso: e.scalar_tensor_tensor(
                        out=u_g[:nq, 1024 + h * 128:1024 + (h + 1) * 128], in0=pso[:nq, 0:128], scalar=ep[:nq, 0:1],
                        in1=sgb_g[:nq, h * 128:(h + 1) * 128], op0=ALU.mult, op1=ALU.mult),
                        r=[Bpso, B_ep, B_sgb_g], w=[B_u_g])

        def diff_epi():
            for h in range(8):
                S.op("dve", lambda e, h=h: e.reciprocal(out=ep[:nq, 0:1], in_=dsave[:nq, h, 0, 128:129]), r=[B_dsave], w=[B_ep])
                S.op("dve", lambda e, h=h: e.reciprocal(out=ep[:nq, 1:2], in_=dsave[:nq, h, 1, 128:129]), r=[B_dsave], w=[B_ep])
                S.op("dve", lambda e, h=h: e.tensor_tensor(out=ep[:nq, 1:2], in0=ep[:nq, 1:2], in1=NEGLAM[:nq, :],
                                                      op=ALU.mult), r=[B_ep, B_lamt], w=[B_ep])
                S.op("dve", lambda e, h=h: e.tensor_scalar(out=eo[:nq, 0:128], in0=dsave[:nq, h, 0, 0:128], scalar1=ep[:nq, 0:1],
                                                      scalar2=None, op0=ALU.mult), r=[B_dsave, B_ep], w=[B_eo])
                S.op("dve", lambda e, h=h: e.scalar_tensor_tensor(out=eo[:nq, 0:128], in0=dsave[:nq, h, 1, 0:128],
                                                             scalar=ep[:nq, 1:2], in1=eo[:nq, 0:128],
                                                             op0=ALU.mult, op1=ALU.add), r=[B_dsave, B_ep, B_eo], w=[B_eo])
                S.op("act", lambda e, h=h: e.activation(out=eo[:nq, 128:256], in_=eo[:nq, 0:128], func=AF.Square,
                                                   accum_out=ep[:nq, 2:3]), r=[B_eo], w=[B_eo, B_ep])
                S.op("act", lambda e, h=h: e.activation(out=ep[:nq, 3:4], in_=ep[:nq, 2:3], func=AF.Sqrt, scale=1.0 / 128,
                                                   bias=epsc[:nq, :]), r=[B_ep, B_epsc], w=[B_ep])
                S.op("dve", lambda e, h=h: e.reciprocal(out=ep[:nq, 3:4], in_=ep[:nq, 3:4]), r=[B_ep], w=[B_ep])
                S.op("dve", lambda e, h=h: e.tensor_scalar(out=eo[:nq, 0:128], in0=eo[:nq, 0:128], scalar1=ep[:nq, 3:4],
                                                      scalar2=1.0 - LAM_INIT, op0=ALU.mult, op1=ALU.mult),
                     r=[B_eo, B_ep], w=[B_eo])
                S.op("dve", lambda e, h=h: e.tensor_tensor(out=eo[:nq, 0:128], in0=eo[:nq, 0:128], in1=G_SUB[:nq, :],
                                                      op=ALU.mult), r=[B_eo, B_gains], w=[B_eo])
                S.op("dve", lambda e, h=h: e.tensor_tensor(out=u_g[:nq, h * 128:(h + 1) * 128], in0=eo[:nq, 0:128],
                                                           in1=sga_g[:nq, h * 128:(h + 1) * 128], op=ALU.mult),
                     r=[B_eo, B_sga_g], w=[B_u_g])

        branch(True)
        bisect()
        diff_epi()
        if DBG and tok0 == 0:
            S.dma("sp", lambda e: e.dma_start(out=dbg_eo[:, :], in_=eo[:, :]), r=[B_eo], w=[B_out])
            S.dma("sp", lambda e: e.dma_start(out=dbg_ep[:, :], in_=ep[:, :]), r=[B_ep], w=[B_out])
            for i_ in range(2):
                S.op("dve", lambda e, i_=i_: e.tensor_copy(out=scores[:, i_ * 512:(i_ + 1) * 512], in_=psB[i_][:, :]),
                     r=[B_psB[i_]], w=[B_scores])
            S.dma("sp", lambda e: e.dma_start(out=dbg_ps[:, :], in_=scores[:, 0:1024]), r=[B_scores], w=[B_out])
            S.dma("sp", lambda e: e.dma_start(out=dbg_lam[:, :], in_=lamt[:, :]), r=[B_lamt], w=[B_out])
        branch(False)
        for q2 in range(2):
            for k in range(8):
                ch = q2 * 8 + k
                S.op("pe", lambda e, ch=ch, k=k: e.transpose(out=psT[:, k * 128:k * 128 + nq],
                                                             in_=u_g[:nq, ch * 128:(ch + 1) * 128],
                                                             identity=ident_b[:nq, :nq]), r=[B_u_g, B_identb], w=[B_psT])
            S.op("act", lambda e, q2=q2: e.copy(out=ustg[:, q2 * 8:(q2 + 1) * 8, :].rearrange("p a b -> p (a b)"),
                                                in_=psT[:, :]), r=[B_psT], w=[B_ustg])
        S.dma("sp", lambda e: e.dma_start(out=uT[:, tok0:tok0 + nq].rearrange("(ch p) t -> p ch t", p=128),
                                          in_=ustg[:, :, :nq]), r=[B_ustg], w=[B_u])

    qgs = []
    for j in range(NBLK):
        qgs.append(dict(nq=128, tok0=128 * j, ks=(kaT, va, kbT, vb, kiT), bks=B_kv, nkeys=(8 * j + 8) * 128,
                        masked=True, ti=j, wrow0=0))
    for b in range(4):
        qgs.append(dict(nq=16, tok0=2048 + 16 * b, ks=(s_kaT[b], s_va[b], s_kbT[b], s_vb[b], s_kiT[b]), bks=B_skv,
                        nkeys=NKS, masked=False, ti=16, wrow0=16 * b))
    KSAMP = int(os.environ.get('KSAMP', '1'))
    for qg in ([] if KSTOP < 3 else ([qgs[0]] + ([qgs[16]] if KSAMP else []) if DBG else (qgs if KSAMP else qgs[:16]))):
        attend(qg)

    areset()
    uTs = aalloc([128, NCH, NOWN], BF16); B_uTs = Buf("uTs")
    hTs, B_hTs = uTs, B_uTs
    wbfD = [aalloc([128, NCH, 512], BF16) for i in range(2)]; B_wbfD = [Buf(f"wbfD{i}") for i in range(2)]
    wbf2 = [aalloc([128, 8, 512], BF16) for i in range(2)]; B_wbf2 = [Buf(f"wbf2{i}") for i in range(2)]
    mTt = [aalloc([128, NCH, 128], BF16) for i in range(2)]; B_mTt = [Buf(f"mTt{i}") for i in range(2)]
    mab = [aalloc([128, 1024], BF16) for i in range(2)]; B_mab = [Buf(f"mab{i}") for i in range(2)]
    hld = [aalloc([128, 512]) for i in range(2)]; B_hld = [Buf(f"hld{i}") for i in range(2)]
    wkAD = [aalloc([128, 512]) for i in range(NW)]; B_wkAD = [Buf(f"wkAD{i}") for i in range(NW)]
    wkBD = [aalloc([128, 512]) for i in range(NW)]; B_wkBD = [Buf(f"wkBD{i}") for i in range(NW)]
    wkHD = [aalloc([128, 512], BF16) for i in range(NW)]; B_wkHD = [Buf(f"wkHD{i}") for i in range(NW)]
    stgD = [aalloc([128, 4 * 128], BF16) for i in range(NW)]; B_stgD = [Buf(f"stgD{i}") for i in range(NW)]
    B_m = Buf("m_scr")
    if KSTOP >= 4:
        S.dma("sp", lambda e: e.dma_start(out=uTs[:, :, :], in_=uT[:, :].rearrange("(ch p) t -> p ch t", p=128)),
              r=[B_u], w=[B_uTs])
    for n in range(4 if KSTOP >= 4 else 0):
        wa = load_w(w_ba, n * 512, 512, 8, wbf2, B_wbf2)
        wb_ = load_w(w_bb, n * 512, 512, 8, wbf2, B_wbf2)
        for ti, (tok0, nt, xc0, g0) in enumerate(own_tiles):
            mi = nxt("x", 2)
            S.dma("sp", lambda e, mi=mi, tok0=tok0, nt=nt, n=n: e.dma_start(
                out=mab[mi][:nt, :].rearrange("p (a c) -> p a c", a=2),
                in_=smg[tok0:tok0 + nt, :].rearrange("t (a c) -> t a c", a=2)[:, :, n * 512:(n + 1) * 512]),
                r=[B_g], w=[B_mab[mi]])
            pa, Bpa = psA[0], B_psA[0]
            pb, Bpb = psA[1], B_psA[1]
            for ch in range(8):
                S.op("pe", lambda e, ch=ch, xc0=xc0, nt=nt, wa=wa: e.matmul(
                    out=pa[:nt, :], lhsT=uTs[:, ch, xc0:xc0 + nt], rhs=wbf2[wa][:, ch, :], start=(ch == 0),
                    stop=(ch == 7)), r=[B_uTs, B_wbf2[wa]], w=[Bpa])
            for ch in range(8):
                S.op("pe", lambda e, ch=ch, xc0=xc0, nt=nt, wb_=wb_: e.matmul(
                    out=pb[:nt, :], lhsT=uTs[:, 8 + ch, xc0:xc0 + nt], rhs=wbf2[wb_][:, ch, :], start=(ch == 0),
                    stop=(ch == 7)), r=[B_uTs, B_wbf2[wb_]], w=[Bpb])
            wi_ = nxt("wk", NW)
            S.op("dve", lambda e, wi_=wi_, mi=mi, nt=nt: e.tensor_tensor(out=wkAD[wi_][:nt, :], in0=pa[:nt, :],
                                                                          in1=mab[mi][:nt, 0:512], op=ALU.mult),
                 r=[Bpa, B_mab[mi]], w=[B_wkAD[wi_]])
            S.op("dve", lambda e, wi_=wi_, mi=mi, nt=nt: e.tensor_tensor(out=wkBD[wi_][:nt, :], in0=pb[:nt, :],
                                                                          in1=mab[mi][:nt, 512:1024], op=ALU.mult),
                 r=[Bpb, B_mab[mi]], w=[B_wkBD[wi_]])
            hi = nxt("l", NW)
            S.op("dve", lambda e, wi_=wi_, hi=hi, nt=nt: e.tensor_tensor(out=wkHD[hi][:nt, :], in0=wkAD[wi_][:nt, :],
                                                                          in1=wkBD[wi_][:nt, :], op=ALU.add),
                 r=[B_wkAD[wi_], B_wkBD[wi_]], w=[B_wkHD[hi]])
            to_T_scratch(wkHD[hi], B_wkHD[hi], nt, 512, [(mT_scr[n * 512:(n + 1) * 512, tok0:tok0 + nt], 0, nt, B_m)], (stgD, B_stgD))
    S.op("dve", lambda e: e.memset(absw[:, :], 0.0), r=[], w=[B_absw])
    for n in range(4 if KSTOP >= 4 else 0):
        wo = load_w(w_out, n * 512, 512, NCH, wbfD, B_wbfD)
        for ti, (tok0, nt, xc0, g0) in enumerate(own_tiles):
            hi_ = nxt("x", 2)
            S.dma("sp", lambda e, hi_=hi_, tok0=tok0, nt=nt, n=n: e.dma_start(
                out=hld[hi_][:nt, :], in_=x_own[tok0:tok0 + nt, n * 512:(n + 1) * 512]), w=[B_hld[hi_]])
            S.dma("sp", lambda e, hi_=hi_, tok0=tok0, nt=nt: e.dma_start(
                out=mTt[hi_][:, :, :nt], in_=mT_scr[:, tok0:tok0 + nt].rearrange("(ch p) t -> p ch t", p=128)),
                r=[B_m], w=[B_mTt[hi_]])
            pi = nxt("ps", 2)
            gemm_tile(psA[pi], B_psA[pi], mTt[hi_], B_mTt[hi_], 0, nt, wbfD[wo], B_wbfD[wo], NCH, 512)
            wi_ = nxt("wk", NW)
            S.op("dve", lambda e, wi_=wi_, hi_=hi_, pi=pi, nt=nt: e.tensor_tensor(
                out=wkAD[wi_][:nt, :], in0=psA[pi][:nt, :], in1=hld[hi_][:nt, :], op=ALU.add),
                r=[B_psA[pi], B_hld[hi_]], w=[B_wkAD[wi_]])
            S.dma("sp", lambda e, wi_=wi_, tok0=tok0, nt=nt, n=n: e.dma_start(
                out=hbuf[tok0:tok0 + nt, n * 512:(n + 1) * 512], in_=wkAD[wi_][:nt, :]), r=[B_wkAD[wi_]], w=[B_h])
            S.op("act", lambda e, wi_=wi_, nt=nt, ti=ti, n=n: e.activation(
                out=wkBD[wi_][:nt, :], in_=wkAD[wi_][:nt, :], func=AF.Square,
                accum_out=absw[:nt, ti * 16 + n:ti * 16 + n + 1]), r=[B_wkAD[wi_]], w=[B_wkBD[wi_], B_absw])
            for k in range(4):
                S.op("pe", lambda e, wi_=wi_, k=k, nt=nt: e.transpose(out=psC[:, k * 128:k * 128 + nt],
                                                                     in_=wkAD[wi_][:nt, k * 128:(k + 1) * 128],
                                                                     identity=ident_f[:nt, :nt]),
                     r=[B_wkAD[wi_], B_identf], w=[B_psC])
            for k in range(4):
                S.op("dve", lambda e, k=k, n=n, xc0=xc0, nt=nt: e.tensor_scalar(
                    out=hTs[:, 4 * n + k, xc0:xc0 + nt], in0=psC[:, k * 128:k * 128 + nt],
                    scalar1=pgcol[:, 4 * n + k:4 * n + k + 1], scalar2=None, op0=ALU.mult),
                    r=[B_psC, B_pgcol], w=[B_hTs])
    for ti, (tok0, nt, xc0, g0) in enumerate(own_tiles if KSTOP >= 4 else []):
        S.op("dve", lambda e, ti=ti, nt=nt: e.tensor_reduce(out=ssqt[:nt, ti:ti + 1], in_=absw[:nt, ti * 16:ti * 16 + 4],
                                                            axis=AX.X, op=ALU.add), r=[B_absw], w=[B_ssqt])
        S.op("act", lambda e, ti=ti, nt=nt: e.activation(out=rstd[:nt, ti:ti + 1], in_=ssqt[:nt, ti:ti + 1], func=AF.Sqrt,
                                                         scale=1.0 / D, bias=epsc[:nt, :]), r=[B_ssqt, B_epsc], w=[B_rstd])
        S.op("dve", lambda e, ti=ti, nt=nt: e.reciprocal(out=rstd[:nt, ti:ti + 1], in_=rstd[:nt, ti:ti + 1]),
             r=[B_rstd], w=[B_rstd])
    for n in range(4 if KSTOP >= 4 else 0):
        wg = load_w(w_pg, n * 512, 512, NCH, wbfD, B_wbfD)
        wp = load_w(w_ple, n * 512, 512, 2, wbf2, B_wbf2)
        for ti, (tok0, nt, xc0, g0) in enumerate(own_tiles):
            hi_ = nxt("x", 2)
            S.dma("sp", lambda e, hi_=hi_, tok0=tok0, nt=nt, n=n: e.dma_start(
                out=hld[hi_][:nt, :], in_=hbuf[tok0:tok0 + nt, n * 512:(n + 1) * 512]), r=[B_h], w=[B_hld[hi_]])
            pi = nxt("ps", 2)
            gemm_tile(psA[pi], B_psA[pi], hTs, B_hTs, xc0, nt, wbfD[wg], B_wbfD[wg], NCH, 512)
            gemm_tile(psB[pi], B_psB[pi], pTt, B_pTt, xc0, nt, wbf2[wp], B_wbf2[wp], 2, 512)
            wi_ = nxt("wk", NW)
            S.op("dve", lambda e, wi_=wi_, pi=pi, nt=nt, ti=ti: e.tensor_scalar(
                out=wkAD[wi_][:nt, :], in0=psA[pi][:nt, :], scalar1=rstd[:nt, ti:ti + 1], scalar2=None, op0=ALU.mult),
                r=[B_psA[pi], B_rstd], w=[B_wkAD[wi_]])
            S.op("act", lambda e, wi_=wi_, nt=nt: e.activation(
                out=wkAD[wi_][:nt, :], in_=wkAD[wi_][:nt, :], func=AF.Sigmoid),
                r=[B_wkAD[wi_]], w=[B_wkAD[wi_]])
            S.op("dve", lambda e, wi_=wi_, pi=pi, nt=nt: e.tensor_tensor(out=wkBD[wi_][:nt, :], in0=wkAD[wi_][:nt, :],
                                                                          in1=psB[pi][:nt, :], op=ALU.mult),
                 r=[B_wkAD[wi_], B_psB[pi]], w=[B_wkBD[wi_]])
            S.op("dve", lambda e, wi_=wi_, hi_=hi_, nt=nt: e.tensor_tensor(out=wkBD[wi_][:nt, :], in0=wkBD[wi_][:nt, :],
                                                                            in1=hld[hi_][:nt, :], op=ALU.add),
                 r=[B_wkBD[wi_], B_hld[hi_]], w=[B_wkBD[wi_]])
            S.dma("sp", lambda e, wi_=wi_, tok0=tok0, nt=nt, n=n: e.dma_start(
                out=y_own[tok0:tok0 + nt, n * 512:(n + 1) * 512], in_=wkBD[wi_][:nt, :]), r=[B_wkBD[wi_]], w=[B_out])

    with nc.Block() as block:
        @block.tensor
        def _(e):
            S.emit("pe", e)

        @block.scalar
        def _(e):
            S.emit("act", e)

        @block.vector
        def _(e):
            S.emit("dve", e)

        @block.gpsimd
        def _(e):
            S.emit("pool", e)

        @block.sync
        def _(e):
            S.emit("sp", e)
            S.final_waits(e)
    es.close()
    return nc, S


def _rope_table(pos):
    out = np.zeros((len(pos), 48), np.float32)
    p = pos.astype(np.float32)[:, None]
    for (r, c0) in ((16, 0), (32, 16)):
        inv = (np.float32(500000.0) ** (-np.arange(0, r, 2, dtype=np.float32) / np.float32(r))).astype(np.float32)
        ang = (p * inv[None, :]).astype(np.float32)
        out[:, c0:c0 + r // 2] = np.cos(ang)
        out[:, c0 + r // 2:c0 + r] = np.sin(ang)
    return out


_CACHE = {}


def kernel(x_prompt, x_sample, p_prompt, p_sample, cache_diff_k, cache_diff_v, cache_dsa_k, cache_dsa_v,
           cache_idx_k, ln_g, w_in, q_norm_a, k_norm_a, lam_q1, lam_k1, lam_q2, lam_k2, subln_a,
           q_norm_b, k_norm_b, k_norm_idx, w_branch_a, w_branch_b, w_out, ple_norm, w_ple_gate, w_ple):
    f = lambda a: np.ascontiguousarray(np.asarray(a, dtype=np.float32))
    xp = f(x_prompt)[0]
    xs = f(x_sample).reshape(512, D)
    pp = f(p_prompt)[0, 0]
    psm = f(p_sample)[0].reshape(512, 256)
    cdk = f(cache_diff_k)[0].reshape(32, PAST, 1024)
    cdv = f(cache_diff_v)[0].reshape(32, PAST, 1024)
    csk = f(cache_dsa_k)[0].reshape(32, PAST, 1024)
    csv = f(cache_dsa_v)[0].reshape(32, PAST, 1024)
    cik = f(cache_idx_k)[0]
    if "nc" not in _CACHE:
        _CACHE["nc"] = build_program()
    nc, S = _CACHE["nc"]
    rope_all = _rope_table(np.arange(SEQ))
    rope_s = _rope_table(PAST + np.arange(16))
    ident = np.eye(128, dtype=np.float32)
    shared = dict(x_all=xp[:XROWS], rope_all=rope_all[:XROWS], w_in=f(w_in)[0], ln_g=f(ln_g)[0], q_norm_a=f(q_norm_a)[0],
                  k_norm_a=f(k_norm_a)[0], lam_q1=f(lam_q1)[0], lam_k1=f(lam_k1)[0], lam_q2=f(lam_q2)[0],
                  lam_k2=f(lam_k2)[0], subln_a=f(subln_a)[0], q_norm_b=f(q_norm_b)[0], k_norm_b=f(k_norm_b)[0],
                  k_norm_idx=f(k_norm_idx)[0], w_branch_a=f(w_branch_a)[0], w_branch_b=f(w_branch_b)[0],
                  w_out=f(w_out)[0], ple_norm=f(ple_norm)[0], w_ple_gate=f(w_ple_gate)[0], w_ple=f(w_ple)[0],
                  ident=ident)
    in_maps = []
    rows_all = []
    kk = np.arange(128)[:, None]
    qq = np.arange(128)[None, :]
    diag_vis = (kk < 64) | (qq >= 64)
    for c in range(NCORES):
        rows = np.concatenate([np.arange((8 * j + c) * 128, (8 * j + c + 1) * 128) for j in range(NBLK)])
        rows_all.append(rows)
        x_own = np.concatenate([xp[rows], xs[64 * c:64 * (c + 1)]], 0)
        p_own = np.concatenate([pp[rows], psm[64 * c:64 * (c + 1)]], 0)
        rope_own = np.concatenate([rope_all[rows], np.tile(rope_s, (4, 1))], 0)
        mT = np.zeros((128, 8, 128), np.float32)
        for m in range(8):
            if m > c:
                mT[:, m, :] = NEG
            elif m == c:
                mT[:, m, :] = np.where(diag_vis, 0.0, NEG)
        mqk = np.ascontiguousarray(mT.transpose(2, 1, 0))
        d = dict(shared)
        d.update(x_own=np.ascontiguousarray(x_own), p_own=np.ascontiguousarray(p_own),
                 rope_own=np.ascontiguousarray(rope_own),
                 c_dk=np.ascontiguousarray(cdk[4 * c:4 * c + 4, :CROWS]), c_dv=np.ascontiguousarray(cdv[4 * c:4 * c + 4, :CROWS]),
                 c_sk=np.ascontiguousarray(csk[4 * c:4 * c + 4, :CROWS]), c_sv=np.ascontiguousarray(csv[4 * c:4 * c + 4, :CROWS]),
                 c_ik=np.ascontiguousarray(cik[4 * c:4 * c + 4, :CROWS]),
                 maskT=mT.reshape(128, 1024), maskqk=mqk.reshape(128, 1024))
        in_maps.append(d)
    res = run_bass_kernel_spmd(nc, in_maps[:KCORES], core_ids=list(range(KCORES)))
    R = res.results
    _CACHE['dbg'] = R[0]
    y_p = np.zeros((1, SEQ, D), np.float32)
    y_s = np.zeros((32, 16, D), np.float32)
    outs_p = {k: np.zeros((SEQ, n), np.float32) for k, n in (("o_dk", 1024), ("o_dv", 1024), ("o_sk", 1024),
                                                                ("o_sv", 1024), ("o_ik", 64))}
    outs_s = {k: np.zeros((512, n), np.float32) for k, n in (("o_dk", 1024), ("o_dv", 1024), ("o_sk", 1024),
                                                               ("o_sv", 1024), ("o_ik", 64))}
    for c in range(KCORES):
        r = R[c]
        yo = np.asarray(r["y_own"])
        y_p[0, rows_all[c]] = yo[:2048]
        y_s[4 * c:4 * c + 4] = yo[2048:].reshape(4, 16, D)
        for k in outs_p:
            a = np.asarray(r[k])
            outs_p[k][rows_all[c]] = a[:2048]
            outs_s[k][64 * c:64 * (c + 1)] = a[2048:]
    return (y_p, y_s,
            outs_p["o_dk"].reshape(1, 1, SEQ, 8, 2, 64), outs_p["o_dv"].reshape(1, 1, SEQ, 8, 128),
            outs_p["o_sk"].reshape(1, 1, SEQ, 8, 128), outs_p["o_sv"].reshape(1, 1, SEQ, 8, 128),
            outs_p["o_ik"].reshape(1, 1, SEQ, 64),
            outs_s["o_dk"].reshape(1, 32, 16, 8, 2, 64), outs_s["o_dv"].reshape(1, 32, 16, 8, 128),
            outs_s["o_sk"].reshape(1, 32, 16, 8, 128), outs_s["o_sv"].reshape(1, 32, 16, 8, 128),
            outs_s["o_ik"].reshape(1, 32, 16, 64))
```
